# Optimizing a Trainium2 kernel written in Bass

```python
import functools
import jax, jax.numpy as jnp
from jax import lax
import numpy as np

D_MODEL = 1024
BATCH = 2
SEQ = 16384
DEPTH = 2
DEC_BATCH = 16
DEC_SEQ = 2048
PAST_LEN = 128

D_FF = 2816
N_SUB = 3
N_MOD = 3
MIX_WIDTH = D_MODEL
CHUNK = 128
HEADS_A = 4
WIDTH_A = MIX_WIDTH // 2
DH_A = WIDTH_A // HEADS_A
GROUPS_B = 4
WIDTH_B = MIX_WIDTH - WIDTH_A
DH_B = WIDTH_B // GROUPS_B
POOL_WINDOWS = (2, 4, 8, 16)
GROUPS_C = 4
WIDTH_C = MIX_WIDTH // 2
CONV_WIDTH = 3
GROUPS_D = 4
WIDTH_D = MIX_WIDTH - WIDTH_C
DH_D = WIDTH_D // GROUPS_D
IN_EVEN = 2 * WIDTH_A + WIDTH_B
IN_ODD = 3 * WIDTH_C + WIDTH_D
N_EVEN = (DEPTH + 1) // 2
N_ODD = DEPTH // 2
RMS_EPS = 1e-6
LN_EPS = 1e-5
ADA_SCALE = 0.25

kernel_name = 'hybrid_bidir_gmlp_pool_conv_fourier_encoder'


def _rmsnorm(x, g):
    xf = x.astype(jnp.float32)
    y = xf * lax.rsqrt(jnp.mean(xf * xf, axis=-1, keepdims=True) + RMS_EPS)
    return (y * g.astype(jnp.float32)).astype(x.dtype)


def _layernorm(x, g, b):
    xf = x.astype(jnp.float32)
    mu = jnp.mean(xf, axis=-1, keepdims=True)
    var = jnp.mean(jnp.square(xf - mu), axis=-1, keepdims=True)
    y = (xf - mu) * lax.rsqrt(var + LN_EPS)
    return (y * g.astype(jnp.float32) + b.astype(jnp.float32)).astype(x.dtype)


def _swiglu(h, w_up, w_down):
    gate, up = jnp.split(h @ w_up, 2, axis=-1)
    return (jax.nn.silu(gate) * up) @ w_down


def _spatial_gating(u, v, ln_g, ln_b, w_s, b_s):
    b, s, _ = v.shape
    n = s // CHUNK
    vn = _layernorm(v, ln_g, ln_b).reshape(b, n, CHUNK, HEADS_A, DH_A)
    sv = jnp.einsum('hpq,bnqhd->bnphd', w_s, vn) + b_s.T[None, None, :, :, None]
    return u * sv.reshape(b, s, WIDTH_A)


def _multiscale_pool(z, w_pool, pool_scale):
    b, s, _ = z.shape
    zf = z.astype(jnp.float32)
    prefix = jnp.concatenate([jnp.zeros((b, 1, WIDTH_B), jnp.float32), jnp.cumsum(zf, axis=1)], axis=1)
    t = np.arange(s)
    outs = []
    for g, w in enumerate(POOL_WINDOWS):
        lo = np.clip(t - w // 2, 0, s)
        hi = np.clip(t + w // 2, 0, s)
        cnt = jnp.asarray((hi - lo).astype(np.float32))
        pg = prefix[..., g * DH_B:(g + 1) * DH_B]
        mean = (jnp.take(pg, jnp.asarray(hi), axis=1) - jnp.take(pg, jnp.asarray(lo), axis=1)) / cnt[None, :, None]
        outs.append(mean - zf[..., g * DH_B:(g + 1) * DH_B])
    d = jnp.stack(outs, axis=2).astype(z.dtype)
    y = jnp.einsum('bsgd,gde->bsge', d, w_pool) * pool_scale.reshape(GROUPS_B, DH_B)
    return y.reshape(b, s, WIDTH_B)


def _short_conv(c_gate, b_gate, xin, conv_w):
    z = c_gate * xin
    s = z.shape[1]
    pad = CONV_WIDTH // 2
    zp = jnp.pad(z, ((0, 0), (pad, CONV_WIDTH - 1 - pad), (0, 0)))
    conv = zp[:, 0:s] * conv_w[0]
    for k in range(1, CONV_WIDTH):
        conv = conv + zp[:, k:k + s] * conv_w[k]
    return b_gate * conv


def _fourier(z, fourier_g, fourier_w):
    b, s, _ = z.shape
    zg = _rmsnorm(z.reshape(b, s, GROUPS_D, DH_D), fourier_g)
    f = jnp.fft.fft2(zg.astype(jnp.float32), axes=(1, 3), norm='ortho').real.astype(z.dtype)
    y = jnp.einsum('bsgd,gde->bsge', f, fourier_w)
    return y.reshape(b, s, WIDTH_D)


def _mixer_even(h, w_in, ln_g, ln_b, w_s, b_s, w_pool, pool_scale, w_out):
    proj = h @ w_in
    u, v = jnp.split(jax.nn.gelu(proj[..., :2 * WIDTH_A], approximate=False), 2, axis=-1)
    ya = _spatial_gating(u, v, ln_g, ln_b, w_s, b_s)
    yb = _multiscale_pool(proj[..., 2 * WIDTH_A:], w_pool, pool_scale)
    return jnp.concatenate([ya, yb], axis=-1) @ w_out


def _mixer_odd(h, w_in, conv_w, fourier_g, fourier_w, w_out):
    proj = h @ w_in
    c_gate = proj[..., :WIDTH_C]
    b_gate = proj[..., WIDTH_C:2 * WIDTH_C]
    xin = proj[..., 2 * WIDTH_C:3 * WIDTH_C]
    yc = _short_conv(c_gate, b_gate, xin, conv_w)
    yd = _fourier(proj[..., 3 * WIDTH_C:], fourier_g, fourier_w)
    return jnp.concatenate([yc, yd], axis=-1) @ w_out


def _sublayer(x, mod_k, g_pre, g_post, fn, res_weight):
    shift, scale, gate = mod_k[:, 0, None], mod_k[:, 1, None], mod_k[:, 2, None]
    h = _rmsnorm(x, g_pre) * (1 + scale) + shift
    y = _rmsnorm(fn(h), g_post)
    return x + res_weight * (1 + gate) * y


def _trunk(x, c, ada_w, ada_b, norm_pre, norm_post, ffn1_w_up, ffn1_w_down, ffn2_w_up, ffn2_w_down,
           ev_w_in, ev_ln_g, ev_ln_b, ev_w_spatial, ev_b_spatial, ev_w_pool, ev_pool_scale, ev_w_out,
           od_w_in, od_conv_w, od_fourier_g, od_fourier_w, od_w_out):
    nb = c.shape[0]
    for l in range(DEPTH):
        mod = (jax.nn.silu(c) @ ada_w[l] + ada_b[l]).reshape(nb, N_SUB, N_MOD, D_MODEL)
        ffn1 = functools.partial(_swiglu, w_up=ffn1_w_up[l], w_down=ffn1_w_down[l])
        ffn2 = functools.partial(_swiglu, w_up=ffn2_w_up[l], w_down=ffn2_w_down[l])
        j = l // 2
        if l % 2 == 0:
            mixer = functools.partial(_mixer_even, w_in=ev_w_in[j], ln_g=ev_ln_g[j], ln_b=ev_ln_b[j],
                                      w_s=ev_w_spatial[j], b_s=ev_b_spatial[j], w_pool=ev_w_pool[j],
                                      pool_scale=ev_pool_scale[j], w_out=ev_w_out[j])
        else:
            mixer = functools.partial(_mixer_odd, w_in=od_w_in[j], conv_w=od_conv_w[j],
                                      fourier_g=od_fourier_g[j], fourier_w=od_fourier_w[j], w_out=od_w_out[j])
        x = _sublayer(x, mod[:, 0], norm_pre[l, 0], norm_post[l, 0], ffn1, 0.5)
        x = _sublayer(x, mod[:, 1], norm_pre[l, 1], norm_post[l, 1], mixer, 1.0)
        x = _sublayer(x, mod[:, 2], norm_pre[l, 2], norm_post[l, 2], ffn2, 0.5)
    return x


def setup_inputs(seed: int = 0) -> dict:
    key = jax.random.key(seed)
    ks = jax.random.split(key, 32)
    f32 = jnp.float32

    def nrm(k, shape, scale):
        return jax.random.normal(k, shape, f32) * scale

    d = D_MODEL
    return {
        'x_prompt': nrm(ks[0], (BATCH, SEQ, d), 1.0),
        'x_sample': nrm(ks[1], (DEC_BATCH, DEC_SEQ, d), 1.0),
        'c_prompt': nrm(ks[2], (BATCH, d), 1.0),
        'c_sample': nrm(ks[3], (DEC_BATCH, d), 1.0),
        'ada_w': nrm(ks[4], (DEPTH, d, N_SUB * N_MOD * d), ADA_SCALE * d ** -0.5),
        'ada_b': nrm(ks[5], (DEPTH, N_SUB * N_MOD * d), 0.02),
        'norm_pre': 1.0 + nrm(ks[6], (DEPTH, N_SUB, d), 0.1),
        'norm_post': 1.0 + nrm(ks[7], (DEPTH, N_SUB, d), 0.1),
        'ffn1_w_up': nrm(ks[8], (DEPTH, d, 2 * D_FF), d ** -0.5),
        'ffn1_w_down': nrm(ks[9], (DEPTH, D_FF, d), D_FF ** -0.5),
        'ffn2_w_up': nrm(ks[10], (DEPTH, d, 2 * D_FF), d ** -0.5),
        'ffn2_w_down': nrm(ks[11], (DEPTH, D_FF, d), D_FF ** -0.5),
        'ev_w_in': nrm(ks[12], (N_EVEN, d, IN_EVEN), d ** -0.5),
        'ev_ln_g': 1.0 + nrm(ks[13], (N_EVEN, WIDTH_A), 0.1),
        'ev_ln_b': nrm(ks[14], (N_EVEN, WIDTH_A), 0.02),
        'ev_w_spatial': nrm(ks[15], (N_EVEN, HEADS_A, CHUNK, CHUNK), CHUNK ** -0.5),
        'ev_b_spatial': 1.0 + nrm(ks[16], (N_EVEN, HEADS_A, CHUNK), 0.1),
        'ev_w_pool': nrm(ks[17], (N_EVEN, GROUPS_B, DH_B, DH_B), DH_B ** -0.5),
        'ev_pool_scale': 1.0 + nrm(ks[18], (N_EVEN, WIDTH_B), 0.1),
        'ev_w_out': nrm(ks[19], (N_EVEN, MIX_WIDTH, d), MIX_WIDTH ** -0.5),
        'od_w_in': nrm(ks[20], (N_ODD, d, IN_ODD), d ** -0.5),
        'od_conv_w': nrm(ks[21], (N_ODD, CONV_WIDTH, WIDTH_C), CONV_WIDTH ** -0.5),
        'od_fourier_g': 1.0 + nrm(ks[22], (N_ODD, GROUPS_D, DH_D), 0.1),
        'od_fourier_w': nrm(ks[23], (N_ODD, GROUPS_D, DH_D, DH_D), DH_D ** -0.5),
        'od_w_out': nrm(ks[24], (N_ODD, MIX_WIDTH, d), MIX_WIDTH ** -0.5),
    }


def reference(x_prompt, x_sample, c_prompt, c_sample, ada_w, ada_b, norm_pre, norm_post,
              ffn1_w_up, ffn1_w_down, ffn2_w_up, ffn2_w_down,
              ev_w_in, ev_ln_g, ev_ln_b, ev_w_spatial, ev_b_spatial, ev_w_pool, ev_pool_scale, ev_w_out,
              od_w_in, od_conv_w, od_fourier_g, od_fourier_w, od_w_out):
    y_prompt = _trunk(x_prompt, c_prompt, ada_w, ada_b, norm_pre, norm_post, ffn1_w_up, ffn1_w_down,
                      ffn2_w_up, ffn2_w_down, ev_w_in, ev_ln_g, ev_ln_b, ev_w_spatial, ev_b_spatial,
                      ev_w_pool, ev_pool_scale, ev_w_out, od_w_in, od_conv_w, od_fourier_g, od_fourier_w, od_w_out)
    y_sample = _trunk(x_sample, c_sample, ada_w, ada_b, norm_pre, norm_post, ffn1_w_up, ffn1_w_down,
                      ffn2_w_up, ffn2_w_down, ev_w_in, ev_ln_g, ev_ln_b, ev_w_spatial, ev_b_spatial,
                      ev_w_pool, ev_pool_scale, ev_w_out, od_w_in, od_conv_w, od_fourier_g, od_fourier_w, od_w_out)
    return (y_prompt, y_sample)
```

```python
import contextlib
import numpy as np
import ml_dtypes
import concourse.bass as bass
import concourse.mybir as mybir
from concourse.bass_utils import run_bass_kernel_spmd

F32 = mybir.dt.float32
BF16 = mybir.dt.bfloat16
AF = mybir.ActivationFunctionType
ALU = mybir.AluOpType

D = 1024
DFF = 2816
NT = 64
NG = 16
ROWS = 8192
RMS_EPS = 1e-6
LN_EPS = 1e-5
SEM_LIMIT = 30000
ENGS = ("sync", "gpsimd", "scalar", "vector", "tensor")

NPASS_DEBUG = None


def seq_of_group(g):
    return 0 if g < 8 else (1 if g < 12 else 2)


class Res:
    __slots__ = ("name", "w", "r", "streams")

    def __init__(self, name):
        self.name = name
        self.w = None
        self.r = {}
        self.streams = {}


class Stream:
    def __init__(self, K, step, kind="eng"):
        self.K = K
        self.step = step
        self.kind = kind
        self.si = None
        self.cnt = 0

    def next(self):
        if self.si is None or self.cnt + self.step > SEM_LIMIT:
            self.si, self.cnt = self.K.acquire_sem(self.kind)
        self.cnt += self.step
        return (self.si, self.cnt)

    def last(self):
        if self.si is None:
            return None
        return (self.si, self.cnt)


class Kern:
    def __init__(self, nc, stack, arena_words):
        self.nc = nc
        self.stack = stack
        self.sems = []
        self.ops = {e: [] for e in ENGS}
        self.seen = {e: {} for e in ENGS}
        self.estream = {e: Stream(self, 1) for e in ENGS}
        self.dma_streams = []
        self.dma_res = []
        self.pools = {}
        self.arena = stack.enter_context(nc.sbuf_tensor("arena", [128, arena_words], F32))
        self.arena_words = arena_words
        self.off = 0
        self.banks = [stack.enter_context(nc.psum_tensor("ps%d" % i, [128, 512], F32)) for i in range(8)]
        self.bank_res = [Res("bank%d" % i) for i in range(8)]

    def alloc(self, shape, dtype):
        n = int(np.prod(shape))
        esz = 4 if dtype == F32 else 2
        words = (n * esz + 3) // 4
        assert self.off + words <= self.arena_words, ("SBUF arena overflow", self.off, words)
        ap = self.arena[:, self.off:self.off + words]
        self.off += words
        if dtype != F32:
            ap = ap.bitcast(dtype)[:, :n]
        if len(shape) == 2:
            ap = ap.rearrange("p (a b) -> p a b", a=shape[0], b=shape[1])
        elif len(shape) == 3:
            ap = ap.rearrange("p (a b c) -> p a b c", a=shape[0], b=shape[1], c=shape[2])
        return ap

    def mark(self):
        return self.off

    def release(self, m):
        self.off = m

    def acquire_sem(self, kind="eng"):
        pool = self.pools.setdefault(kind, [])
        while pool:
            si, c = pool.pop()
            if c + 2048 <= SEM_LIMIT:
                return si, c
        s = self.stack.enter_context(self.nc.semaphore("s%d" % len(self.sems)))
        self.sems.append(s)
        return len(self.sems) - 1, 0

    def op(self, eng, fn, R=(), W=(), dma=None, step=16):
        waits = {}

        def need(ev):
            if ev is None:
                return
            si, v = ev
            if waits.get(si, 0) < v:
                waits[si] = v

        for r in R:
            need(r.w)
        for w in W:
            need(w.w)
            for ev in w.r.values():
                need(ev)
        own = self.estream[eng].si
        seen = self.seen[eng]
        wl = []
        for si, v in waits.items():
            if eng == "tensor" and si == own:
                continue
            if seen.get(si, 0) < v:
                seen[si] = v
                wl.append((si, v))
        if dma is not None:
            kind = "cc" if step == 1 else ("sw" if eng == "gpsimd" else "hw")
            st = dma.streams.get(kind)
            if st is None:
                st = Stream(self, step, kind)
                dma.streams[kind] = st
                self.dma_streams.append(st)
                self.dma_res.append(dma)
            ev = st.next()
            inc = st.step
        else:
            ev = self.estream[eng].next()
            inc = 1
        self.ops[eng].append((wl, fn, ev, inc))
        for r in R:
            r.r[ev[0]] = ev
        for w in W:
            w.w = ev
            w.r = {}
        return ev

    def barrier(self):
        evs = []
        for e in ENGS:
            ev = self.estream[e].last()
            if ev is not None:
                evs.append(ev)
        for s in self.dma_streams:
            ev = s.last()
            if ev is not None:
                evs.append(ev)
        for e in ENGS:
            seen = self.seen[e]
            wl = []
            for si, v in evs:
                if seen.get(si, 0) < v:
                    seen[si] = v
                    wl.append((si, v))
            self.ops[e].append((wl, None, None, 0))
        for st in self.dma_streams:
            if st.si is not None:
                self.pools.setdefault(st.kind, []).append((st.si, st.cnt))
        for r in self.dma_res:
            r.streams = {}
        self.dma_streams = []
        self.dma_res = []

    def emit(self):
        nc = self.nc
        K = self

        def run(name, e):
            for wl, fn, ev, inc in K.ops[name]:
                for si, v in wl:
                    e.wait_ge(K.sems[si], v)
                if fn is not None:
                    ins = fn(e)
                    ins.then_inc(K.sems[ev[0]], inc)

        with nc.Block() as block:
            @block.sync
            def _(e):
                run("sync", e)

            @block.gpsimd
            def _(e):
                run("gpsimd", e)

            @block.scalar
            def _(e):
                run("scalar", e)

            @block.vector
            def _(e):
                run("vector", e)

            @block.tensor
            def _(e):
                run("tensor", e)

    def dma(self, out, in_, R, W, res, q="sync", **kw):
        return self.op(q, lambda e: e.dma_start(out=out, in_=in_, **kw), R, W, dma=res)

    def mm(self, out, pairs, R, W, transpose_ident=None):
        n = len(pairs)

        def fn(e):
            ins = None
            for i, (a, b) in enumerate(pairs):
                ins = e.matmul(out, a, b, start=(i == 0), stop=(i == n - 1))
            return ins
        return self.op("tensor", fn, R, W)

    def act(self, out, in_, func, R, W, bias=None, scale=None, accum_out=None):
        kw = {}
        if bias is not None:
            kw["bias"] = bias
        if scale is not None:
            kw["scale"] = scale
        if accum_out is not None:
            kw["accum_out"] = accum_out
        return self.op("scalar", lambda e: e.activation(out, in_, func, **kw), R, W)


def build_program():
    nc = bass.Bass("TRN2", target_bir_lowering=False)
    stack = contextlib.ExitStack()
    with stack:
        T = {}

        in_names = []

        def din(name, shape, dt=F32):
            T[name] = nc.dram_tensor(name, list(shape), dt, kind="ExternalInput")
            in_names.append(name)

        din("x", [ROWS, D])
        din("cT", [128, 8, 3])
        din("ada_w", [2, D, 9 * D])
        din("ada_bT", [128, 2, 72])
        din("gpreT", [128, 6, 8])
        din("gpostT", [128, 6, 8])
        din("ffn1_w_up", [2, D, 2 * DFF])
        din("ffn1_w_down", [2, DFF, D])
        din("ffn2_w_up", [2, D, 2 * DFF])
        din("ffn2_w_down", [2, DFF, D])
        din("identb", [128, 128], BF16)
        din("identf", [128, 128])
        din("onesf", [128, 128])
        T["y"] = nc.dram_tensor("y", [ROWS, D], F32, kind="ExternalOutput")
        T["S0"] = nc.dram_tensor("S0", [ROWS, D], F32)
        T["S1"] = nc.dram_tensor("S1", [ROWS, D], F32)
        T["GB"] = nc.dram_tensor("GB", [18, 128, D], F32)
        din("ev_w_in", [1, D, 1536])
        din("ev_w_out", [1, D, D])
        din("wsT", [128, 4, 128])
        din("wpoolT", [128, 4, 128])
        din("bs_bc", [128, 512])
        din("lng_bc", [128, 512])
        din("lnb_bc", [128, 512])
        din("pscT", [128, 4])
        din("pbM", [5, 128, 512], BF16)
        din("pbH", [3, 16, 512], BF16)
        din("pbE", [2, 64, 512], BF16)
        T["ZB"] = nc.dram_tensor("ZB", [ROWS + 16, 512], BF16)
        T["EDGE"] = nc.dram_tensor("EDGE", [16, 512], BF16)
        T["EDGEG"] = nc.dram_tensor("EDGEG", [64, 512], BF16)
        din("od_w_in", [1, D, 2048])
        din("od_w_out", [1, D, D])
        din("fwT", [128, 4, 128])
        din("convT", [128, 4, 3])
        din("fg_bc", [128, 512])
        din("cbM", [5, 128, 384], BF16)
        din("cbH", [3, 16, 384], BF16)
        din("cbE", [2, 64, 384], BF16)
        din("CD", [128, 256], BF16)
        din("F128", [128, 512], BF16)
        din("F16", [16, 64], BF16)
        din("Fbd", [128, 512], BF16)
        din("GGp", [128, 128 * 2 * 32], BF16)
        din("GGs", [128, 16 * 2 * 128], BF16)
        T["ZC"] = nc.dram_tensor("ZC", [ROWS + 16, 512], BF16)
        T["EDGE2"] = nc.dram_tensor("EDGE2", [16, 512], BF16)
        T["EDGE2G"] = nc.dram_tensor("EDGE2G", [64, 512], BF16)
        for u in range(8):
            T["ABP%d" % u] = nc.dram_tensor("ABP%d" % u, [4096, 128], BF16)
            T["ABG%d" % u] = nc.dram_tensor("ABG%d" % u, [16384, 128], BF16)
            T["ABS%d" % u] = nc.dram_tensor("ABS%d" % u, [4096, 128], BF16)
        T["YDT"] = nc.dram_tensor("YDT", [512, ROWS], BF16)

        K = Kern(nc, stack, 53100)
        K.T = T
        K.dres = {}

        def dr(name, idx):
            key = (name, idx)
            if key not in K.dres:
                K.dres[key] = Res("%s_%s" % key)
            return K.dres[key]
        K.dr = dr

        prologue(K)
        K.barrier()
        passes = [(0, 0, "ffn1"), (0, 1, "even"), (0, 2, "ffn2"), (1, 0, "ffn1"), (1, 1, "odd"), (1, 2, "ffn2")]
        if NPASS_DEBUG is not None:
            passes = passes[:NPASS_DEBUG]
        src = "x"
        for pi, (l, sub, kind) in enumerate(passes):
            dst = "y" if pi == len(passes) - 1 else ("S0" if pi % 2 == 0 else "S1")
            m = K.mark()
            if kind in ("ffn1", "ffn2"):
                ffn_pass(K, l, sub, kind, src, dst)
            elif kind == "even":
                even_pass2(K, l, sub, src, dst)
            else:
                odd_pass2(K, l, sub, src, dst)
            K.barrier()
            K.release(m)
            src = dst
        K.emit()
    nc._in_names = in_names
    return nc


def prologue(K):
    nc, T = K.nc, K.T
    K.identb = K.alloc([128], BF16)
    r_identb = Res("identb")
    K.r_identb = r_identb
    K.dma(K.identb, T["identb"][:, :], [], [r_identb], r_identb)
    K.AT = K.alloc([6, 8, 3], F32)
    K.ST = K.alloc([6, 8, 3], F32)
    K.GT = K.alloc([6, 8, 3], F32)
    K.r_mod = Res("modvecs")
    K.stat = K.alloc([64], F32)
    K.stat_res = [Res("stat%d" % i) for i in range(64)]
    K.eps_rms = K.alloc([1], F32)
    m = K.mark()

    identf = K.alloc([128], F32)
    onesf = K.alloc([128], F32)
    r_identf, r_onesf = Res("identf"), Res("onesf")
    K.dma(identf, T["identf"][:, :], [], [r_identf], r_identf)
    K.dma(onesf, T["onesf"][:, :], [], [r_onesf], r_onesf)
    cT = K.alloc([8, 3], F32)
    adab = K.alloc([2, 72], F32)
    gpre = K.alloc([6, 8], F32)
    gpost = K.alloc([6, 8], F32)
    r_c, r_ab, r_gp, r_gq = Res("cT"), Res("adab"), Res("gpre"), Res("gpost")
    K.dma(cT, T["cT"][:, :, :], [], [r_c], r_c)
    K.dma(adab, T["ada_bT"][:, :, :], [], [r_ab], r_ab)
    K.dma(gpre, T["gpreT"][:, :, :], [], [r_gp], r_gp)
    K.dma(gpost, T["gpostT"][:, :, :], [], [r_gq], r_gq)
    scT = K.alloc([8, 3], BF16)
    r_sc = Res("scT")
    K.act(scT, cT, AF.Silu, [r_c], [r_sc])
    modT = K.alloc([2, 72, 3], F32)
    r_modT = Res("modT")
    aw = [K.alloc([8, 512], BF16) for _ in range(4)]
    r_aw = [Res("aw%d" % i) for i in range(4)]
    it = 0
    for l in range(2):
        bank = K.banks[l]
        rb = K.bank_res[l]
        for fb in range(18):
            slot = it % 4
            it += 1
            src = T["ada_w"][l, :, fb * 512:(fb + 1) * 512].rearrange("(k p) f -> p k f", p=128)
            K.dma(aw[slot], src, [], [r_aw[slot]], r_aw[slot], q="gpsimd")
            for ch in range(4):
                cidx = fb * 4 + ch
                pairs = [(aw[slot][:, k, ch * 128:(ch + 1) * 128], scT[:, k, :]) for k in range(8)]
                K.mm(bank[:, cidx * 3:(cidx + 1) * 3], pairs, [r_aw[slot], r_sc], [rb])
        ps = bank[:, 0:216].rearrange("p (a b) -> p a b", a=72, b=3)
        bia = adab[:, l, :].unsqueeze(2).to_broadcast([128, 72, 3])
        K.op("vector", lambda e, o=modT[:, l], a=ps, b=bia: e.tensor_tensor(o, a, b, ALU.add),
             [rb, r_ab], [r_modT])
    for l in range(2):
        for sub in range(3):
            ls = l * 3 + sub
            sh = modT[:, l, (sub * 3 + 0) * 8:(sub * 3 + 1) * 8, :]
            sc = modT[:, l, (sub * 3 + 1) * 8:(sub * 3 + 2) * 8, :]
            ga = modT[:, l, (sub * 3 + 2) * 8:(sub * 3 + 3) * 8, :]
            gp = gpre[:, ls, :].unsqueeze(2).to_broadcast([128, 8, 3])
            gq = gpost[:, ls, :].unsqueeze(2).to_broadcast([128, 8, 3])
            K.op("vector", lambda e, o=K.AT[:, ls], a=sc, b=gp: e.scalar_tensor_tensor(
                out=o, in0=a, scalar=1.0, in1=b, op0=ALU.add, op1=ALU.mult), [r_modT, r_gp], [K.r_mod])
            K.op("vector", lambda e, o=K.ST[:, ls], a=sh: e.tensor_copy(o, a), [r_modT], [K.r_mod])
            K.op("vector", lambda e, o=K.GT[:, ls], a=ga, b=gq: e.scalar_tensor_tensor(
                out=o, in0=a, scalar=1.0, in1=b, op0=ALU.add, op1=ALU.mult), [r_modT, r_gq], [K.r_mod])
            if sub != 1:
                K.op("vector", lambda e, o=K.GT[:, ls]: e.tensor_scalar(o, o, 0.5, None, ALU.mult),
                     [K.r_mod], [K.r_mod])
    K.op("vector", lambda e: e.memset(K.eps_rms, RMS_EPS), [], [K.r_mod])
    Dg = [K.alloc([8, 128], F32) for _ in range(2)]
    r_Dg = [Res("Dg0"), Res("Dg1")]
    gb = [K.alloc([D], F32) for _ in range(2)]
    r_gb = [Res("gb0"), Res("gb1")]
    it = 0
    for ls in range(6):
        for s in range(3):
            slot = it % 2
            it += 1
            for c in range(8):
                K.op("vector", lambda e, o=Dg[slot][:, c, :], sc1=K.GT[:, ls, c, s:s + 1]: e.tensor_scalar(
                    o, identf, sc1, None, ALU.mult), [r_identf, K.r_mod], [r_Dg[slot]])
            for h in range(2):
                b = 2 + h
                for cc in range(4):
                    c = h * 4 + cc
                    K.mm(K.banks[b][:, cc * 128:(cc + 1) * 128], [(onesf, Dg[slot][:, c, :])],
                         [r_onesf, r_Dg[slot]], [K.bank_res[b]])
                K.act(gb[slot][:, h * 512:(h + 1) * 512], K.banks[b][:, :], AF.Copy,
                      [K.bank_res[b]], [r_gb[slot]])
            K.dma(T["GB"][ls * 3 + s, :, :], gb[slot], [r_gb[slot]], [K.dr("GB", ls * 3 + s)], r_gb[slot])
    K.barrier()
    K.release(m)


def load_weight_cast(K, dst, src, res):
    K.dma(dst, src, [], [res], res, q="gpsimd")


def pre_norm_A(K, g, ls, srcname, bufs):
    T = K.T
    xin, r_xin, xn, r_xn = bufs["xin"], bufs["r_xin"], bufs["xn"], bufs["r_xn"]
    gi = g % 2
    ss4 = K.stat[:, gi * 8:gi * 8 + 4]
    rs4 = K.stat[:, gi * 8 + 4:gi * 8 + 8]
    r_ss = K.stat_res[gi * 8:gi * 8 + 4]
    r_rs = K.stat_res[gi * 8 + 4]
    for j in range(4):
        t = g * 4 + j
        xs, rxs = xin[t % len(xin)], r_xin[t % len(xin)]
        K.dma(xs, T[srcname][t * 128:(t + 1) * 128, :], [K.dr(srcname, g)], [rxs], rxs)
    for j in range(4):
        t = g * 4 + j
        xs, rxs = xin[t % len(xin)], r_xin[t % len(xin)]
        K.act(xn[t % len(xn)], xs, AF.Square, [rxs], [r_xn[t % len(xn)], r_ss[j]], accum_out=ss4[:, j:j + 1])
    K.act(rs4, ss4, AF.Sqrt, r_ss + [K.r_mod], [r_rs], bias=K.eps_rms, scale=1.0 / D)
    K.op("vector", lambda e, o=rs4: e.reciprocal(o, o), [r_rs], [r_rs])
    for j in range(4):
        t = g * 4 + j
        xs, rxs = xin[t % len(xin)], r_xin[t % len(xin)]
        if j % 2 == 0:
            K.op("vector", lambda e, o=xn[t % len(xn)], a=xs, b=rs4[:, j:j + 1]: e.tensor_scalar(o, a, b, None, ALU.mult),
                 [rxs, r_rs], [r_xn[t % len(xn)]])
        else:
            K.act(xn[t % len(xn)], xs, AF.Identity, [rxs, r_rs], [r_xn[t % len(xn)]], scale=rs4[:, j:j + 1])


def pre_norm_B(K, g, ls, srcname, bufs, halves=(0, 1)):
    s = seq_of_group(g)
    xn, r_xn, hT, r_hT = bufs["xn"], bufs["r_xn"], bufs["hT"], bufs["r_hT"]
    if "hT2" in bufs:
        hT, r_hT = bufs["hT2"][g % 2], bufs["r_hT2"][g % 2]
    pts = [K.banks[6].bitcast(BF16), K.banks[7].bitcast(BF16)]
    for half in halves:
        for j in range(4):
            t = g * 4 + j
            xns, rxns = xn[t % len(xn)], r_xn[t % len(xn)]

            def fn(e, xns=xns, j=j, half=half):
                ins = None
                for cl in range(4):
                    c = half * 4 + cl
                    off = (cl % 2) * 512 + j * 128
                    ins = e.transpose(pts[cl // 2][:, off:off + 128], xns[:, c * 128:(c + 1) * 128], K.identb)
                return ins
            K.op("tensor", fn, [rxns, K.r_identb], [K.bank_res[6], K.bank_res[7]])
        for cl in range(4):
            c = half * 4 + cl
            tb = 6 + cl // 2
            o = hT[:, c, :]
            i_ = pts[cl // 2][:, (cl % 2) * 512:(cl % 2 + 1) * 512]
            a_ = K.AT[:, ls, c, s:s + 1]
            b_ = K.ST[:, ls, c, s:s + 1]
            if c % 2 == 0:
                K.act(o, i_, AF.Identity, [K.bank_res[tb], K.r_mod], [r_hT], bias=b_, scale=a_)
            else:
                K.op("vector", lambda e, o=o, i_=i_, a_=a_, b_=b_: e.tensor_scalar(o, i_, a_, b_, ALU.mult, ALU.add),
                     [K.bank_res[tb], K.r_mod], [r_hT])


def pre_norm_group(K, g, ls, srcname, bufs):
    pre_norm_A(K, g, ls, srcname, bufs)
    pre_norm_B(K, g, ls, srcname, bufs)


def load_gbc_seq(K, ls, s, bufs):
    K.dma(bufs["Gbc"][s], K.T["GB"][ls * 3 + s, :, :], [K.dr("GB", ls * 3 + s)], [bufs["r_G"][s]], bufs["r_G"][s])


def load_gbc(K, ls, bufs):
    for s in range(3):
        K.dma(bufs["Gbc"][s], K.T["GB"][ls * 3 + s, :, :], [K.dr("GB", ls * 3 + s)], [bufs["r_G"][s]], bufs["r_G"][s])


def common_bufs(K, n_gbc=3, n_xin=4, n_hT=1):
    n_x = 8 if n_hT == 2 else 4
    b = {}
    b["xin"] = [K.alloc([D], F32) for _ in range(n_x)]
    b["r_xin"] = [Res("xin%d" % i) for i in range(n_x)]
    b["xres"] = [K.alloc([D], F32) for _ in range(2)]
    b["r_xres"] = [Res("xres%d" % i) for i in range(2)]
    b["xn"] = [K.alloc([D], BF16) for _ in range(n_x)]
    b["r_xn"] = [Res("xn%d" % i) for i in range(n_x)]
    b["hT"] = K.alloc([8, 512], BF16)
    b["r_hT"] = Res("hT")
    if n_hT == 2:
        b["hT2"] = [b["hT"], K.alloc([8, 512], BF16)]
        b["r_hT2"] = [b["r_hT"], Res("hTb")]
    if n_gbc == 3:
        b["Gbc"] = [K.alloc([D], F32) for _ in range(3)]
        b["r_G"] = [Res("G%d" % i) for i in range(3)]
    else:
        g1, r1 = K.alloc([D], F32), Res("G")
        b["Gbc"] = [g1, g1, g1]
        b["r_G"] = [r1, r1, r1]
    b["junk"] = K.alloc([512], BF16)
    b["r_junk"] = Res("junk")
    b["tmp"] = [K.alloc([512], F32) for _ in range(2)]
    b["r_tmp"] = [Res("tmp%d" % i) for i in range(2)]
    return b


def ffn_pass(K, l, sub, kind, srcname, dstname):
    T = K.T
    ls = l * 3 + sub
    Wup = K.alloc([8, 2 * DFF], BF16)
    Wdn = K.alloc([22, D], BF16)
    r_Wupb = [Res("Wup%d" % b) for b in range(11)]
    r_Wdn = Res("Wdn")
    wu = T[kind + "_w_up"]
    wd = T[kind + "_w_down"]
    for b in range(11):
        for off in (0, DFF):
            c0 = off + b * 256
            load_weight_cast(K, Wup[:, :, c0:c0 + 256],
                             wu[l, :, c0:c0 + 256].rearrange("(k p) f -> p k f", p=128), r_Wupb[b])
    for q in range(2):
        load_weight_cast(K, Wdn[:, q * 11:(q + 1) * 11, :],
                         wd[l, q * 11 * 128:(q + 1) * 11 * 128, :].rearrange("(f p) d -> p f d", p=128), r_Wdn)
    bufs = common_bufs(K, n_gbc=1)
    gT = K.alloc([22, 512], BF16)
    r_gT = [Res("gT%d" % i) for i in range(22)]
    sil, r_sil = bufs["tmp"], bufs["r_tmp"]
    hT, r_hT = bufs["hT"], bufs["r_hT"]

    def up(g):
        for fc in range(22):
            if fc == 6 and g + 1 < NG:
                pre_norm_A(K, g + 1, ls, srcname, bufs)
            bg, bu = (0, 1) if fc % 2 == 0 else (2, 3)
            pg = [(Wup[:, k, fc * 128:(fc + 1) * 128], hT[:, k, :]) for k in range(8)]
            pu = [(Wup[:, k, DFF + fc * 128:DFF + (fc + 1) * 128], hT[:, k, :]) for k in range(8)]
            K.mm(K.banks[bg][:, :], pg, [r_Wupb[fc // 2], r_hT], [K.bank_res[bg]])
            K.mm(K.banks[bu][:, :], pu, [r_Wupb[fc // 2], r_hT], [K.bank_res[bu]])
            sl, rsl = sil[fc % 2], r_sil[fc % 2]
            K.act(sl, K.banks[bg][:, :], AF.Silu, [K.bank_res[bg]], [rsl])
            K.op("vector", lambda e, o=gT[:, fc, :], a=K.banks[bu][:, :], b=sl: e.tensor_tensor(o, a, b, ALU.mult),
                 [K.bank_res[bu], rsl], [r_gT[fc]])

    def down_post(g):
        def ybanks(j):
            return (4, 5)
        for j in range(4):
            yb = (4, 5) if j % 2 == 0 else (2, 3)
            for h in range(2):
                b = yb[h]
                pairs = [(gT[:, fc, j * 128:(j + 1) * 128], Wdn[:, fc, h * 512:(h + 1) * 512]) for fc in range(22)]
                K.mm(K.banks[b][:, :], pairs, r_gT + [r_Wdn], [K.bank_res[b]])
            post_norm_tile(K, g, j, ls, srcname, dstname, bufs, yb)

    pre_norm_group(K, 0, ls, srcname, bufs)
    for g in range(NG):
        up(g)
        if g + 1 < NG:
            pre_norm_B(K, g + 1, ls, srcname, bufs)
        if g in (0, 8, 12):
            load_gbc_seq(K, ls, seq_of_group(g), bufs)
        down_post(g)


def post_norm_tile(K, g, j, ls, srcname, dstname, bufs, yb):
    T = K.T
    s = seq_of_group(g)
    xres, r_xres, Gbc, r_G, junk, r_junk = (bufs["xres"], bufs["r_xres"], bufs["Gbc"], bufs["r_G"],
                                            bufs["junk"], bufs["r_junk"])
    t = g * 4 + j
    xs, rxs = xres[t % 2], r_xres[t % 2]
    K.dma(xs, T[srcname][t * 128:(t + 1) * 128, :], [K.dr(srcname, g)], [rxs], rxs)
    b0, b1 = yb
    si = 16 + (t % 4) * 4
    st = [(K.stat[:, si + i:si + i + 1], K.stat_res[si + i]) for i in range(4)]
    K.act(junk, K.banks[b0][:, :], AF.Square, [K.bank_res[b0]], [r_junk, st[0][1]], accum_out=st[0][0])
    K.act(junk, K.banks[b1][:, :], AF.Square, [K.bank_res[b1]], [r_junk, st[1][1]], accum_out=st[1][0])
    K.op("vector", lambda e, o=st[2][0], a=st[0][0], b=st[1][0]: e.tensor_tensor(o, a, b, ALU.add),
         [st[0][1], st[1][1]], [st[2][1]])
    K.act(st[2][0], st[2][0], AF.Sqrt, [st[2][1], K.r_mod], [st[2][1]], bias=K.eps_rms, scale=1.0 / D)
    K.op("vector", lambda e, o=st[3][0], a=st[2][0]: e.reciprocal(o, a), [st[2][1]], [st[3][1]])
    for h, b in ((0, b0), (1, b1)):
        tmp = bufs["tmp"][h]
        r_tmp = bufs["r_tmp"][h]
        K.op("vector", lambda e, o=tmp, a=K.banks[b][:, :], sc=st[3][0], g_=Gbc[s][:, h * 512:(h + 1) * 512]:
             e.scalar_tensor_tensor(out=o, in0=a, scalar=sc, in1=g_, op0=ALU.mult, op1=ALU.mult),
             [K.bank_res[b], st[3][1], r_G[s]], [r_tmp])
        K.op("gpsimd" if h == 0 else "vector",
             lambda e, o=xs[:, h * 512:(h + 1) * 512], a=tmp: e.tensor_tensor(o, o, a, ALU.add),
             [r_tmp, rxs], [rxs])
    K.dma(T[dstname][t * 128:(t + 1) * 128, :], xs, [rxs], [K.dr(dstname, g)], rxs, q="gpsimd")


_NC_CACHE = {}


def _fm(v, chunks):
    v = np.asarray(v, np.float32)
    lead = v.shape[:-1]
    v = v.reshape(lead + (chunks, 128))
    return np.ascontiguousarray(np.moveaxis(v, -1, 0))


def _band_build(kind, pos, S):
    n = 4 if kind == "pool" else 3
    M = np.zeros((128, n, 128), np.float64)
    H = np.zeros((16, n, 128), np.float64)

    def put(rel, i, t, val):
        if 0 <= rel < 128:
            M[rel, i, t] += val
        elif -8 <= rel < 0:
            H[rel + 8, i, t] += val
        elif 128 <= rel < 136:
            H[rel - 128 + 8, i, t] += val
        else:
            raise AssertionError
    for t in range(128):
        Tt = pos * 128 + t
        if kind == "pool":
            for i, w in enumerate((2, 4, 8, 16)):
                lo = max(Tt - w // 2, 0)
                hi = min(Tt + w // 2, S)
                for tp in range(lo, hi):
                    put(tp - pos * 128, i, t, 1.0 / (hi - lo))
                M[t, i, t] -= 1.0
        else:
            for i, dlt in enumerate((-1, 0, 1)):
                tp = Tt + dlt
                if 0 <= tp < S:
                    put(tp - pos * 128, i, t, 1.0)
    return M, H


_TAB_CACHE = {}


def _core_tables(r):
    if r in _TAB_CACHE:
        return _TAB_CACHE[r]
    bf = ml_dtypes.bfloat16
    out = {}
    for kind, pre in (("pool", "pb"), ("conv", "cb")):
        n = 4 if kind == "pool" else 3
        Mi, Hi = _band_build(kind, 1, 384)
        Mf, Hf = _band_build(kind, 0, 384)
        Ml, Hl = _band_build(kind, 2, 384)
        E0 = np.zeros((64, n, 128))
        E1 = np.zeros((64, n, 128))
        if r == 0:
            M0 = Mf
        else:
            M0 = Mi
            E0[(r - 1) * 16 + 8:(r - 1) * 16 + 16] = Hi[0:8]
        if r == 3:
            M31 = Ml
        else:
            M31 = Mi
            E1[(r + 1) * 16:(r + 1) * 16 + 8] = Hi[8:16]
        out[pre + "M"] = np.stack([Mi, Mf, Ml, M0, M31]).reshape(5, 128, n * 128).astype(np.float32).astype(bf)
        out[pre + "H"] = np.stack([Hi, Hf, Hl]).reshape(3, 16, n * 128).astype(np.float32).astype(bf)
        out[pre + "E"] = np.stack([E0, E1]).reshape(2, 64, n * 128).astype(np.float32).astype(bf)
    two_pi = 2.0 * np.pi
    dd = np.arange(128)
    ang = two_pi * ((dd[:, None] * dd[None, :]) % 128) / 128.0
    out["CD"] = np.concatenate([np.cos(ang)[:, 0:64], np.sin(ang)[:, 0:64], np.cos(ang)[:, 64:128],
                                np.sin(ang)[:, 64:128]], axis=1).astype(np.float32).astype(bf)
    c, s_ = np.cos(ang), np.sin(ang)
    out["F128"] = np.concatenate([c, -s_, -s_, -c], axis=1).astype(np.float32).astype(bf)
    a16 = np.arange(16)
    ang16 = two_pi * ((a16[:, None] * a16[None, :]) % 16) / 16.0
    c, s_ = np.cos(ang16), np.sin(ang16)
    out["F16"] = np.concatenate([c, -s_, -s_, -c], axis=1).astype(np.float32).astype(bf)
    fbd = np.zeros((128, 2, 8, 2, 16), np.float64)
    for u_ in range(8):
        fbd[u_ * 16:(u_ + 1) * 16, 0, u_, 0, :] = c
        fbd[u_ * 16:(u_ + 1) * 16, 0, u_, 1, :] = -s_
        fbd[u_ * 16:(u_ + 1) * 16, 1, u_, 0, :] = -s_
        fbd[u_ * 16:(u_ + 1) * 16, 1, u_, 1, :] = -c
    out["Fbd"] = fbd.reshape(128, 512).astype(np.float32).astype(bf)
    b_ = np.arange(128)[:, None, None]
    k1 = np.arange(128)[None, :, None]
    k2 = (32 * r + np.arange(32))[None, None, :]
    th = two_pi * ((b_ * (k1 + 128 * k2)) % 16384) / 16384.0
    nrm = 1.0 / np.sqrt(16384.0 * 128.0)
    out["GGp"] = np.stack([np.cos(th) * nrm, np.sin(th) * nrm], axis=2).reshape(128, -1).astype(np.float32).astype(bf)
    k1 = np.arange(16)[None, :, None]
    k2 = np.arange(128)[None, None, :]
    th = two_pi * ((b_ * (k1 + 16 * k2)) % 2048) / 2048.0
    nrm = 1.0 / np.sqrt(2048.0 * 128.0)
    out["GGs"] = np.stack([np.cos(th) * nrm, np.sin(th) * nrm], axis=2).reshape(128, -1).astype(np.float32).astype(bf)
    _TAB_CACHE[r] = out
    return out


def kernel(x_prompt, x_sample, c_prompt, c_sample, ada_w, ada_b, norm_pre, norm_post,
           ffn1_w_up, ffn1_w_down, ffn2_w_up, ffn2_w_down,
           ev_w_in, ev_ln_g, ev_ln_b, ev_w_spatial, ev_b_spatial, ev_w_pool, ev_pool_scale, ev_w_out,
           od_w_in, od_conv_w, od_fourier_g, od_fourier_w, od_w_out):
    f32 = np.float32
    if "nc" not in _NC_CACHE:
        _NC_CACHE["nc"] = build_program()
    nc = _NC_CACHE["nc"]
    x_prompt = np.asarray(x_prompt, f32)
    x_sample = np.asarray(x_sample, f32)
    shared = {
        "ada_w": np.ascontiguousarray(np.asarray(ada_w, f32)),
        "ada_bT": np.ascontiguousarray(_fm(ada_b, 72)),
        "gpreT": np.ascontiguousarray(_fm(np.asarray(norm_pre, f32).reshape(6, D), 8)),
        "gpostT": np.ascontiguousarray(_fm(np.asarray(norm_post, f32).reshape(6, D), 8)),
        "ffn1_w_up": np.ascontiguousarray(np.asarray(ffn1_w_up, f32)),
        "ffn1_w_down": np.ascontiguousarray(np.asarray(ffn1_w_down, f32)),
        "ffn2_w_up": np.ascontiguousarray(np.asarray(ffn2_w_up, f32)),
        "ffn2_w_down": np.ascontiguousarray(np.asarray(ffn2_w_down, f32)),
        "identb": np.eye(128, dtype=f32).astype(ml_dtypes.bfloat16),
        "identf": np.eye(128, dtype=f32),
        "onesf": np.ones((128, 128), f32),
    }
    bf = ml_dtypes.bfloat16
    tile128 = lambda v: np.ascontiguousarray(np.broadcast_to(np.asarray(v, f32).reshape(1, -1), (128, np.asarray(v).size)))
    shared.update({
        "ev_w_in": np.ascontiguousarray(np.asarray(ev_w_in, f32)),
        "ev_w_out": np.ascontiguousarray(np.asarray(ev_w_out, f32)),
        "wsT": np.ascontiguousarray(np.transpose(np.asarray(ev_w_spatial, f32)[0], (2, 0, 1))),
        "wpoolT": np.ascontiguousarray(np.transpose(np.asarray(ev_w_pool, f32)[0], (1, 0, 2))),
        "bs_bc": tile128(np.asarray(ev_b_spatial, f32)[0]),
        "lng_bc": tile128(np.asarray(ev_ln_g, f32)[0]),
        "lnb_bc": tile128(np.asarray(ev_ln_b, f32)[0]),
        "pscT": np.ascontiguousarray(np.asarray(ev_pool_scale, f32)[0].reshape(4, 128).T),
        "od_w_in": np.ascontiguousarray(np.asarray(od_w_in, f32)),
        "od_w_out": np.ascontiguousarray(np.asarray(od_w_out, f32)),
        "fwT": np.ascontiguousarray(np.transpose(np.asarray(od_fourier_w, f32)[0], (1, 0, 2))),
        "convT": np.ascontiguousarray(np.transpose(np.asarray(od_conv_w, f32)[0].reshape(3, 4, 128), (2, 1, 0))),
        "fg_bc": tile128(np.asarray(od_fourier_g, f32)[0]),
    })
    in_maps = []
    for i in range(8):
        b, r = i // 4, i % 4
        xc = np.concatenate([x_prompt[b, r * 4096:(r + 1) * 4096], x_sample[2 * i], x_sample[2 * i + 1]], axis=0)
        cc = np.stack([np.asarray(c_prompt, f32)[b], np.asarray(c_sample, f32)[2 * i],
                       np.asarray(c_sample, f32)[2 * i + 1]], axis=0)
        cT = np.ascontiguousarray(np.transpose(cc.reshape(3, 8, 128), (2, 1, 0)))
        m = dict(shared)
        m["x"] = np.ascontiguousarray(xc)
        m["cT"] = cT
        m.update(_core_tables(r))
        in_maps.append(m)
    in_maps = [{k: m[k] for k in nc._in_names} for m in in_maps]
    res = run_bass_kernel_spmd(nc, in_maps, core_ids=list(range(8)))
    y_prompt = np.empty((2, 16384, D), f32)
    y_sample = np.empty((16, 2048, D), f32)
    for i in range(8):
        b, r = i // 4, i % 4
        y = res.results[i]["y"]
        y_prompt[b, r * 4096:(r + 1) * 4096] = y[0:4096]
        y_sample[2 * i] = y[4096:6144]
        y_sample[2 * i + 1] = y[6144:8192]
    return (y_prompt, y_sample)


GROUPS4 = [[0, 1, 2, 3], [4, 5, 6, 7]]
MV = {"int": 0, "first": 1, "last": 2, "p0": 3, "p31": 4}
HV = {"int": 0, "first": 1, "last": 2, "p0": 1, "p31": 2}
EV = {"p0": 0, "p31": 1}


def tile_variant(t):
    if t == 0:
        return "p0"
    if t == 31:
        return "p31"
    if t in (32, 48):
        return "first"
    if t in (47, 63):
        return "last"
    return "int"


def allgather(K, src_t, dst_t, R, W):
    cres = Res("cc_" + src_t.name)
    K.op("gpsimd", lambda e: e.collective_compute(
        "AllGather", ALU.bypass, replica_groups=GROUPS4,
        ins=[src_t.ap().opt()], outs=[dst_t.ap().opt()]), R, W, dma=cres, step=1)


def zero_pads(K, ZN):
    T = K.T
    z = K.alloc([512], BF16)
    rz = Res("zpad")
    K.op("vector", lambda e: e.memset(z[0:16, :], 0.0), [], [rz])
    K.dma(T[ZN][0:8, :], z[0:8, :], [rz], [K.dr(ZN, "pad0")], rz)
    K.dma(T[ZN][8 + ROWS:16 + ROWS, :], z[0:8, :], [rz], [K.dr(ZN, "pad1")], rz)


def store_rows_with_edges(K, zt, rzt, t, ZN, EN):
    T = K.T
    K.dma(T[ZN][8 + t * 128:8 + (t + 1) * 128, :], zt, [rzt], [K.dr(ZN, t)], rzt)
    if t == 0:
        K.dma(T[EN][0:8, :], zt[0:8, :], [rzt], [K.dr(EN, 0)], rzt)
    if t == 31:
        K.dma(T[EN][8:16, :], zt[120:128, :], [rzt], [K.dr(EN, 1)], rzt)


def load_tile_with_halo(K, t, ZN, EGN, zt, rzt, ht, rht, et, ret):
    T = K.T
    K.dma(zt, T[ZN][8 + t * 128:8 + (t + 1) * 128, :], [K.dr(ZN, t)], [rzt], rzt)
    deps = [K.dr(ZN, "pad0"), K.dr(ZN, "pad1")]
    if t > 0:
        deps.append(K.dr(ZN, t - 1))
    if t < NT - 1:
        deps.append(K.dr(ZN, t + 1))
    K.dma(ht[0:8, :], T[ZN][t * 128:t * 128 + 8, :], deps, [rht], rht)
    K.dma(ht[8:16, :], T[ZN][8 + (t + 1) * 128:16 + (t + 1) * 128, :], deps, [rht], rht)
    if t in (0, 31):
        K.dma(et[0:64, :], T[EGN][:, :], [K.dr(EGN, 0)], [ret], ret)


def bandmix(K, t, outs, zt, rzt, ht, rht, et, ret, bM, bH, bE, r_tab):
    var = tile_variant(t)
    for (o, b, cc, tc) in outs:
        pairs = [(zt[:, cc * 128:(cc + 1) * 128], bM[MV[var]][:, tc * 128:(tc + 1) * 128]),
                 (ht[0:16, cc * 128:(cc + 1) * 128], bH[HV[var]][0:16, tc * 128:(tc + 1) * 128])]
        R = [rzt, rht, r_tab]
        if var in EV:
            pairs.append((et[0:64, cc * 128:(cc + 1) * 128], bE[EV[var]][0:64, tc * 128:(tc + 1) * 128]))
            R.append(ret)
        K.mm(o, pairs, R, [K.bank_res[b]])


def load_const(K, shape, dtype, src_ap, name, q="sync"):
    a = K.alloc(shape, dtype)
    r = Res(name)
    K.dma(a, src_ap, [], [r], r, q=q)
    return a, r


def wout_post(K, g, ls, srcname, dstname, bufs, yT, r_yT, Wout, r_Wout):
    for j in range(4):
        tc = slice(j * 128, (j + 1) * 128)
        yb = (4, 5) if j % 2 == 0 else (2, 3)
        for h in range(2):
            b = yb[h]
            K.mm(K.banks[b][:, :], [(yT[:, kc, tc], Wout[:, kc, h * 512:(h + 1) * 512]) for kc in range(8)],
                 [r_yT, r_Wout], [K.bank_res[b]])
        post_norm_tile(K, g, j, ls, srcname, dstname, bufs, yb)


def even_pass2(K, l, sub, srcname, dstname):
    T = K.T
    ls = l * 3 + sub
    j_ = l // 2
    m0 = K.mark()
    bufs = common_bufs(K, n_hT=2)
    Wz = K.alloc([8, 512], BF16)
    r_Wz = Res("Wz")
    load_weight_cast(K, Wz, T["ev_w_in"][j_, :, 1024:1536].rearrange("(k p) f -> p k f", p=128), r_Wz)
    zero_pads(K, "ZB")
    zb = [K.alloc([512], BF16) for _ in range(4)]
    r_zb = [Res("zb%d" % i) for i in range(4)]
    pre_norm_group(K, 0, ls, srcname, bufs)
    pre_norm_A(K, 1, ls, srcname, bufs)
    for g in range(NG):
        hT, r_hT = bufs["hT2"][g % 2], bufs["r_hT2"][g % 2]
        if g + 2 < NG:
            pre_norm_A(K, g + 2, ls, srcname, bufs)
        if g + 1 < NG:
            pre_norm_B(K, g + 1, ls, srcname, bufs)
        for j in range(4):
            b = j
            K.mm(K.banks[b][:, :], [(hT[:, k, j * 128:(j + 1) * 128], Wz[:, k, :]) for k in range(8)],
                 [r_hT, r_Wz], [K.bank_res[b]])
        for j in range(4):
            t = g * 4 + j
            if j % 2 == 0:
                K.act(zb[j], K.banks[j][:, :], AF.Copy, [K.bank_res[j]], [r_zb[j]])
            else:
                K.op("vector", lambda e, o=zb[j], a=K.banks[j][:, :]: e.tensor_copy(o, a), [K.bank_res[j]], [r_zb[j]])
            store_rows_with_edges(K, zb[j], r_zb[j], t, "ZB", "EDGE")
    K.barrier()
    K.release(m0)
    bufs = common_bufs(K, n_hT=2)
    Win = K.alloc([8, 1024], BF16)
    r_Win = Res("Win")
    load_weight_cast(K, Win, T["ev_w_in"][j_, :, 0:1024].rearrange("(k p) f -> p k f", p=128), r_Win)
    Wout = K.alloc([8, D], BF16)
    r_Wout = Res("Wout")
    load_weight_cast(K, Wout, T["ev_w_out"][j_, :, :].rearrange("(k p) f -> p k f", p=128), r_Wout)
    WsT, r_WsT = load_const(K, [4, 128], BF16, T["wsT"][:, :, :], "WsT", q="gpsimd")
    Wp, r_Wp = load_const(K, [4, 128], BF16, T["wpoolT"][:, :, :], "Wp", q="gpsimd")
    allgather(K, T["EDGE"], T["EDGEG"], [K.dr("EDGE", 0), K.dr("EDGE", 1)], [K.dr("EDGEG", 0)])
    bsb, r_bsb = load_const(K, [512], F32, T["bs_bc"][:, :], "bsb")
    lng, r_lng = load_const(K, [512], F32, T["lng_bc"][:, :], "lng")
    lnb, r_lnb = load_const(K, [512], F32, T["lnb_bc"][:, :], "lnb")
    psc, r_psc = load_const(K, [4], F32, T["pscT"][:, :], "psc")
    r_tab = Res("pooltabs")
    bM = [K.alloc([512], BF16) for _ in range(5)]
    bH = [K.alloc([512], BF16) for _ in range(3)]
    bE = [K.alloc([512], BF16) for _ in range(2)]
    for i in range(5):
        K.dma(bM[i], T["pbM"][i, :, :], [], [r_tab], r_tab)
    for i in range(3):
        K.dma(bH[i][0:16, :], T["pbH"][i, :, :], [], [r_tab], r_tab)
    for i in range(2):
        K.dma(bE[i][0:64, :], T["pbE"][i, :, :], [], [r_tab], r_tab)
    epsln = K.alloc([1], F32)
    r_eps = Res("epsln")
    K.op("vector", lambda e: e.memset(epsln, LN_EPS), [], [r_eps])
    load_gbc(K, ls, bufs)
    uT = K.alloc([4, 512], F32)
    r_uT = [Res("uT%d" % i) for i in range(4)]
    v = [K.alloc([512], F32) for _ in range(4)]
    r_v = [Res("v%d" % i) for i in range(4)]
    vn = [K.alloc([512], BF16) for _ in range(4)]
    r_vn = [Res("vn%d" % i) for i in range(4)]
    tya = [K.alloc([512], F32) for _ in range(2)]
    r_tya = [Res("tya0"), Res("tya1")]
    yT2 = [K.alloc([8, 512], BF16) for _ in range(2)]
    r_yT2 = [Res("yTa"), Res("yTb")]
    zt = [K.alloc([512], BF16) for _ in range(4)]
    r_zt = [Res("zt%d" % i) for i in range(4)]
    ht = [K.alloc([512], BF16) for _ in range(4)]
    r_ht = [Res("ht%d" % i) for i in range(4)]
    et = K.alloc([512], BF16)
    r_et = Res("et")
    dT = [K.alloc([512], BF16) for _ in range(4)]
    r_dT = [Res("dT%d" % i) for i in range(4)]
    lst = K.alloc([7, 4], F32)
    r_lst = [Res("lst%d" % i) for i in range(7)]
    r_s1 = [Res("s1_%d" % i) for i in range(4)]
    r_s2 = [Res("s2_%d" % i) for i in range(4)]
    junk, r_junk = bufs["junk"], bufs["r_junk"]
    s1, s2, mean, msq, var, rstd, nmr = [lst[:, i, :] for i in range(7)]

    pre_norm_group(K, 0, ls, srcname, bufs)
    pre_norm_A(K, 1, ls, srcname, bufs)
    for g in range(NG):
        hT, r_hT = bufs["hT2"][g % 2], bufs["r_hT2"][g % 2]
        yT, r_yT = yT2[g % 2], r_yT2[g % 2]
        if g + 2 < NG:
            pre_norm_A(K, g + 2, ls, srcname, bufs)
        for j in range(4):
            load_tile_with_halo(K, g * 4 + j, "ZB", "EDGEG", zt[j], r_zt[j], ht[j], r_ht[j], et, r_et)
        for j in range(4):
            b = 2 + j % 2
            K.mm(K.banks[b][:, :], [(hT[:, k, j * 128:(j + 1) * 128], Win[:, k, 512:1024]) for k in range(8)],
                 [r_Win, r_hT], [K.bank_res[b]])
            K.act(v[j], K.banks[b][:, :], AF.Gelu, [K.bank_res[b]], [r_v[j], r_s1[j]], accum_out=s1[:, j:j + 1])
        for fc in range(4):
            b = fc % 2
            K.mm(K.banks[b][:, :], [(Win[:, k, fc * 128:(fc + 1) * 128], hT[:, k, :]) for k in range(8)],
                 [r_Win, r_hT], [K.bank_res[b]])
            K.act(uT[:, fc, :], K.banks[b][:, :], AF.Gelu, [K.bank_res[b]], [r_uT[fc]])
        for j in range(4):
            K.act(junk, v[j], AF.Square, [r_v[j]], [r_junk, r_s2[j]], accum_out=s2[:, j:j + 1])
        K.op("vector", lambda e: e.tensor_scalar(mean, s1, 1.0 / 512, None, ALU.mult), r_s1, [r_lst[2]])
        K.op("vector", lambda e: e.tensor_tensor(msq, mean, mean, ALU.mult), [r_lst[2]], [r_lst[3]])
        K.op("vector", lambda e: e.scalar_tensor_tensor(out=var, in0=s2, scalar=1.0 / 512, in1=msq,
                                                         op0=ALU.mult, op1=ALU.subtract),
             r_s2 + [r_lst[3]], [r_lst[4]])
        K.act(rstd, var, AF.Sqrt, [r_lst[4], r_eps], [r_lst[5]], bias=epsln, scale=1.0)
        K.op("vector", lambda e: e.reciprocal(rstd, rstd), [r_lst[5]], [r_lst[5]])
        K.op("vector", lambda e: e.scalar_tensor_tensor(out=nmr, in0=mean, scalar=-1.0, in1=rstd,
                                                         op0=ALU.mult, op1=ALU.mult),
             [r_lst[2], r_lst[5]], [r_lst[6]])
        for j in range(4):
            if j % 2 == 0:
                K.act(v[j], v[j], AF.Identity, [r_lst[5], r_lst[6], r_v[j]], [r_v[j]],
                      bias=nmr[:, j:j + 1], scale=rstd[:, j:j + 1])
            else:
                K.op("vector", lambda e, o=v[j], a=rstd[:, j:j + 1], b_=nmr[:, j:j + 1]:
                     e.tensor_scalar(o, o, a, b_, ALU.mult, ALU.add), [r_lst[5], r_lst[6], r_v[j]], [r_v[j]])
        for j in range(4):
            K.op("vector", lambda e, o=v[j]: e.tensor_tensor(o, o, lng, ALU.mult), [r_v[j], r_lng], [r_v[j]])
            K.op("vector", lambda e, o=vn[j], a=v[j]: e.tensor_tensor(o, a, lnb, ALU.add), [r_v[j], r_lnb], [r_vn[j]])
        for j in range(4):
            t = g * 4 + j
            b = j % 2
            outs = [(K.banks[b][:, gc * 128:(gc + 1) * 128], b, gc, gc) for gc in range(4)]
            bandmix(K, t, outs, zt[j], r_zt[j], ht[j], r_ht[j], et, r_et, bM, bH, bE, r_tab)
            K.act(dT[j], K.banks[b][:, :], AF.Copy, [K.bank_res[b]], [r_dT[j]])
        if g + 1 < NG:
            pre_norm_B(K, g + 1, ls, srcname, bufs, halves=(0,))
        if g > 0:
            wout_post(K, g - 1, ls, srcname, dstname, bufs, yT2[(g - 1) % 2], r_yT2[(g - 1) % 2], Wout, r_Wout)
        if g + 1 < NG:
            pre_norm_B(K, g + 1, ls, srcname, bufs, halves=(1,))
        for j in range(4):
            tc = slice(j * 128, (j + 1) * 128)
            b = j % 2
            for h in range(4):
                K.mm(K.banks[b][:, h * 128:(h + 1) * 128], [(vn[j][:, h * 128:(h + 1) * 128], WsT[:, h, :])],
                     [r_vn[j], r_WsT], [K.bank_res[b]])
            K.op("vector", lambda e, o=tya[j % 2], a=K.banks[b][:, :]: e.tensor_tensor(o, a, bsb, ALU.add),
                 [K.bank_res[b], r_bsb], [r_tya[j % 2]])
            K.op("vector", lambda e, o=yT[:, 0:4, tc], a=tya[j % 2].rearrange("p (h q) -> p h q", h=4),
                 b_=uT[:, :, tc]: e.tensor_tensor(o, a, b_, ALU.mult), [r_tya[j % 2]] + r_uT, [r_yT])
        for j in range(4):
            tc = slice(j * 128, (j + 1) * 128)
            b = j % 2
            for gc in range(4):
                K.mm(K.banks[b][:, gc * 128:(gc + 1) * 128], [(Wp[:, gc, :], dT[j][:, gc * 128:(gc + 1) * 128])],
                     [r_Wp, r_dT[j]], [K.bank_res[b]])
            K.op("vector", lambda e, o=yT[:, 4:8, tc], a=K.banks[b][:, :].rearrange("p (h q) -> p h q", h=4),
                 b_=psc.unsqueeze(2).to_broadcast([128, 4, 128]): e.tensor_tensor(o, a, b_, ALU.mult),
                 [K.bank_res[b], r_psc], [r_yT])
    wout_post(K, NG - 1, ls, srcname, dstname, bufs, yT2[(NG - 1) % 2], r_yT2[(NG - 1) % 2], Wout, r_Wout)


def odd_pass2(K, l, sub, srcname, dstname):
    T = K.T
    ls = l * 3 + sub
    j_ = l // 2
    m0 = K.mark()
    bufs = common_bufs(K, n_hT=2)
    junk, r_junk = bufs["junk"], bufs["r_junk"]
    Wc = K.alloc([8, 512], BF16)
    r_Wc = Res("Wc")
    load_weight_cast(K, Wc, T["od_w_in"][j_, :, 0:512].rearrange("(k p) f -> p k f", p=128), r_Wc)
    Wxd = K.alloc([8, 1024], BF16)
    r_Wxd = Res("Wxd")
    load_weight_cast(K, Wxd, T["od_w_in"][j_, :, 1024:2048].rearrange("(k p) f -> p k f", p=128), r_Wxd)
    fgb, r_fgb = load_const(K, [512], F32, T["fg_bc"][:, :], "fgb")
    CD, r_CD = load_const(K, [256], BF16, T["CD"][:, :], "CD")
    zero_pads(K, "ZC")
    csb = [K.alloc([512], F32) for _ in range(2)]
    r_csb = [Res("csb0"), Res("csb1")]
    z = [K.alloc([512], BF16) for _ in range(4)]
    r_z = [Res("z%d" % i) for i in range(4)]
    zg = [K.alloc([512], BF16) for _ in range(2)]
    r_zg = [Res("zg0"), Res("zg1")]
    ztmp = [K.alloc([512], F32) for _ in range(2)]
    r_ztmp = [Res("ztmp0"), Res("ztmp1")]
    zgT = [K.alloc([512], BF16) for _ in range(2)]
    r_zgT = [Res("zgT0"), Res("zgT1")]
    ab = [K.alloc([1024], BF16) for _ in range(4)]
    r_ab = [Res("ab%d" % i) for i in range(4)]
    sd = K.alloc([2, 16], F32)
    r_ss = [Res("sdss%d" % i) for i in range(16)]
    abp_done = Res("abp_done")
    r_rd = Res("sdr")
    pre_norm_group(K, 0, ls, srcname, bufs)
    pre_norm_A(K, 1, ls, srcname, bufs)
    for g in range(NG):
        hT, r_hT = bufs["hT2"][g % 2], bufs["r_hT2"][g % 2]
        if g + 2 < NG:
            pre_norm_A(K, g + 2, ls, srcname, bufs)
        for j in range(4):
            t = g * 4 + j
            tc = slice(j * 128, (j + 1) * 128)
            b0, b1 = (0, 1) if j % 2 == 0 else (2, 3)
            K.mm(K.banks[b0][:, :], [(hT[:, k, tc], Wc[:, k, :]) for k in range(8)], [r_hT, r_Wc], [K.bank_res[b0]])
            K.mm(K.banks[b1][:, :], [(hT[:, k, tc], Wxd[:, k, 0:512]) for k in range(8)], [r_hT, r_Wxd],
                 [K.bank_res[b1]])
            K.act(csb[j % 2], K.banks[b0][:, :], AF.Copy, [K.bank_res[b0]], [r_csb[j % 2]])
            K.op("vector", lambda e, o=z[j], a=K.banks[b1][:, :], c_=csb[j % 2]: e.tensor_tensor(o, a, c_, ALU.mult),
                 [K.bank_res[b1], r_csb[j % 2]], [r_z[j]])
            store_rows_with_edges(K, z[j], r_z[j], t, "ZC", "EDGE2")
        for jp in (0, 2):
            if g + 1 < NG:
                pre_norm_B(K, g + 1, ls, srcname, bufs, halves=(jp // 2,))
            for j in (jp, jp + 1):
                q2 = j % 2
                tc = slice(j * 128, (j + 1) * 128)
                K.mm(K.banks[q2][:, :], [(hT[:, k, tc], Wxd[:, k, 512:1024]) for k in range(8)], [r_hT, r_Wxd],
                     [K.bank_res[q2]])
                for gq in range(4):
                    K.act(junk[:, 0:128], K.banks[q2][:, gq * 128:(gq + 1) * 128], AF.Square, [K.bank_res[q2]],
                          [r_junk, r_ss[q2 * 4 + gq]], accum_out=sd[:, 0, q2 * 4 + gq:q2 * 4 + gq + 1])
            K.act(sd[:, 1, 0:8], sd[:, 0, 0:8], AF.Sqrt, r_ss[0:8] + [K.r_mod], [r_rd], bias=K.eps_rms, scale=1.0 / 128)
            K.op("vector", lambda e: e.reciprocal(sd[:, 1, 0:8], sd[:, 1, 0:8]), [r_rd], [r_rd])
            for j in (jp, jp + 1):
                t = g * 4 + j
                q2 = j % 2
                rb = sd[:, 1, q2 * 4:(q2 + 1) * 4].unsqueeze(2).to_broadcast([128, 4, 128])
                K.op("vector", lambda e, o=ztmp[q2].rearrange("p (a b) -> p a b", a=4),
                     a=K.banks[q2][:, :].rearrange("p (a b) -> p a b", a=4), rb=rb: e.tensor_tensor(o, a, rb, ALU.mult),
                     [K.bank_res[q2], r_rd], [r_ztmp[q2]])
                K.op("vector", lambda e, o=zg[q2], a=ztmp[q2]: e.tensor_tensor(o, a, fgb, ALU.mult),
                     [r_ztmp[q2], r_fgb], [r_zg[q2]])
                tb = 4 + q2
                pt = K.banks[tb].bitcast(BF16)
                K.op("tensor", lambda e, pt=pt, zz=zg[q2]: [e.transpose(pt[:, q * 128:(q + 1) * 128],
                                                                      zz[:, q * 128:(q + 1) * 128], K.identb)
                                                          for q in range(4)][-1],
                     [r_zg[q2], K.r_identb], [K.bank_res[tb]])
                K.act(zgT[q2], pt[:, 0:512], AF.Copy, [K.bank_res[tb]], [r_zgT[q2]])
                ba, bb = 2, 3
                for gq in range(4):
                    b = ba if gq < 2 else bb
                    K.mm(K.banks[b][:, (gq % 2) * 256:(gq % 2 + 1) * 256],
                         [(zgT[q2][:, gq * 128:(gq + 1) * 128], CD)], [r_zgT[q2], r_CD], [K.bank_res[b]])
                K.act(ab[j][:, 0:512], K.banks[ba][:, :], AF.Copy, [K.bank_res[ba]], [r_ab[j]])
                K.op("vector", lambda e, o=ab[j][:, 512:1024], a=K.banks[bb][:, :]: e.tensor_copy(o, a),
                     [K.bank_res[bb]], [r_ab[j]])
                for u in range(8):
                    if t < 32:
                        K.dma(T["ABP%d" % u][t * 128:(t + 1) * 128, :], ab[j][:, u * 128:(u + 1) * 128],
                              [r_ab[j], abp_done], [K.dr("ABP", u)], r_ab[j])
                    else:
                        K.dma(T["ABS%d" % u][(t - 32) * 128:(t - 31) * 128, :], ab[j][:, u * 128:(u + 1) * 128],
                              [r_ab[j]], [K.dr("ABS", u)], r_ab[j])
        if g >= 7 and g < 15:
            u = g - 7
            allgather(K, T["ABP%d" % u], T["ABG%d" % u], [K.dr("ABP", u)],
                      [K.dr("ABG", u)] + ([abp_done] if u == 0 else []))
            if g == 7:
                allgather(K, T["EDGE2"], T["EDGE2G"], [K.dr("EDGE2", 0), K.dr("EDGE2", 1)], [K.dr("EDGE2G", 0)])
    K.barrier()
    K.release(m0)
    fourier_stage2(K)
    K.barrier()
    K.release(m0)
    bufs = common_bufs(K, n_hT=2)
    Wb = K.alloc([8, 512], BF16)
    r_Wb = Res("Wb")
    load_weight_cast(K, Wb, T["od_w_in"][j_, :, 512:1024].rearrange("(k p) f -> p k f", p=128), r_Wb)
    Wout = K.alloc([8, D], BF16)
    r_Wout = Res("Wout")
    load_weight_cast(K, Wout, T["od_w_out"][j_, :, :].rearrange("(k p) f -> p k f", p=128), r_Wout)
    fw, r_fw = load_const(K, [4, 128], BF16, T["fwT"][:, :, :], "fw", q="gpsimd")
    cw, r_cw = load_const(K, [4, 3], F32, T["convT"][:, :, :], "cw")
    r_tab = Res("convtabs")
    bM = [K.alloc([384], BF16) for _ in range(5)]
    bH = [K.alloc([384], BF16) for _ in range(3)]
    bE = [K.alloc([384], BF16) for _ in range(2)]
    for i in range(5):
        K.dma(bM[i], T["cbM"][i, :, :], [], [r_tab], r_tab)
    for i in range(3):
        K.dma(bH[i][0:16, :], T["cbH"][i, :, :], [], [r_tab], r_tab)
    for i in range(2):
        K.dma(bE[i][0:64, :], T["cbE"][i, :, :], [], [r_tab], r_tab)
    load_gbc(K, ls, bufs)
    bT = K.alloc([4, 512], F32)
    r_bT = [Res("bT%d" % i) for i in range(4)]
    yT2 = [K.alloc([8, 512], BF16) for _ in range(2)]
    r_yT2 = [Res("yTa"), Res("yTb")]
    ydin = K.alloc([4, 512], BF16)
    r_ydin = Res("ydin")
    zt = [K.alloc([512], BF16) for _ in range(4)]
    r_zt = [Res("zt%d" % i) for i in range(4)]
    ht = [K.alloc([512], BF16) for _ in range(4)]
    r_ht = [Res("ht%d" % i) for i in range(4)]
    et = K.alloc([512], BF16)
    r_et = Res("et")
    t1 = [K.alloc([512], F32) for _ in range(2)]
    r_t1 = [Res("t1a"), Res("t1b")]
    t2 = [K.alloc([512], F32) for _ in range(2)]
    r_t2 = [Res("t2a"), Res("t2b")]
    pre_norm_group(K, 0, ls, srcname, bufs)
    pre_norm_A(K, 1, ls, srcname, bufs)
    for g in range(NG):
        hT, r_hT = bufs["hT2"][g % 2], bufs["r_hT2"][g % 2]
        yT, r_yT = yT2[g % 2], r_yT2[g % 2]
        if g + 2 < NG:
            pre_norm_A(K, g + 2, ls, srcname, bufs)
        K.dma(ydin, T["YDT"][:, g * 512:(g + 1) * 512].rearrange("(q m) t -> m q t", q=4), [K.dr("YDT", 0)],
              [r_ydin], r_ydin)
        for j in range(4):
            load_tile_with_halo(K, g * 4 + j, "ZC", "EDGE2G", zt[j], r_zt[j], ht[j], r_ht[j], et, r_et)
        for fc in range(4):
            b = fc % 2
            K.mm(K.banks[b][:, :], [(Wb[:, k, fc * 128:(fc + 1) * 128], hT[:, k, :]) for k in range(8)],
                 [r_Wb, r_hT], [K.bank_res[b]])
            K.act(bT[:, fc, :], K.banks[b][:, :], AF.Copy, [K.bank_res[b]], [r_bT[fc]])
        for gq in range(4):
            b = 2 + gq % 2
            K.mm(K.banks[b][:, :], [(fw[:, gq, :], ydin[:, gq, :])], [r_fw, r_ydin], [K.bank_res[b]])
            if gq % 2 == 0:
                K.act(yT[:, 4 + gq, :], K.banks[b][:, :], AF.Copy, [K.bank_res[b]], [r_yT])
            else:
                K.op("vector", lambda e, o=yT[:, 4 + gq, :], a=K.banks[b][:, :]: e.tensor_copy(o, a),
                     [K.bank_res[b]], [r_yT])
        if g + 1 < NG:
            pre_norm_B(K, g + 1, ls, srcname, bufs, halves=(0,))
        if g > 0:
            wout_post(K, g - 1, ls, srcname, dstname, bufs, yT2[(g - 1) % 2], r_yT2[(g - 1) % 2], Wout, r_Wout)
        if g + 1 < NG:
            pre_norm_B(K, g + 1, ls, srcname, bufs, halves=(1,))
        for j in range(4):
            t = g * 4 + j
            tc = slice(j * 128, (j + 1) * 128)
            tb = (0, 1, 2) if j % 2 == 0 else (3, 4, 5)
            for tap in range(3):
                outs = [(K.banks[tb[tap]][:, cc * 128:(cc + 1) * 128], tb[tap], cc, tap) for cc in range(4)]
                bandmix(K, t, outs, zt[j], r_zt[j], ht[j], r_ht[j], et, r_et, bM, bH, bE, r_tab)
            q2 = j % 2
            v3 = lambda a: a.rearrange("p (a b) -> p a b", a=4)
            wbc = lambda tap: cw[:, :, tap].unsqueeze(2).to_broadcast([128, 4, 128])
            K.op("vector", lambda e, o=v3(t1[q2]), a=v3(K.banks[tb[0]][:, :]), w=wbc(0): e.tensor_tensor(o, a, w, ALU.mult),
                 [K.bank_res[tb[0]], r_cw], [r_t1[q2]])
            K.op("vector", lambda e, o=v3(t2[q2]), a=v3(K.banks[tb[1]][:, :]), w=wbc(1): e.tensor_tensor(o, a, w, ALU.mult),
                 [K.bank_res[tb[1]], r_cw], [r_t2[q2]])
            K.op("vector", lambda e, o=t1[q2], a=t2[q2]: e.tensor_tensor(o, o, a, ALU.add), [r_t2[q2], r_t1[q2]], [r_t1[q2]])
            K.op("vector", lambda e, o=v3(t2[q2]), a=v3(K.banks[tb[2]][:, :]), w=wbc(2): e.tensor_tensor(o, a, w, ALU.mult),
                 [K.bank_res[tb[2]], r_cw, r_t1[q2]], [r_t2[q2]])
            K.op("vector", lambda e, o=t1[q2], a=t2[q2]: e.tensor_tensor(o, o, a, ALU.add), [r_t2[q2], r_t1[q2]], [r_t1[q2]])
            K.op("vector", lambda e, o=yT[:, 0:4, tc], a=v3(t1[q2]), b_=bT[:, :, tc]: e.tensor_tensor(o, a, b_, ALU.mult),
                 [r_t1[q2]] + r_bT, [r_yT])
    wout_post(K, NG - 1, ls, srcname, dstname, bufs, yT2[(NG - 1) % 2], r_yT2[(NG - 1) % 2], Wout, r_Wout)


def fourier_load(K, srcs, X, r_X, dres_in):
    for (p0, p1, sap) in srcs:
        K.dma(X[p0:p1], sap, dres_in, [r_X], r_X)


def fourier_step1(K, NAp, NC, Ft, r_Ft, X, r_X, TTflat, r_TT, cnt):
    MB = 512 // NC
    for bi, m0 in enumerate(range(0, 64, MB)):
        b = bi % 4
        for mi in range(MB):
            m = m0 + mi
            out = K.banks[b][:, mi * NC:(mi + 1) * NC]
            K.mm(out, [(X[0:NAp, :, m], Ft[0:NAp, 0, :]), (X[0:NAp, :, 64 + m], Ft[0:NAp, 1, :])],
                 [r_X, r_Ft], [K.bank_res[b]])
        src = K.banks[b][:, 0:MB * NC]
        dst = TTflat[:, m0:m0 + MB, :].rearrange("p m c -> p (m c)")
        cnt[0] += 1
        if cnt[0] % 2 == 0:
            K.act(dst, src, AF.Copy, [K.bank_res[b]], [r_TT])
        else:
            K.op("vector", lambda e, o=dst, a=src: e.tensor_copy(o, a), [K.bank_res[b]], [r_TT])


def fourier_step3(K, NA, NK2, TT4, r_TT, GG, r_GG, ydg, r_ydg, dst_ap, dres_out, cnt):
    K1B = 512 // NK2
    yv = ydg[0:64, 0:NK2 * NA].rearrange("p (k2 k1) -> p k2 k1", k2=NK2, k1=NA)
    for kb in range(NA // K1B):
        b = 4 + kb % 2
        for kl in range(K1B):
            k1 = kb * K1B + kl
            K.mm(K.banks[b][0:64, kl * NK2:(kl + 1) * NK2],
                 [(TT4[:, :, 0, k1], GG[:, k1, 0, :]), (TT4[:, :, 1, k1], GG[:, k1, 1, :])],
                 [r_TT, r_GG], [K.bank_res[b]])
        src = K.banks[b][0:64, :].rearrange("p (kl k2) -> p k2 kl", kl=K1B, k2=NK2)
        dst = yv[:, :, kb * K1B:(kb + 1) * K1B]
        cnt[0] += 1
        if cnt[0] % 2 == 0:
            K.act(dst, src, AF.Copy, [K.bank_res[b]], [r_ydg])
        else:
            K.op("vector", lambda e, o=dst, a=src: e.tensor_copy(o, a), [K.bank_res[b]], [r_ydg])
    K.dma(dst_ap, ydg[0:64, 0:NK2 * NA], [r_ydg], dres_out, r_ydg, q="gpsimd")


def fourier_stage2(K):
    T = K.T
    F128, r_F128 = load_const(K, [2, 256], BF16, T["F128"][:, :].rearrange("p (a b) -> p a b", a=2), "F128")
    Fbd, r_Fbd = load_const(K, [2, 256], BF16, T["Fbd"][:, :].rearrange("p (a b) -> p a b", a=2), "Fbd")
    GGp, r_GGp = load_const(K, [128, 2, 32], BF16,
                            T["GGp"][:, :].rearrange("p (k a b) -> p k a b", k=128, a=2), "GGp")
    GGs, r_GGs = load_const(K, [16, 2, 128], BF16,
                            T["GGs"][:, :].rearrange("p (k a b) -> p k a b", k=16, a=2), "GGs")
    X = [K.alloc([128, 128], BF16) for _ in range(2)]
    r_X = [Res("X0"), Res("X1")]
    TT = [K.alloc([64, 256], BF16) for _ in range(2)]
    r_TT = [Res("TT0"), Res("TT1")]
    ydg = [K.alloc([4096], BF16) for _ in range(2)]
    r_ydg = [Res("ydg0"), Res("ydg1")]
    cnt = [0]
    units = []
    for q in range(2):
        srcs = [(u * 16, (u + 1) * 16, T["ABS%d" % u][q * 2048:(q + 1) * 2048, :].rearrange("(a b) c -> a b c", b=128))
                for u in range(8)]
        units.append(dict(kind="s", q=q, srcs=srcs, din=[K.dr("ABS", u) for u in range(8)]))
    for u in range(8):
        units.append(dict(kind="p", u=u, srcs=[(0, 128, T["ABG%d" % u][:, :].rearrange("(a b) c -> a b c", b=128))],
                          din=[K.dr("ABG", u)]))
    fourier_load(K, units[0]["srcs"], X[0], r_X[0], units[0]["din"])
    ny = 0
    for n, U in enumerate(units):
        if n + 1 < len(units):
            V = units[n + 1]
            fourier_load(K, V["srcs"], X[(n + 1) % 2], r_X[(n + 1) % 2], V["din"])
        tt, rtt = TT[n % 2], r_TT[n % 2]
        if U["kind"] == "s":
            q = U["q"]
            fourier_step1(K, 128, 256, Fbd, r_Fbd, X[n % 2], r_X[n % 2], tt, rtt, cnt)
            tt5 = tt.rearrange("p m (u part k) -> p m u part k", u=8, part=2, k=16)
            for u in range(8):
                fourier_step3(K, 16, 128, tt5[:, :, u, :, :], rtt, GGs, r_GGs, ydg[ny % 2], r_ydg[ny % 2],
                              T["YDT"][u * 64:(u + 1) * 64, 4096 + q * 2048:4096 + (q + 1) * 2048],
                              [K.dr("YDT", 0)], cnt)
                ny += 1
        else:
            u = U["u"]
            fourier_step1(K, 128, 256, F128, r_F128, X[n % 2], r_X[n % 2], tt, rtt, cnt)
            tt4 = tt.rearrange("p m (part k) -> p m part k", part=2, k=128)
            fourier_step3(K, 128, 32, tt4, rtt, GGp, r_GGp, ydg[ny % 2], r_ydg[ny % 2],
                          T["YDT"][u * 64:(u + 1) * 64, 0:4096], [K.dr("YDT", 0)], cnt)
            ny += 1
```

```python
import contextlib
import numpy as np
import ml_dtypes
import concourse.bass as bass
import concourse.mybir as mybir
from concourse.bass_utils import run_bass_kernel_spmd

F32 = mybir.dt.float32
BF16 = mybir.dt.bfloat16
AF = mybir.ActivationFunctionType
ALU = mybir.AluOpType

D = 1024
DFF = 2816
NT = 64
NG = 16
ROWS = 8192
RMS_EPS = 1e-6
LN_EPS = 1e-5
SEM_LIMIT = 30000
ENGS = ("sync", "gpsimd", "scalar", "vector", "tensor")

NPASS_DEBUG = None


def seq_of_group(g):
    return 0 if g < 8 else (1 if g < 12 else 2)


class Res:
    __slots__ = ("name", "w", "r", "streams")

    def __init__(self, name):
        self.name = name
        self.w = None
        self.r = {}
        self.streams = {}


class Stream:
    def __init__(self, K, step, kind="eng"):
        self.K = K
        self.step = step
        self.kind = kind
        self.si = None
        self.cnt = 0

    def next(self):
        if self.si is None or self.cnt + self.step > SEM_LIMIT:
            self.si, self.cnt = self.K.acquire_sem(self.kind)
        self.cnt += self.step
        return (self.si, self.cnt)

    def last(self):
        if self.si is None:
            return None
        return (self.si, self.cnt)


class Kern:
    def __init__(self, nc, stack, arena_words):
        self.nc = nc
        self.stack = stack
        self.sems = []
        self.ops = {e: [] for e in ENGS}
        self.seen = {e: {} for e in ENGS}
        self.estream = {e: Stream(self, 1) for e in ENGS}
        self.dma_streams = []
        self.dma_res = []
        self.pools = {}
        self.arena = stack.enter_context(nc.sbuf_tensor("arena", [128, arena_words], F32))
        self.arena_words = arena_words
        self.off = 0
        self.banks = [stack.enter_context(nc.psum_tensor("ps%d" % i, [128, 512], F32)) for i in range(8)]
        self.bank_res = [Res("bank%d" % i) for i in range(8)]

    def alloc(self, shape, dtype):
        n = int(np.prod(shape))
        esz = 4 if dtype == F32 else 2
        words = (n * esz + 3) // 4
        assert self.off + words <= self.arena_words, ("SBUF arena overflow", self.off, words)
        ap = self.arena[:, self.off:self.off + words]
        self.off += words
        if dtype != F32:
            ap = ap.bitcast(dtype)[:, :n]
        if len(shape) == 2:
            ap = ap.rearrange("p (a b) -> p a b", a=shape[0], b=shape[1])
        elif len(shape) == 3:
            ap = ap.rearrange("p (a b c) -> p a b c", a=shape[0], b=shape[1], c=shape[2])
        return ap

    def mark(self):
        return self.off

    def release(self, m):
        self.off = m

    def acquire_sem(self, kind="eng"):
        pool = self.pools.setdefault(kind, [])
        while pool:
            si, c = pool.pop()
            if c + 2048 <= SEM_LIMIT:
                return si, c
        s = self.stack.enter_context(self.nc.semaphore("s%d" % len(self.sems)))
        self.sems.append(s)
        return len(self.sems) - 1, 0

    def op(self, eng, fn, R=(), W=(), dma=None, step=16):
        waits = {}

        def need(ev):
            if ev is None:
                return
            si, v = ev
            if waits.get(si, 0) < v:
                waits[si] = v

        for r in R:
            need(r.w)
        for w in W:
            need(w.w)
            for ev in w.r.values():
                need(ev)
        own = self.estream[eng].si
        seen = self.seen[eng]
        wl = []
        for si, v in waits.items():
            if eng == "tensor" and si == own:
                continue
            if seen.get(si, 0) < v:
                seen[si] = v
                wl.append((si, v))
        if dma is not None:
            kind = "cc" if step == 1 else ("sw" if eng == "gpsimd" else "hw")
            st = dma.streams.get(kind)
            if st is None:
                st = Stream(self, step, kind)
                dma.streams[kind] = st
                self.dma_streams.append(st)
                self.dma_res.append(dma)
            ev = st.next()
            inc = st.step
        else:
            ev = self.estream[eng].next()
            inc = 1
        self.ops[eng].append((wl, fn, ev, inc))
        for r in R:
            r.r[ev[0]] = ev
        for w in W:
            w.w = ev
            w.r = {}
        return ev

    def barrier(self):
        evs = []
        for e in ENGS:
            ev = self.estream[e].last()
            if ev is not None:
                evs.append(ev)
        for s in self.dma_streams:
            ev = s.last()
            if ev is not None:
                evs.append(ev)
        for e in ENGS:
            seen = self.seen[e]
            wl = []
            for si, v in evs:
                if seen.get(si, 0) < v:
                    seen[si] = v
                    wl.append((si, v))
            self.ops[e].append((wl, None, None, 0))
        for st in self.dma_streams:
            if st.si is not None:
                self.pools.setdefault(st.kind, []).append((st.si, st.cnt))
        for r in self.dma_res:
            r.streams = {}
        self.dma_streams = []
        self.dma_res = []

    def emit(self):
        nc = self.nc
        K = self

        def run(name, e):
            for wl, fn, ev, inc in K.ops[name]:
                for si, v in wl:
                    e.wait_ge(K.sems[si], v)
                if fn is not None:
                    ins = fn(e)
                    ins.then_inc(K.sems[ev[0]], inc)

        with nc.Block() as block:
            @block.sync
            def _(e):
                run("sync", e)

            @block.gpsimd
            def _(e):
                run("gpsimd", e)

            @block.scalar
            def _(e):
                run("scalar", e)

            @block.vector
            def _(e):
                run("vector", e)

            @block.tensor
            def _(e):
                run("tensor", e)

    def dma(self, out, in_, R, W, res, q="sync", **kw):
        return self.op(q, lambda e: e.dma_start(out=out, in_=in_, **kw), R, W, dma=res)

    def mm(self, out, pairs, R, W, transpose_ident=None):
        n = len(pairs)

        def fn(e):
            ins = None
            for i, (a, b) in enumerate(pairs):
                ins = e.matmul(out, a, b, start=(i == 0), stop=(i == n - 1))
            return ins
        return self.op("tensor", fn, R, W)

    def act(self, out, in_, func, R, W, bias=None, scale=None, accum_out=None):
        kw = {}
        if bias is not None:
            kw["bias"] = bias
        if scale is not None:
            kw["scale"] = scale
        if accum_out is not None:
            kw["accum_out"] = accum_out
        return self.op("scalar", lambda e: e.activation(out, in_, func, **kw), R, W)


def build_program():
    nc = bass.Bass("TRN2", target_bir_lowering=False)
    stack = contextlib.ExitStack()
    with stack:
        T = {}

        in_names = []

        def din(name, shape, dt=F32):
            T[name] = nc.dram_tensor(name, list(shape), dt, kind="ExternalInput")
            in_names.append(name)

        din("x", [ROWS, D])
        din("cT", [128, 8, 3])
        din("ada_w", [2, D, 9 * D])
        din("ada_bT", [128, 2, 72])
        din("gpreT", [128, 6, 8])
        din("gpostT", [128, 6, 8])
        din("ffn1_w_up", [2, D, 2 * DFF])
        din("ffn1_w_down", [2, DFF, D])
        din("ffn2_w_up", [2, D, 2 * DFF])
        din("ffn2_w_down", [2, DFF, D])
        din("identb", [128, 128], BF16)
        din("identf", [128, 128])
        din("onesf", [128, 128])
        T["y"] = nc.dram_tensor("y", [ROWS, D], F32, kind="ExternalOutput")
        T["S0"] = nc.dram_tensor("S0", [ROWS, D], F32)
        T["S1"] = nc.dram_tensor("S1", [ROWS, D], F32)
        T["GB"] = nc.dram_tensor("GB", [18, 128, D], F32)
        din("ev_w_in", [1, D, 1536])
        din("ev_w_out", [1, D, D])
        din("wsT", [128, 4, 128])
        din("wpoolT", [128, 4, 128])
        din("bs_bc", [128, 512])
        din("lng_bc", [128, 512])
        din("lnb_bc", [128, 512])
        din("pscT", [128, 4])
        din("pbM", [5, 128, 512], BF16)
        din("pbH", [3, 16, 512], BF16)
        din("pbE", [2, 64, 512], BF16)
        T["ZB"] = nc.dram_tensor("ZB", [ROWS + 16, 512], BF16)
        T["EDGE"] = nc.dram_tensor("EDGE", [16, 512], BF16)
        T["EDGEG"] = nc.dram_tensor("EDGEG", [64, 512], BF16)
        din("od_w_in", [1, D, 2048])
        din("od_w_out", [1, D, D])
        din("fwT", [128, 4, 128])
        din("convT", [128, 4, 3])
        din("fg_bc", [128, 512])
        din("cbM", [5, 128, 384], BF16)
        din("cbH", [3, 16, 384], BF16)
        din("cbE", [2, 64, 384], BF16)
        din("CD", [128, 256], BF16)
        din("F128", [128, 512], BF16)
        din("F16", [16, 64], BF16)
        din("Fbd", [128, 512], BF16)
        din("GGp", [128, 128 * 2 * 32], BF16)
        din("GGs", [128, 16 * 2 * 128], BF16)
        T["ZC"] = nc.dram_tensor("ZC", [ROWS + 16, 512], BF16)
        T["EDGE2"] = nc.dram_tensor("EDGE2", [16, 512], BF16)
        T["EDGE2G"] = nc.dram_tensor("EDGE2G", [64, 512], BF16)
        for u in range(8):
            T["ABP%d" % u] = nc.dram_tensor("ABP%d" % u, [4096, 128], BF16)
            T["ABG%d" % u] = nc.dram_tensor("ABG%d" % u, [16384, 128], BF16)
            T["ABS%d" % u] = nc.dram_tensor("ABS%d" % u, [4096, 128], BF16)
        T["YDT"] = nc.dram_tensor("YDT", [512, ROWS], BF16)

        K = Kern(nc, stack, 53100)
        K.T = T
        K.dres = {}

        def dr(name, idx):
            key = (name, idx)
            if key not in K.dres:
                K.dres[key] = Res("%s_%s" % key)
            return K.dres[key]
        K.dr = dr

        prologue(K)
        K.barrier()
        passes = [(0, 0, "ffn1"), (0, 1, "even"), (0, 2, "ffn2"), (1, 0, "ffn1"), (1, 1, "odd"), (1, 2, "ffn2")]
        if NPASS_DEBUG is not None:
            passes = passes[:NPASS_DEBUG]
        src = "x"
        for pi, (l, sub, kind) in enumerate(passes):
            dst = "y" if pi == len(passes) - 1 else ("S0" if pi % 2 == 0 else "S1")
            m = K.mark()
            if kind in ("ffn1", "ffn2"):
                ffn_pass(K, l, sub, kind, src, dst)
            elif kind == "even":
                even_pass2(K, l, sub, src, dst)
            else:
                odd_pass2(K, l, sub, src, dst)
            K.barrier()
            K.release(m)
            src = dst
        K.emit()
    nc._in_names = in_names
    return nc


def prologue(K):
    nc, T = K.nc, K.T
    K.identb = K.alloc([128], BF16)
    r_identb = Res("identb")
    K.r_identb = r_identb
    K.dma(K.identb, T["identb"][:, :], [], [r_identb], r_identb)
    K.AT = K.alloc([6, 8, 3], F32)
    K.ST = K.alloc([6, 8, 3], F32)
    K.GT = K.alloc([6, 8, 3], F32)
    K.r_mod = Res("modvecs")
    K.stat = K.alloc([64], F32)
    K.stat_res = [Res("stat%d" % i) for i in range(64)]
    K.eps_rms = K.alloc([1], F32)
    m = K.mark()

    identf = K.alloc([128], F32)
    onesf = K.alloc([128], F32)
    r_identf, r_onesf = Res("identf"), Res("onesf")
    K.dma(identf, T["identf"][:, :], [], [r_identf], r_identf)
    K.dma(onesf, T["onesf"][:, :], [], [r_onesf], r_onesf)
    cT = K.alloc([8, 3], F32)
    adab = K.alloc([2, 72], F32)
    gpre = K.alloc([6, 8], F32)
    gpost = K.alloc([6, 8], F32)
    r_c, r_ab, r_gp, r_gq = Res("cT"), Res("adab"), Res("gpre"), Res("gpost")
    K.dma(cT, T["cT"][:, :, :], [], [r_c], r_c)
    K.dma(adab, T["ada_bT"][:, :, :], [], [r_ab], r_ab)
    K.dma(gpre, T["gpreT"][:, :, :], [], [r_gp], r_gp)
    K.dma(gpost, T["gpostT"][:, :, :], [], [r_gq], r_gq)
    scT = K.alloc([8, 3], BF16)
    r_sc = Res("scT")
    K.act(scT, cT, AF.Silu, [r_c], [r_sc])
    modT = K.alloc([2, 72, 3], F32)
    r_modT = Res("modT")
    aw = [K.alloc([8, 512], BF16) for _ in range(4)]
    r_aw = [Res("aw%d" % i) for i in range(4)]
    it = 0
    for l in range(2):
        bank = K.banks[l]
        rb = K.bank_res[l]
        for fb in range(18):
            slot = it % 4
            it += 1
            src = T["ada_w"][l, :, fb * 512:(fb + 1) * 512].rearrange("(k p) f -> p k f", p=128)
            K.dma(aw[slot], src, [], [r_aw[slot]], r_aw[slot], q="gpsimd")
            for ch in range(4):
                cidx = fb * 4 + ch
                pairs = [(aw[slot][:, k, ch * 128:(ch + 1) * 128], scT[:, k, :]) for k in range(8)]
                K.mm(bank[:, cidx * 3:(cidx + 1) * 3], pairs, [r_aw[slot], r_sc], [rb])
        ps = bank[:, 0:216].rearrange("p (a b) -> p a b", a=72, b=3)
        bia = adab[:, l, :].unsqueeze(2).to_broadcast([128, 72, 3])
        K.op("vector", lambda e, o=modT[:, l], a=ps, b=bia: e.tensor_tensor(o, a, b, ALU.add),
             [rb, r_ab], [r_modT])
    for l in range(2):
        for sub in range(3):
            ls = l * 3 + sub
            sh = modT[:, l, (sub * 3 + 0) * 8:(sub * 3 + 1) * 8, :]
            sc = modT[:, l, (sub * 3 + 1) * 8:(sub * 3 + 2) * 8, :]
            ga = modT[:, l, (sub * 3 + 2) * 8:(sub * 3 + 3) * 8, :]
            gp = gpre[:, ls, :].unsqueeze(2).to_broadcast([128, 8, 3])
            gq = gpost[:, ls, :].unsqueeze(2).to_broadcast([128, 8, 3])
            K.op("vector", lambda e, o=K.AT[:, ls], a=sc, b=gp: e.scalar_tensor_tensor(
                out=o, in0=a, scalar=1.0, in1=b, op0=ALU.add, op1=ALU.mult), [r_modT, r_gp], [K.r_mod])
            K.op("vector", lambda e, o=K.ST[:, ls], a=sh: e.tensor_copy(o, a), [r_modT], [K.r_mod])
            K.op("vector", lambda e, o=K.GT[:, ls], a=ga, b=gq: e.scalar_tensor_tensor(
                out=o, in0=a, scalar=1.0, in1=b, op0=ALU.add, op1=ALU.mult), [r_modT, r_gq], [K.r_mod])
            if sub != 1:
                K.op("vector", lambda e, o=K.GT[:, ls]: e.tensor_scalar(o, o, 0.5, None, ALU.mult),
                     [K.r_mod], [K.r_mod])
    K.op("vector", lambda e: e.memset(K.eps_rms, RMS_EPS), [], [K.r_mod])
    Dg = [K.alloc([8, 128], F32) for _ in range(2)]
    r_Dg = [Res("Dg0"), Res("Dg1")]
    gb = [K.alloc([D], F32) for _ in range(2)]
    r_gb = [Res("gb0"), Res("gb1")]
    it = 0
    for ls in range(6):
        for s in range(3):
            slot = it % 2
            it += 1
            for c in range(8):
                K.op("vector", lambda e, o=Dg[slot][:, c, :], sc1=K.GT[:, ls, c, s:s + 1]: e.tensor_scalar(
                    o, identf, sc1, None, ALU.mult), [r_identf, K.r_mod], [r_Dg[slot]])
            for h in range(2):
                b = 2 + h
                for cc in range(4):
                    c = h * 4 + cc
                    K.mm(K.banks[b][:, cc * 128:(cc + 1) * 128], [(onesf, Dg[slot][:, c, :])],
                         [r_onesf, r_Dg[slot]], [K.bank_res[b]])
                K.act(gb[slot][:, h * 512:(h + 1) * 512], K.banks[b][:, :], AF.Copy,
                      [K.bank_res[b]], [r_gb[slot]])
            K.dma(T["GB"][ls * 3 + s, :, :], gb[slot], [r_gb[slot]], [K.dr("GB", ls * 3 + s)], r_gb[slot])
    K.barrier()
    K.release(m)


def load_weight_cast(K, dst, src, res):
    K.dma(dst, src, [], [res], res, q="gpsimd")


def pre_norm_A(K, g, ls, srcname, bufs):
    T = K.T
    xin, r_xin, xn, r_xn = bufs["xin"], bufs["r_xin"], bufs["xn"], bufs["r_xn"]
    gi = g % 2
    ss4 = K.stat[:, gi * 8:gi * 8 + 4]
    rs4 = K.stat[:, gi * 8 + 4:gi * 8 + 8]
    r_ss = K.stat_res[gi * 8:gi * 8 + 4]
    r_rs = K.stat_res[gi * 8 + 4]
    for j in range(4):
        t = g * 4 + j
        xs, rxs = xin[t % len(xin)], r_xin[t % len(xin)]
        K.dma(xs, T[srcname][t * 128:(t + 1) * 128, :], [K.dr(srcname, g)], [rxs], rxs)
    for j in range(4):
        t = g * 4 + j
        xs, rxs = xin[t % len(xin)], r_xin[t % len(xin)]
        K.act(xn[t % len(xn)], xs, AF.Square, [rxs], [r_xn[t % len(xn)], r_ss[j]], accum_out=ss4[:, j:j + 1])
    K.act(rs4, ss4, AF.Sqrt, r_ss + [K.r_mod], [r_rs], bias=K.eps_rms, scale=1.0 / D)
    K.op("vector", lambda e, o=rs4: e.reciprocal(o, o), [r_rs], [r_rs])
    for j in range(4):
        t = g * 4 + j
        xs, rxs = xin[t % len(xin)], r_xin[t % len(xin)]
        if j % 2 == 0:
            K.op("vector", lambda e, o=xn[t % len(xn)], a=xs, b=rs4[:, j:j + 1]: e.tensor_scalar(o, a, b, None, ALU.mult),
                 [rxs, r_rs], [r_xn[t % len(xn)]])
        else:
            K.act(xn[t % len(xn)], xs, AF.Identity, [rxs, r_rs], [r_xn[t % len(xn)]], scale=rs4[:, j:j + 1])


def pre_norm_B(K, g, ls, srcname, bufs, halves=(0, 1)):
    s = seq_of_group(g)
    xn, r_xn, hT, r_hT = bufs["xn"], bufs["r_xn"], bufs["hT"], bufs["r_hT"]
    if "hT2" in bufs:
        hT, r_hT = bufs["hT2"][g % 2], bufs["r_hT2"][g % 2]
    pts = [K.banks[6].bitcast(BF16), K.banks[7].bitcast(BF16)]
    for half in halves:
        for j in range(4):
            t = g * 4 + j
            xns, rxns = xn[t % len(xn)], r_xn[t % len(xn)]

            def fn(e, xns=xns, j=j, half=half):
                ins = None
                for cl in range(4):
                    c = half * 4 + cl
                    off = (cl % 2) * 512 + j * 128
                    ins = e.transpose(pts[cl // 2][:, off:off + 128], xns[:, c * 128:(c + 1) * 128], K.identb)
                return ins
            K.op("tensor", fn, [rxns, K.r_identb], [K.bank_res[6], K.bank_res[7]])
        for cl in range(4):
            c = half * 4 + cl
            tb = 6 + cl // 2
            o = hT[:, c, :]
            i_ = pts[cl // 2][:, (cl % 2) * 512:(cl % 2 + 1) * 512]
            a_ = K.AT[:, ls, c, s:s + 1]
            b_ = K.ST[:, ls, c, s:s + 1]
            if c % 2 == 0:
                K.act(o, i_, AF.Identity, [K.bank_res[tb], K.r_mod], [r_hT], bias=b_, scale=a_)
            else:
                K.op("vector", lambda e, o=o, i_=i_, a_=a_, b_=b_: e.tensor_scalar(o, i_, a_, b_, ALU.mult, ALU.add),
                     [K.bank_res[tb], K.r_mod], [r_hT])


def pre_norm_group(K, g, ls, srcname, bufs):
    pre_norm_A(K, g, ls, srcname, bufs)
    pre_norm_B(K, g, ls, srcname, bufs)


def load_gbc_seq(K, ls, s, bufs):
    K.dma(bufs["Gbc"][s], K.T["GB"][ls * 3 + s, :, :], [K.dr("GB", ls * 3 + s)], [bufs["r_G"][s]], bufs["r_G"][s])


def load_gbc(K, ls, bufs):
    for s in range(3):
        K.dma(bufs["Gbc"][s], K.T["GB"][ls * 3 + s, :, :], [K.dr("GB", ls * 3 + s)], [bufs["r_G"][s]], bufs["r_G"][s])


def common_bufs(K, n_gbc=3, n_xin=4, n_hT=1):
    n_x = 8 if n_hT == 2 else 4
    b = {}
    b["xin"] = [K.alloc([D], F32) for _ in range(n_x)]
    b["r_xin"] = [Res("xin%d" % i) for i in range(n_x)]
    b["xres"] = [K.alloc([D], F32) for _ in range(2)]
    b["r_xres"] = [Res("xres%d" % i) for i in range(2)]
    b["xn"] = [K.alloc([D], BF16) for _ in range(n_x)]
    b["r_xn"] = [Res("xn%d" % i) for i in range(n_x)]
    b["hT"] = K.alloc([8, 512], BF16)
    b["r_hT"] = Res("hT")
    if n_hT == 2:
        b["hT2"] = [b["hT"], K.alloc([8, 512], BF16)]
        b["r_hT2"] = [b["r_hT"], Res("hTb")]
    if n_gbc == 3:
        b["Gbc"] = [K.alloc([D], F32) for _ in range(3)]
        b["r_G"] = [Res("G%d" % i) for i in range(3)]
    else:
        g1, r1 = K.alloc([D], F32), Res("G")
        b["Gbc"] = [g1, g1, g1]
        b["r_G"] = [r1, r1, r1]
    b["junk"] = K.alloc([512], BF16)
    b["r_junk"] = Res("junk")
    b["tmp"] = [K.alloc([512], F32) for _ in range(2)]
    b["r_tmp"] = [Res("tmp%d" % i) for i in range(2)]
    return b


def ffn_pass(K, l, sub, kind, srcname, dstname):
    T = K.T
    ls = l * 3 + sub
    Wup = K.alloc([8, 2 * DFF], BF16)
    Wdn = K.alloc([22, D], BF16)
    r_Wupb = [Res("Wup%d" % b) for b in range(11)]
    r_Wdn = Res("Wdn")
    wu = T[kind + "_w_up"]
    wd = T[kind + "_w_down"]
    for b in range(11):
        for off in (0, DFF):
            c0 = off + b * 256
            load_weight_cast(K, Wup[:, :, c0:c0 + 256],
                             wu[l, :, c0:c0 + 256].rearrange("(k p) f -> p k f", p=128), r_Wupb[b])
    for q in range(2):
        load_weight_cast(K, Wdn[:, q * 11:(q + 1) * 11, :],
                         wd[l, q * 11 * 128:(q + 1) * 11 * 128, :].rearrange("(f p) d -> p f d", p=128), r_Wdn)
    bufs = common_bufs(K, n_gbc=1)
    gT = K.alloc([22, 512], BF16)
    r_gT = [Res("gT%d" % i) for i in range(22)]
    sil, r_sil = bufs["tmp"], bufs["r_tmp"]
    hT, r_hT = bufs["hT"], bufs["r_hT"]

    def up(g):
        for fc in range(22):
            if fc == 6 and g + 1 < NG:
                pre_norm_A(K, g + 1, ls, srcname, bufs)
            bg, bu = (0, 1) if fc % 2 == 0 else (2, 3)
            pg = [(Wup[:, k, fc * 128:(fc + 1) * 128], hT[:, k, :]) for k in range(8)]
            pu = [(Wup[:, k, DFF + fc * 128:DFF + (fc + 1) * 128], hT[:, k, :]) for k in range(8)]
            K.mm(K.banks[bg][:, :], pg, [r_Wupb[fc // 2], r_hT], [K.bank_res[bg]])
            K.mm(K.banks[bu][:, :], pu, [r_Wupb[fc // 2], r_hT], [K.bank_res[bu]])
            sl, rsl = sil[fc % 2], r_sil[fc % 2]
            K.act(sl, K.banks[bg][:, :], AF.Silu, [K.bank_res[bg]], [rsl])
            K.op("vector", lambda e, o=gT[:, fc, :], a=K.banks[bu][:, :], b=sl: e.tensor_tensor(o, a, b, ALU.mult),
                 [K.bank_res[bu], rsl], [r_gT[fc]])

    def down_post(g):
        def ybanks(j):
            return (4, 5)
        for j in range(4):
            yb = (4, 5) if j % 2 == 0 else (2, 3)
            for h in range(2):
                b = yb[h]
                pairs = [(gT[:, fc, j * 128:(j + 1) * 128], Wdn[:, fc, h * 512:(h + 1) * 512]) for fc in range(22)]
                K.mm(K.banks[b][:, :], pairs, r_gT + [r_Wdn], [K.bank_res[b]])
            post_norm_tile(K, g, j, ls, srcname, dstname, bufs, yb)

    pre_norm_group(K, 0, ls, srcname, bufs)
    for g in range(NG):
        up(g)
        if g + 1 < NG:
            pre_norm_B(K, g + 1, ls, srcname, bufs)
        if g in (0, 8, 12):
            load_gbc_seq(K, ls, seq_of_group(g), bufs)
        down_post(g)


def post_norm_tile(K, g, j, ls, srcname, dstname, bufs, yb):
    T = K.T
    s = seq_of_group(g)
    xres, r_xres, Gbc, r_G, junk, r_junk = (bufs["xres"], bufs["r_xres"], bufs["Gbc"], bufs["r_G"],
                                            bufs["junk"], bufs["r_junk"])
    t = g * 4 + j
    xs, rxs = xres[t % 2], r_xres[t % 2]
    K.dma(xs, T[srcname][t * 128:(t + 1) * 128, :], [K.dr(srcname, g)], [rxs], rxs)
    b0, b1 = yb
    si = 16 + (t % 4) * 4
    st = [(K.stat[:, si + i:si + i + 1], K.stat_res[si + i]) for i in range(4)]
    K.act(junk, K.banks[b0][:, :], AF.Square, [K.bank_res[b0]], [r_junk, st[0][1]], accum_out=st[0][0])
    K.act(junk, K.banks[b1][:, :], AF.Square, [K.bank_res[b1]], [r_junk, st[1][1]], accum_out=st[1][0])
    K.op("vector", lambda e, o=st[2][0], a=st[0][0], b=st[1][0]: e.tensor_tensor(o, a, b, ALU.add),
         [st[0][1], st[1][1]], [st[2][1]])
    K.act(st[2][0], st[2][0], AF.Sqrt, [st[2][1], K.r_mod], [st[2][1]], bias=K.eps_rms, scale=1.0 / D)
    K.op("vector", lambda e, o=st[3][0], a=st[2][0]: e.reciprocal(o, a), [st[2][1]], [st[3][1]])
    for h, b in ((0, b0), (1, b1)):
        tmp = bufs["tmp"][h]
        r_tmp = bufs["r_tmp"][h]
        K.op("vector", lambda e, o=tmp, a=K.banks[b][:, :], sc=st[3][0], g_=Gbc[s][:, h * 512:(h + 1) * 512]:
             e.scalar_tensor_tensor(out=o, in0=a, scalar=sc, in1=g_, op0=ALU.mult, op1=ALU.mult),
             [K.bank_res[b], st[3][1], r_G[s]], [r_tmp])
        K.op("vector",
             lambda e, o=xs[:, h * 512:(h + 1) * 512], a=tmp: e.tensor_tensor(o, o, a, ALU.add),
             [r_tmp, rxs], [rxs])
    K.dma(T[dstname][t * 128:(t + 1) * 128, :], xs, [rxs], [K.dr(dstname, g)], rxs, q="gpsimd")


_NC_CACHE = {}


def _fm(v, chunks):
    v = np.asarray(v, np.float32)
    lead = v.shape[:-1]
    v = v.reshape(lead + (chunks, 128))
    return np.ascontiguousarray(np.moveaxis(v, -1, 0))


def _band_build(kind, pos, S):
    n = 4 if kind == "pool" else 3
    M = np.zeros((128, n, 128), np.float64)
    H = np.zeros((16, n, 128), np.float64)

    def put(rel, i, t, val):
        if 0 <= rel < 128:
            M[rel, i, t] += val
        elif -8 <= rel < 0:
            H[rel + 8, i, t] += val
        elif 128 <= rel < 136:
            H[rel - 128 + 8, i, t] += val
        else:
            raise AssertionError
    for t in range(128):
        Tt = pos * 128 + t
        if kind == "pool":
            for i, w in enumerate((2, 4, 8, 16)):
                lo = max(Tt - w // 2, 0)
                hi = min(Tt + w // 2, S)
                for tp in range(lo, hi):
                    put(tp - pos * 128, i, t, 1.0 / (hi - lo))
                M[t, i, t] -= 1.0
        else:
            for i, dlt in enumerate((-1, 0, 1)):
                tp = Tt + dlt
                if 0 <= tp < S:
                    put(tp - pos * 128, i, t, 1.0)
    return M, H


_TAB_CACHE = {}


def _core_tables(r):
    if r in _TAB_CACHE:
        return _TAB_CACHE[r]
    bf = ml_dtypes.bfloat16
    out = {}
    for kind, pre in (("pool", "pb"), ("conv", "cb")):
        n = 4 if kind == "pool" else 3
        Mi, Hi = _band_build(kind, 1, 384)
        Mf, Hf = _band_build(kind, 0, 384)
        Ml, Hl = _band_build(kind, 2, 384)
        E0 = np.zeros((64, n, 128))
        E1 = np.zeros((64, n, 128))
        if r == 0:
            M0 = Mf
        else:
            M0 = Mi
            E0[(r - 1) * 16 + 8:(r - 1) * 16 + 16] = Hi[0:8]
        if r == 3:
            M31 = Ml
        else:
            M31 = Mi
            E1[(r + 1) * 16:(r + 1) * 16 + 8] = Hi[8:16]
        out[pre + "M"] = np.stack([Mi, Mf, Ml, M0, M31]).reshape(5, 128, n * 128).astype(np.float32).astype(bf)
        out[pre + "H"] = np.stack([Hi, Hf, Hl]).reshape(3, 16, n * 128).astype(np.float32).astype(bf)
        out[pre + "E"] = np.stack([E0, E1]).reshape(2, 64, n * 128).astype(np.float32).astype(bf)
    two_pi = 2.0 * np.pi
    dd = np.arange(128)
    ang = two_pi * ((dd[:, None] * dd[None, :]) % 128) / 128.0
    out["CD"] = np.concatenate([np.cos(ang)[:, 0:64], np.sin(ang)[:, 0:64], np.cos(ang)[:, 64:128],
                                np.sin(ang)[:, 64:128]], axis=1).astype(np.float32).astype(bf)
    c, s_ = np.cos(ang), np.sin(ang)
    out["F128"] = np.concatenate([c, -s_, -s_, -c], axis=1).astype(np.float32).astype(bf)
    a16 = np.arange(16)
    ang16 = two_pi * ((a16[:, None] * a16[None, :]) % 16) / 16.0
    c, s_ = np.cos(ang16), np.sin(ang16)
    out["F16"] = np.concatenate([c, -s_, -s_, -c], axis=1).astype(np.float32).astype(bf)
    fbd = np.zeros((128, 2, 8, 2, 16), np.float64)
    for u_ in range(8):
        fbd[u_ * 16:(u_ + 1) * 16, 0, u_, 0, :] = c
        fbd[u_ * 16:(u_ + 1) * 16, 0, u_, 1, :] = -s_
        fbd[u_ * 16:(u_ + 1) * 16, 1, u_, 0, :] = -s_
        fbd[u_ * 16:(u_ + 1) * 16, 1, u_, 1, :] = -c
    out["Fbd"] = fbd.reshape(128, 512).astype(np.float32).astype(bf)
    b_ = np.arange(128)[:, None, None]
    k1 = np.arange(128)[None, :, None]
    k2 = (32 * r + np.arange(32))[None, None, :]
    th = two_pi * ((b_ * (k1 + 128 * k2)) % 16384) / 16384.0
    nrm = 1.0 / np.sqrt(16384.0 * 128.0)
    out["GGp"] = np.stack([np.cos(th) * nrm, np.sin(th) * nrm], axis=2).reshape(128, -1).astype(np.float32).astype(bf)
    k1 = np.arange(16)[None, :, None]
    k2 = np.arange(128)[None, None, :]
    th = two_pi * ((b_ * (k1 + 16 * k2)) % 2048) / 2048.0
    nrm = 1.0 / np.sqrt(2048.0 * 128.0)
    out["GGs"] = np.stack([np.cos(th) * nrm, np.sin(th) * nrm], axis=2).reshape(128, -1).astype(np.float32).astype(bf)
    _TAB_CACHE[r] = out
    return out


def kernel(x_prompt, x_sample, c_prompt, c_sample, ada_w, ada_b, norm_pre, norm_post,
           ffn1_w_up, ffn1_w_down, ffn2_w_up, ffn2_w_down,
           ev_w_in, ev_ln_g, ev_ln_b, ev_w_spatial, ev_b_spatial, ev_w_pool, ev_pool_scale, ev_w_out,
           od_w_in, od_conv_w, od_fourier_g, od_fourier_w, od_w_out):
    f32 = np.float32
    if "nc" not in _NC_CACHE:
        _NC_CACHE["nc"] = build_program()
    nc = _NC_CACHE["nc"]
    x_prompt = np.asarray(x_prompt, f32)
    x_sample = np.asarray(x_sample, f32)
    shared = {
        "ada_w": np.ascontiguousarray(np.asarray(ada_w, f32)),
        "ada_bT": np.ascontiguousarray(_fm(ada_b, 72)),
        "gpreT": np.ascontiguousarray(_fm(np.asarray(norm_pre, f32).reshape(6, D), 8)),
        "gpostT": np.ascontiguousarray(_fm(np.asarray(norm_post, f32).reshape(6, D), 8)),
        "ffn1_w_up": np.ascontiguousarray(np.asarray(ffn1_w_up, f32)),
        "ffn1_w_down": np.ascontiguousarray(np.asarray(ffn1_w_down, f32)),
        "ffn2_w_up": np.ascontiguousarray(np.asarray(ffn2_w_up, f32)),
        "ffn2_w_down": np.ascontiguousarray(np.asarray(ffn2_w_down, f32)),
        "identb": np.eye(128, dtype=f32).astype(ml_dtypes.bfloat16),
        "identf": np.eye(128, dtype=f32),
        "onesf": np.ones((128, 128), f32),
    }
    bf = ml_dtypes.bfloat16
    tile128 = lambda v: np.ascontiguousarray(np.broadcast_to(np.asarray(v, f32).reshape(1, -1), (128, np.asarray(v).size)))
    shared.update({
        "ev_w_in": np.ascontiguousarray(np.asarray(ev_w_in, f32)),
        "ev_w_out": np.ascontiguousarray(np.asarray(ev_w_out, f32)),
        "wsT": np.ascontiguousarray(np.transpose(np.asarray(ev_w_spatial, f32)[0], (2, 0, 1))),
        "wpoolT": np.ascontiguousarray(np.transpose(np.asarray(ev_w_pool, f32)[0], (1, 0, 2))),
        "bs_bc": tile128(np.asarray(ev_b_spatial, f32)[0]),
        "lng_bc": tile128(np.asarray(ev_ln_g, f32)[0]),
        "lnb_bc": tile128(np.asarray(ev_ln_b, f32)[0]),
        "pscT": np.ascontiguousarray(np.asarray(ev_pool_scale, f32)[0].reshape(4, 128).T),
        "od_w_in": np.ascontiguousarray(np.asarray(od_w_in, f32)),
        "od_w_out": np.ascontiguousarray(np.asarray(od_w_out, f32)),
        "fwT": np.ascontiguousarray(np.transpose(np.asarray(od_fourier_w, f32)[0], (1, 0, 2))),
        "convT": np.ascontiguousarray(np.transpose(np.asarray(od_conv_w, f32)[0].reshape(3, 4, 128), (2, 1, 0))),
        "fg_bc": tile128(np.asarray(od_fourier_g, f32)[0]),
    })
    in_maps = []
    for i in range(8):
        b, r = i // 4, i % 4
        xc = np.concatenate([x_prompt[b, r * 4096:(r + 1) * 4096], x_sample[2 * i], x_sample[2 * i + 1]], axis=0)
        cc = np.stack([np.asarray(c_prompt, f32)[b], np.asarray(c_sample, f32)[2 * i],
                       np.asarray(c_sample, f32)[2 * i + 1]], axis=0)
        cT = np.ascontiguousarray(np.transpose(cc.reshape(3, 8, 128), (2, 1, 0)))
        m = dict(shared)
        m["x"] = np.ascontiguousarray(xc)
        m["cT"] = cT
        m.update(_core_tables(r))
        in_maps.append(m)
    in_maps = [{k: m[k] for k in nc._in_names} for m in in_maps]
    res = run_bass_kernel_spmd(nc, in_maps, core_ids=list(range(8)))
    y_prompt = np.empty((2, 16384, D), f32)
    y_sample = np.empty((16, 2048, D), f32)
    for i in range(8):
        b, r = i // 4, i % 4
        y = res.results[i]["y"]
        y_prompt[b, r * 4096:(r + 1) * 4096] = y[0:4096]
        y_sample[2 * i] = y[4096:6144]
        y_sample[2 * i + 1] = y[6144:8192]
    return (y_prompt, y_sample)


GROUPS4 = [[0, 1, 2, 3], [4, 5, 6, 7]]
MV = {"int": 0, "first": 1, "last": 2, "p0": 3, "p31": 4}
HV = {"int": 0, "first": 1, "last": 2, "p0": 1, "p31": 2}
EV = {"p0": 0, "p31": 1}


def tile_variant(t):
    if t == 0:
        return "p0"
    if t == 31:
        return "p31"
    if t in (32, 48):
        return "first"
    if t in (47, 63):
        return "last"
    return "int"


def allgather(K, src_t, dst_t, R, W):
    cres = Res("cc_" + src_t.name)
    K.op("gpsimd", lambda e: e.collective_compute(
        "AllGather", ALU.bypass, replica_groups=GROUPS4,
        ins=[src_t.ap().opt()], outs=[dst_t.ap().opt()]), R, W, dma=cres, step=1)


def zero_pads(K, ZN):
    T = K.T
    z = K.alloc([512], BF16)
    rz = Res("zpad")
    K.op("vector", lambda e: e.memset(z[0:16, :], 0.0), [], [rz])
    K.dma(T[ZN][0:8, :], z[0:8, :], [rz], [K.dr(ZN, "pad0")], rz)
    K.dma(T[ZN][8 + ROWS:16 + ROWS, :], z[0:8, :], [rz], [K.dr(ZN, "pad1")], rz)


def store_rows_with_edges(K, zt, rzt, t, ZN, EN):
    T = K.T
    K.dma(T[ZN][8 + t * 128:8 + (t + 1) * 128, :], zt, [rzt], [K.dr(ZN, t)], rzt)
    if t == 0:
        K.dma(T[EN][0:8, :], zt[0:8, :], [rzt], [K.dr(EN, 0)], rzt)
    if t == 31:
        K.dma(T[EN][8:16, :], zt[120:128, :], [rzt], [K.dr(EN, 1)], rzt)


def load_tile_with_halo(K, t, ZN, EGN, zt, rzt, ht, rht, et, ret):
    T = K.T
    K.dma(zt, T[ZN][8 + t * 128:8 + (t + 1) * 128, :], [K.dr(ZN, t)], [rzt], rzt)
    deps = [K.dr(ZN, "pad0"), K.dr(ZN, "pad1")]
    if t > 0:
        deps.append(K.dr(ZN, t - 1))
    if t < NT - 1:
        deps.append(K.dr(ZN, t + 1))
    K.dma(ht[0:8, :], T[ZN][t * 128:t * 128 + 8, :], deps, [rht], rht)
    K.dma(ht[8:16, :], T[ZN][8 + (t + 1) * 128:16 + (t + 1) * 128, :], deps, [rht], rht)
    if t in (0, 31):
        K.dma(et[0:64, :], T[EGN][:, :], [K.dr(EGN, 0)], [ret], ret)


def bandmix(K, t, outs, zt, rzt, ht, rht, et, ret, bM, bH, bE, r_tab):
    var = tile_variant(t)
    for (o, b, cc, tc) in outs:
        pairs = [(zt[:, cc * 128:(cc + 1) * 128], bM[MV[var]][:, tc * 128:(tc + 1) * 128]),
                 (ht[0:16, cc * 128:(cc + 1) * 128], bH[HV[var]][0:16, tc * 128:(tc + 1) * 128])]
        R = [rzt, rht, r_tab]
        if var in EV:
            pairs.append((et[0:64, cc * 128:(cc + 1) * 128], bE[EV[var]][0:64, tc * 128:(tc + 1) * 128]))
            R.append(ret)
        K.mm(o, pairs, R, [K.bank_res[b]])


def load_const(K, shape, dtype, src_ap, name, q="sync"):
    a = K.alloc(shape, dtype)
    r = Res(name)
    K.dma(a, src_ap, [], [r], r, q=q)
    return a, r


def wout_post(K, g, ls, srcname, dstname, bufs, yT, r_yT, Wout, r_Wout):
    for j in range(4):
        tc = slice(j * 128, (j + 1) * 128)
        yb = (4, 5) if j % 2 == 0 else (2, 3)
        for h in range(2):
            b = yb[h]
            K.mm(K.banks[b][:, :], [(yT[:, kc, tc], Wout[:, kc, h * 512:(h + 1) * 512]) for kc in range(8)],
                 [r_yT, r_Wout], [K.bank_res[b]])
        post_norm_tile(K, g, j, ls, srcname, dstname, bufs, yb)


def even_pass2(K, l, sub, srcname, dstname):
    T = K.T
    ls = l * 3 + sub
    j_ = l // 2
    m0 = K.mark()
    bufs = common_bufs(K, n_hT=2)
    Wz = K.alloc([8, 512], BF16)
    r_Wz = Res("Wz")
    load_weight_cast(K, Wz, T["ev_w_in"][j_, :, 1024:1536].rearrange("(k p) f -> p k f", p=128), r_Wz)
    zero_pads(K, "ZB")
    zb = [K.alloc([512], BF16) for _ in range(4)]
    r_zb = [Res("zb%d" % i) for i in range(4)]
    pre_norm_group(K, 0, ls, srcname, bufs)
    pre_norm_A(K, 1, ls, srcname, bufs)
    for g in range(NG):
        hT, r_hT = bufs["hT2"][g % 2], bufs["r_hT2"][g % 2]
        if g + 2 < NG:
            pre_norm_A(K, g + 2, ls, srcname, bufs)
        if g + 1 < NG:
            pre_norm_B(K, g + 1, ls, srcname, bufs)
        for j in range(4):
            b = j
            K.mm(K.banks[b][:, :], [(hT[:, k, j * 128:(j + 1) * 128], Wz[:, k, :]) for k in range(8)],
                 [r_hT, r_Wz], [K.bank_res[b]])
        for j in range(4):
            t = g * 4 + j
            K.op("vector", lambda e, o=zb[j], a=K.banks[j][:, :]: e.tensor_copy(o, a), [K.bank_res[j]], [r_zb[j]])
            store_rows_with_edges(K, zb[j], r_zb[j], t, "ZB", "EDGE")
    K.barrier()
    K.release(m0)
    bufs = common_bufs(K, n_hT=2)
    Win = K.alloc([8, 1024], BF16)
    r_Win = Res("Win")
    load_weight_cast(K, Win, T["ev_w_in"][j_, :, 0:1024].rearrange("(k p) f -> p k f", p=128), r_Win)
    Wout = K.alloc([8, D], BF16)
    r_Wout = Res("Wout")
    load_weight_cast(K, Wout, T["ev_w_out"][j_, :, :].rearrange("(k p) f -> p k f", p=128), r_Wout)
    WsT, r_WsT = load_const(K, [4, 128], BF16, T["wsT"][:, :, :], "WsT", q="gpsimd")
    Wp, r_Wp = load_const(K, [4, 128], BF16, T["wpoolT"][:, :, :], "Wp", q="gpsimd")
    allgather(K, T["EDGE"], T["EDGEG"], [K.dr("EDGE", 0), K.dr("EDGE", 1)], [K.dr("EDGEG", 0)])
    bsb, r_bsb = load_const(K, [512], F32, T["bs_bc"][:, :], "bsb")
    lng, r_lng = load_const(K, [512], F32, T["lng_bc"][:, :], "lng")
    lnb, r_lnb = load_const(K, [512], F32, T["lnb_bc"][:, :], "lnb")
    psc, r_psc = load_const(K, [4], F32, T["pscT"][:, :], "psc")
    r_tab = Res("pooltabs")
    bM = [K.alloc([512], BF16) for _ in range(5)]
    bH = [K.alloc([512], BF16) for _ in range(3)]
    bE = [K.alloc([512], BF16) for _ in range(2)]
    for i in range(5):
        K.dma(bM[i], T["pbM"][i, :, :], [], [r_tab], r_tab)
    for i in range(3):
        K.dma(bH[i][0:16, :], T["pbH"][i, :, :], [], [r_tab], r_tab)
    for i in range(2):
        K.dma(bE[i][0:64, :], T["pbE"][i, :, :], [], [r_tab], r_tab)
    epsln = K.alloc([1], F32)
    r_eps = Res("epsln")
    K.op("vector", lambda e: e.memset(epsln, LN_EPS), [], [r_eps])
    load_gbc(K, ls, bufs)
    uT = K.alloc([4, 512], F32)
    r_uT = [Res("uT%d" % i) for i in range(4)]
    v = [K.alloc([512], F32) for _ in range(4)]
    r_v = [Res("v%d" % i) for i in range(4)]
    vn = [K.alloc([512], BF16) for _ in range(4)]
    r_vn = [Res("vn%d" % i) for i in range(4)]
    tya = [K.alloc([512], F32) for _ in range(2)]
    r_tya = [Res("tya0"), Res("tya1")]
    yT2 = [K.alloc([8, 512], BF16) for _ in range(2)]
    r_yT2 = [Res("yTa"), Res("yTb")]
    zt = [K.alloc([512], BF16) for _ in range(4)]
    r_zt = [Res("zt%d" % i) for i in range(4)]
    ht = [K.alloc([512], BF16) for _ in range(4)]
    r_ht = [Res("ht%d" % i) for i in range(4)]
    et = K.alloc([512], BF16)
    r_et = Res("et")
    dT = [K.alloc([512], BF16) for _ in range(4)]
    r_dT = [Res("dT%d" % i) for i in range(4)]
    lst = K.alloc([7, 4], F32)
    r_lst = [Res("lst%d" % i) for i in range(7)]
    r_s1 = [Res("s1_%d" % i) for i in range(4)]
    r_s2 = [Res("s2_%d" % i) for i in range(4)]
    junk, r_junk = bufs["junk"], bufs["r_junk"]
    s1, s2, mean, msq, var, rstd, nmr = [lst[:, i, :] for i in range(7)]

    pre_norm_group(K, 0, ls, srcname, bufs)
    pre_norm_A(K, 1, ls, srcname, bufs)
    for g in range(NG):
        hT, r_hT = bufs["hT2"][g % 2], bufs["r_hT2"][g % 2]
        yT, r_yT = yT2[g % 2], r_yT2[g % 2]
        if g + 2 < NG:
            pre_norm_A(K, g + 2, ls, srcname, bufs)
        for j in range(4):
            load_tile_with_halo(K, g * 4 + j, "ZB", "EDGEG", zt[j], r_zt[j], ht[j], r_ht[j], et, r_et)
        for j in range(4):
            b = 2 + j % 2
            K.mm(K.banks[b][:, :], [(hT[:, k, j * 128:(j + 1) * 128], Win[:, k, 512:1024]) for k in range(8)],
                 [r_Win, r_hT], [K.bank_res[b]])
            K.act(v[j], K.banks[b][:, :], AF.Gelu, [K.bank_res[b]], [r_v[j], r_s1[j]], accum_out=s1[:, j:j + 1])
        for fc in range(4):
            b = fc % 2
            K.mm(K.banks[b][:, :], [(Win[:, k, fc * 128:(fc + 1) * 128], hT[:, k, :]) for k in range(8)],
                 [r_Win, r_hT], [K.bank_res[b]])
            K.act(uT[:, fc, :], K.banks[b][:, :], AF.Gelu, [K.bank_res[b]], [r_uT[fc]])
        for j in range(4):
            K.act(junk, v[j], AF.Square, [r_v[j]], [r_junk, r_s2[j]], accum_out=s2[:, j:j + 1])
        K.op("vector", lambda e: e.tensor_scalar(mean, s1, 1.0 / 512, None, ALU.mult), r_s1, [r_lst[2]])
        K.op("vector", lambda e: e.tensor_tensor(msq, mean, mean, ALU.mult), [r_lst[2]], [r_lst[3]])
        K.op("vector", lambda e: e.scalar_tensor_tensor(out=var, in0=s2, scalar=1.0 / 512, in1=msq,
                                                         op0=ALU.mult, op1=ALU.subtract),
             r_s2 + [r_lst[3]], [r_lst[4]])
        K.act(rstd, var, AF.Sqrt, [r_lst[4], r_eps], [r_lst[5]], bias=epsln, scale=1.0)
        K.op("vector", lambda e: e.reciprocal(rstd, rstd), [r_lst[5]], [r_lst[5]])
        K.op("vector", lambda e: e.scalar_tensor_tensor(out=nmr, in0=mean, scalar=-1.0, in1=rstd,
                                                         op0=ALU.mult, op1=ALU.mult),
             [r_lst[2], r_lst[5]], [r_lst[6]])
        for j in range(4):
            if j % 2 == 0:
                K.act(v[j], v[j], AF.Identity, [r_lst[5], r_lst[6], r_v[j]], [r_v[j]],
                      bias=nmr[:, j:j + 1], scale=rstd[:, j:j + 1])
            else:
                K.op("vector", lambda e, o=v[j], a=rstd[:, j:j + 1], b_=nmr[:, j:j + 1]:
                     e.tensor_scalar(o, o, a, b_, ALU.mult, ALU.add), [r_lst[5], r_lst[6], r_v[j]], [r_v[j]])
        for j in range(4):
            K.op("vector", lambda e, o=v[j]: e.tensor_tensor(o, o, lng, ALU.mult), [r_v[j], r_lng], [r_v[j]])
            K.op("vector", lambda e, o=vn[j], a=v[j]: e.tensor_tensor(o, a, lnb, ALU.add), [r_v[j], r_lnb], [r_vn[j]])
        for j in range(4):
            t = g * 4 + j
            b = j % 2
            outs = [(K.banks[b][:, gc * 128:(gc + 1) * 128], b, gc, gc) for gc in range(4)]
            bandmix(K, t, outs, zt[j], r_zt[j], ht[j], r_ht[j], et, r_et, bM, bH, bE, r_tab)
            K.act(dT[j], K.banks[b][:, :], AF.Copy, [K.bank_res[b]], [r_dT[j]])
        if g + 1 < NG:
            pre_norm_B(K, g + 1, ls, srcname, bufs, halves=(0,))
        if g > 0:
            wout_post(K, g - 1, ls, srcname, dstname, bufs, yT2[(g - 1) % 2], r_yT2[(g - 1) % 2], Wout, r_Wout)
        if g + 1 < NG:
            pre_norm_B(K, g + 1, ls, srcname, bufs, halves=(1,))
        for j in range(4):
            tc = slice(j * 128, (j + 1) * 128)
            b = j % 2
            for h in range(4):
                K.mm(K.banks[b][:, h * 128:(h + 1) * 128], [(vn[j][:, h * 128:(h + 1) * 128], WsT[:, h, :])],
                     [r_vn[j], r_WsT], [K.bank_res[b]])
            K.op("vector", lambda e, o=tya[j % 2], a=K.banks[b][:, :]: e.tensor_tensor(o, a, bsb, ALU.add),
                 [K.bank_res[b], r_bsb], [r_tya[j % 2]])
            K.op("vector", lambda e, o=yT[:, 0:4, tc], a=tya[j % 2].rearrange("p (h q) -> p h q", h=4),
                 b_=uT[:, :, tc]: e.tensor_tensor(o, a, b_, ALU.mult), [r_tya[j % 2]] + r_uT, [r_yT])
        for j in range(4):
            tc = slice(j * 128, (j + 1) * 128)
            b = j % 2
            for gc in range(4):
                K.mm(K.banks[b][:, gc * 128:(gc + 1) * 128], [(Wp[:, gc, :], dT[j][:, gc * 128:(gc + 1) * 128])],
                     [r_Wp, r_dT[j]], [K.bank_res[b]])
            K.op("vector", lambda e, o=yT[:, 4:8, tc], a=K.banks[b][:, :].rearrange("p (h q) -> p h q", h=4),
                 b_=psc.unsqueeze(2).to_broadcast([128, 4, 128]): e.tensor_tensor(o, a, b_, ALU.mult),
                 [K.bank_res[b], r_psc], [r_yT])
    wout_post(K, NG - 1, ls, srcname, dstname, bufs, yT2[(NG - 1) % 2], r_yT2[(NG - 1) % 2], Wout, r_Wout)


def odd_pass2(K, l, sub, srcname, dstname):
    T = K.T
    ls = l * 3 + sub
    j_ = l // 2
    m0 = K.mark()
    bufs = common_bufs(K, n_hT=2)
    junk, r_junk = bufs["junk"], bufs["r_junk"]
    Wc = K.alloc([8, 512], BF16)
    r_Wc = Res("Wc")
    load_weight_cast(K, Wc, T["od_w_in"][j_, :, 0:512].rearrange("(k p) f -> p k f", p=128), r_Wc)
    Wxd = K.alloc([8, 1024], BF16)
    r_Wxd = Res("Wxd")
    load_weight_cast(K, Wxd, T["od_w_in"][j_, :, 1024:2048].rearrange("(k p) f -> p k f", p=128), r_Wxd)
    fgb, r_fgb = load_const(K, [512], F32, T["fg_bc"][:, :], "fgb")
    CD, r_CD = load_const(K, [256], BF16, T["CD"][:, :], "CD")
    zero_pads(K, "ZC")
    csb = [K.alloc([512], F32) for _ in range(2)]
    r_csb = [Res("csb0"), Res("csb1")]
    z = [K.alloc([512], BF16) for _ in range(4)]
    r_z = [Res("z%d" % i) for i in range(4)]
    zg = [K.alloc([512], BF16) for _ in range(2)]
    r_zg = [Res("zg0"), Res("zg1")]
    ztmp = [K.alloc([512], F32) for _ in range(2)]
    r_ztmp = [Res("ztmp0"), Res("ztmp1")]
    zgT = [K.alloc([512], BF16) for _ in range(2)]
    r_zgT = [Res("zgT0"), Res("zgT1")]
    ab = [K.alloc([1024], BF16) for _ in range(4)]
    r_ab = [Res("ab%d" % i) for i in range(4)]
    sd = K.alloc([2, 16], F32)
    r_ss = [Res("sdss%d" % i) for i in range(16)]
    abp_done = Res("abp_done")
    r_rd = Res("sdr")
    pre_norm_group(K, 0, ls, srcname, bufs)
    pre_norm_A(K, 1, ls, srcname, bufs)
    for g in range(NG):
        hT, r_hT = bufs["hT2"][g % 2], bufs["r_hT2"][g % 2]
        if g + 2 < NG:
            pre_norm_A(K, g + 2, ls, srcname, bufs)
        for j in range(4):
            t = g * 4 + j
            tc = slice(j * 128, (j + 1) * 128)
            b0, b1 = (0, 1) if j % 2 == 0 else (2, 3)
            K.mm(K.banks[b0][:, :], [(hT[:, k, tc], Wc[:, k, :]) for k in range(8)], [r_hT, r_Wc], [K.bank_res[b0]])
            K.mm(K.banks[b1][:, :], [(hT[:, k, tc], Wxd[:, k, 0:512]) for k in range(8)], [r_hT, r_Wxd],
                 [K.bank_res[b1]])
            K.act(csb[j % 2], K.banks[b0][:, :], AF.Copy, [K.bank_res[b0]], [r_csb[j % 2]])
            K.op("vector", lambda e, o=z[j], a=K.banks[b1][:, :], c_=csb[j % 2]: e.tensor_tensor(o, a, c_, ALU.mult),
                 [K.bank_res[b1], r_csb[j % 2]], [r_z[j]])
            store_rows_with_edges(K, z[j], r_z[j], t, "ZC", "EDGE2")
        for jp in (0, 2):
            if g + 1 < NG:
                pre_norm_B(K, g + 1, ls, srcname, bufs, halves=(jp // 2,))
            for j in (jp, jp + 1):
                q2 = j % 2
                tc = slice(j * 128, (j + 1) * 128)
                K.mm(K.banks[q2][:, :], [(hT[:, k, tc], Wxd[:, k, 512:1024]) for k in range(8)], [r_hT, r_Wxd],
                     [K.bank_res[q2]])
                for gq in range(4):
                    K.act(junk[:, 0:128], K.banks[q2][:, gq * 128:(gq + 1) * 128], AF.Square, [K.bank_res[q2]],
                          [r_junk, r_ss[q2 * 4 + gq]], accum_out=sd[:, 0, q2 * 4 + gq:q2 * 4 + gq + 1])
            K.act(sd[:, 1, 0:8], sd[:, 0, 0:8], AF.Sqrt, r_ss[0:8] + [K.r_mod], [r_rd], bias=K.eps_rms, scale=1.0 / 128)
            K.op("vector", lambda e: e.reciprocal(sd[:, 1, 0:8], sd[:, 1, 0:8]), [r_rd], [r_rd])
            for j in (jp, jp + 1):
                t = g * 4 + j
                q2 = j % 2
                rb = sd[:, 1, q2 * 4:(q2 + 1) * 4].unsqueeze(2).to_broadcast([128, 4, 128])
                K.op("vector", lambda e, o=ztmp[q2].rearrange("p (a b) -> p a b", a=4),
                     a=K.banks[q2][:, :].rearrange("p (a b) -> p a b", a=4), rb=rb: e.tensor_tensor(o, a, rb, ALU.mult),
                     [K.bank_res[q2], r_rd], [r_ztmp[q2]])
                K.op("vector", lambda e, o=zg[q2], a=ztmp[q2]: e.tensor_tensor(o, a, fgb, ALU.mult),
                     [r_ztmp[q2], r_fgb], [r_zg[q2]])
                tb = 4 + q2
                pt = K.banks[tb].bitcast(BF16)
                K.op("tensor", lambda e, pt=pt, zz=zg[q2]: [e.transpose(pt[:, q * 128:(q + 1) * 128],
                                                                      zz[:, q * 128:(q + 1) * 128], K.identb)
                                                          for q in range(4)][-1],
                     [r_zg[q2], K.r_identb], [K.bank_res[tb]])
                K.act(zgT[q2], pt[:, 0:512], AF.Copy, [K.bank_res[tb]], [r_zgT[q2]])
                ba, bb = 2, 3
                for gq in range(4):
                    b = ba if gq < 2 else bb
                    K.mm(K.banks[b][:, (gq % 2) * 256:(gq % 2 + 1) * 256],
                         [(zgT[q2][:, gq * 128:(gq + 1) * 128], CD)], [r_zgT[q2], r_CD], [K.bank_res[b]])
                K.act(ab[j][:, 0:512], K.banks[ba][:, :], AF.Copy, [K.bank_res[ba]], [r_ab[j]])
                K.op("vector", lambda e, o=ab[j][:, 512:1024], a=K.banks[bb][:, :]: e.tensor_copy(o, a),
                     [K.bank_res[bb]], [r_ab[j]])
                for u in range(8):
                    if t < 32:
                        K.dma(T["ABP%d" % u][t * 128:(t + 1) * 128, :], ab[j][:, u * 128:(u + 1) * 128],
                              [r_ab[j], abp_done], [K.dr("ABP", u)], r_ab[j])
                    else:
                        K.dma(T["ABS%d" % u][(t - 32) * 128:(t - 31) * 128, :], ab[j][:, u * 128:(u + 1) * 128],
                              [r_ab[j]], [K.dr("ABS", u)], r_ab[j])
        if g >= 7 and g < 15:
            u = g - 7
            allgather(K, T["ABP%d" % u], T["ABG%d" % u], [K.dr("ABP", u)],
                      [K.dr("ABG", u)] + ([abp_done] if u == 0 else []))
            if g == 7:
                allgather(K, T["EDGE2"], T["EDGE2G"], [K.dr("EDGE2", 0), K.dr("EDGE2", 1)], [K.dr("EDGE2G", 0)])
    K.barrier()
    K.release(m0)
    fourier_stage2(K)
    K.barrier()
    K.release(m0)
    bufs = common_bufs(K, n_hT=2)
    Wb = K.alloc([8, 512], BF16)
    r_Wb = Res("Wb")
    load_weight_cast(K, Wb, T["od_w_in"][j_, :, 512:1024].rearrange("(k p) f -> p k f", p=128), r_Wb)
    Wout = K.alloc([8, D], BF16)
    r_Wout = Res("Wout")
    load_weight_cast(K, Wout, T["od_w_out"][j_, :, :].rearrange("(k p) f -> p k f", p=128), r_Wout)
    fw, r_fw = load_const(K, [4, 128], BF16, T["fwT"][:, :, :], "fw", q="gpsimd")
    cw, r_cw = load_const(K, [4, 3], F32, T["convT"][:, :, :], "cw")
    r_tab = Res("convtabs")
    bM = [K.alloc([384], BF16) for _ in range(5)]
    bH = [K.alloc([384], BF16) for _ in range(3)]
    bE = [K.alloc([384], BF16) for _ in range(2)]
    for i in range(5):
        K.dma(bM[i], T["cbM"][i, :, :], [], [r_tab], r_tab)
    for i in range(3):
        K.dma(bH[i][0:16, :], T["cbH"][i, :, :], [], [r_tab], r_tab)
    for i in range(2):
        K.dma(bE[i][0:64, :], T["cbE"][i, :, :], [], [r_tab], r_tab)
    load_gbc(K, ls, bufs)
    bT = K.alloc([4, 512], F32)
    r_bT = [Res("bT%d" % i) for i in range(4)]
    yT2 = [K.alloc([8, 512], BF16) for _ in range(2)]
    r_yT2 = [Res("yTa"), Res("yTb")]
    ydin = K.alloc([4, 512], BF16)
    r_ydin = Res("ydin")
    zt = [K.alloc([512], BF16) for _ in range(4)]
    r_zt = [Res("zt%d" % i) for i in range(4)]
    ht = [K.alloc([512], BF16) for _ in range(4)]
    r_ht = [Res("ht%d" % i) for i in range(4)]
    et = K.alloc([512], BF16)
    r_et = Res("et")
    t1 = [K.alloc([512], F32) for _ in range(2)]
    r_t1 = [Res("t1a"), Res("t1b")]
    t2 = [K.alloc([512], F32) for _ in range(2)]
    r_t2 = [Res("t2a"), Res("t2b")]
    pre_norm_group(K, 0, ls, srcname, bufs)
    pre_norm_A(K, 1, ls, srcname, bufs)
    for g in range(NG):
        hT, r_hT = bufs["hT2"][g % 2], bufs["r_hT2"][g % 2]
        yT, r_yT = yT2[g % 2], r_yT2[g % 2]
        if g + 2 < NG:
            pre_norm_A(K, g + 2, ls, srcname, bufs)
        K.dma(ydin, T["YDT"][:, g * 512:(g + 1) * 512].rearrange("(q m) t -> m q t", q=4), [K.dr("YDT", 0)],
              [r_ydin], r_ydin)
        for j in range(4):
            load_tile_with_halo(K, g * 4 + j, "ZC", "EDGE2G", zt[j], r_zt[j], ht[j], r_ht[j], et, r_et)
        for fc in range(4):
            b = fc % 2
            K.mm(K.banks[b][:, :], [(Wb[:, k, fc * 128:(fc + 1) * 128], hT[:, k, :]) for k in range(8)],
                 [r_Wb, r_hT], [K.bank_res[b]])
            K.act(bT[:, fc, :], K.banks[b][:, :], AF.Copy, [K.bank_res[b]], [r_bT[fc]])
        for gq in range(4):
            b = 2 + gq % 2
            K.mm(K.banks[b][:, :], [(fw[:, gq, :], ydin[:, gq, :])], [r_fw, r_ydin], [K.bank_res[b]])
            if gq % 2 == 0:
                K.act(yT[:, 4 + gq, :], K.banks[b][:, :], AF.Copy, [K.bank_res[b]], [r_yT])
            else:
                K.op("vector", lambda e, o=yT[:, 4 + gq, :], a=K.banks[b][:, :]: e.tensor_copy(o, a),
                     [K.bank_res[b]], [r_yT])
        if g + 1 < NG:
            pre_norm_B(K, g + 1, ls, srcname, bufs, halves=(0,))
        if g > 0:
            wout_post(K, g - 1, ls, srcname, dstname, bufs, yT2[(g - 1) % 2], r_yT2[(g - 1) % 2], Wout, r_Wout)
        if g + 1 < NG:
            pre_norm_B(K, g + 1, ls, srcname, bufs, halves=(1,))
        for j in range(4):
            t = g * 4 + j
            tc = slice(j * 128, (j + 1) * 128)
            tb = (0, 1, 2) if j % 2 == 0 else (3, 4, 5)
            for tap in range(3):
                outs = [(K.banks[tb[tap]][:, cc * 128:(cc + 1) * 128], tb[tap], cc, tap) for cc in range(4)]
                bandmix(K, t, outs, zt[j], r_zt[j], ht[j], r_ht[j], et, r_et, bM, bH, bE, r_tab)
            q2 = j % 2
            v3 = lambda a: a.rearrange("p (a b) -> p a b", a=4)
            wbc = lambda tap: cw[:, :, tap].unsqueeze(2).to_broadcast([128, 4, 128])
            K.op("vector", lambda e, o=v3(t1[q2]), a=v3(K.banks[tb[0]][:, :]), w=wbc(0): e.tensor_tensor(o, a, w, ALU.mult),
                 [K.bank_res[tb[0]], r_cw], [r_t1[q2]])
            K.op("vector", lambda e, o=v3(t2[q2]), a=v3(K.banks[tb[1]][:, :]), w=wbc(1): e.tensor_tensor(o, a, w, ALU.mult),
                 [K.bank_res[tb[1]], r_cw], [r_t2[q2]])
            K.op("vector", lambda e, o=t1[q2], a=t2[q2]: e.tensor_tensor(o, o, a, ALU.add), [r_t2[q2], r_t1[q2]], [r_t1[q2]])
            K.op("vector", lambda e, o=v3(t2[q2]), a=v3(K.banks[tb[2]][:, :]), w=wbc(2): e.tensor_tensor(o, a, w, ALU.mult),
                 [K.bank_res[tb[2]], r_cw, r_t1[q2]], [r_t2[q2]])
            K.op("vector", lambda e, o=t1[q2], a=t2[q2]: e.tensor_tensor(o, o, a, ALU.add), [r_t2[q2], r_t1[q2]], [r_t1[q2]])
            K.op("vector", lambda e, o=yT[:, 0:4, tc], a=v3(t1[q2]), b_=bT[:, :, tc]: e.tensor_tensor(o, a, b_, ALU.mult),
                 [r_t1[q2]] + r_bT, [r_yT])
    wout_post(K, NG - 1, ls, srcname, dstname, bufs, yT2[(NG - 1) % 2], r_yT2[(NG - 1) % 2], Wout, r_Wout)


def fourier_load(K, srcs, X, r_X, dres_in):
    for (p0, p1, sap) in srcs:
        K.dma(X[p0:p1], sap, dres_in, [r_X], r_X)


def fourier_step1(K, NAp, NC, Ft, r_Ft, X, r_X, TTflat, r_TT, cnt):
    MB = 512 // NC
    for bi, m0 in enumerate(range(0, 64, MB)):
        b = bi % 4
        for mi in range(MB):
            m = m0 + mi
            out = K.banks[b][:, mi * NC:(mi + 1) * NC]
            K.mm(out, [(X[0:NAp, :, m], Ft[0:NAp, 0, :]), (X[0:NAp, :, 64 + m], Ft[0:NAp, 1, :])],
                 [r_X, r_Ft], [K.bank_res[b]])
        src = K.banks[b][:, 0:MB * NC]
        dst = TTflat[:, m0:m0 + MB, :].rearrange("p m c -> p (m c)")
        cnt[0] += 1
        if cnt[0] % 2 == 0:
            K.act(dst, src, AF.Copy, [K.bank_res[b]], [r_TT])
        else:
            K.op("vector", lambda e, o=dst, a=src: e.tensor_copy(o, a), [K.bank_res[b]], [r_TT])


def fourier_step3(K, NA, NK2, TT4, r_TT, GG, r_GG, ydg, r_ydg, dst_ap, dres_out, cnt):
    K1B = 512 // NK2
    yv = ydg[0:64, 0:NK2 * NA].rearrange("p (k2 k1) -> p k2 k1", k2=NK2, k1=NA)
    for kb in range(NA // K1B):
        b = 4 + kb % 2
        for kl in range(K1B):
            k1 = kb * K1B + kl
            K.mm(K.banks[b][0:64, kl * NK2:(kl + 1) * NK2],
                 [(TT4[:, :, 0, k1], GG[:, k1, 0, :]), (TT4[:, :, 1, k1], GG[:, k1, 1, :])],
                 [r_TT, r_GG], [K.bank_res[b]])
        src = K.banks[b][0:64, :].rearrange("p (kl k2) -> p k2 kl", kl=K1B, k2=NK2)
        dst = yv[:, :, kb * K1B:(kb + 1) * K1B]
        cnt[0] += 1
        if cnt[0] % 2 == 0:
            K.act(dst, src, AF.Copy, [K.bank_res[b]], [r_ydg])
        else:
            K.op("vector", lambda e, o=dst, a=src: e.tensor_copy(o, a), [K.bank_res[b]], [r_ydg])
    K.dma(dst_ap, ydg[0:64, 0:NK2 * NA], [r_ydg], dres_out, r_ydg, q="gpsimd")


def fourier_stage2(K):
    T = K.T
    F128, r_F128 = load_const(K, [2, 256], BF16, T["F128"][:, :].rearrange("p (a b) -> p a b", a=2), "F128")
    Fbd, r_Fbd = load_const(K, [2, 256], BF16, T["Fbd"][:, :].rearrange("p (a b) -> p a b", a=2), "Fbd")
    GGp, r_GGp = load_const(K, [128, 2, 32], BF16,
                            T["GGp"][:, :].rearrange("p (k a b) -> p k a b", k=128, a=2), "GGp")
    GGs, r_GGs = load_const(K, [16, 2, 128], BF16,
                            T["GGs"][:, :].rearrange("p (k a b) -> p k a b", k=16, a=2), "GGs")
    X = [K.alloc([128, 128], BF16) for _ in range(2)]
    r_X = [Res("X0"), Res("X1")]
    TT = [K.alloc([64, 256], BF16) for _ in range(2)]
    r_TT = [Res("TT0"), Res("TT1")]
    ydg = [K.alloc([4096], BF16) for _ in range(2)]
    r_ydg = [Res("ydg0"), Res("ydg1")]
    cnt = [0]
    units = []
    for q in range(2):
        srcs = [(u * 16, (u + 1) * 16, T["ABS%d" % u][q * 2048:(q + 1) * 2048, :].rearrange("(a b) c -> a b c", b=128))
                for u in range(8)]
        units.append(dict(kind="s", q=q, srcs=srcs, din=[K.dr("ABS", u) for u in range(8)]))
    for u in range(8):
        units.append(dict(kind="p", u=u, srcs=[(0, 128, T["ABG%d" % u][:, :].rearrange("(a b) c -> a b c", b=128))],
                          din=[K.dr("ABG", u)]))
    fourier_load(K, units[0]["srcs"], X[0], r_X[0], units[0]["din"])
    ny = 0
    for n, U in enumerate(units):
        if n + 1 < len(units):
            V = units[n + 1]
            fourier_load(K, V["srcs"], X[(n + 1) % 2], r_X[(n + 1) % 2], V["din"])
        tt, rtt = TT[n % 2], r_TT[n % 2]
        if U["kind"] == "s":
            q = U["q"]
            fourier_step1(K, 128, 256, Fbd, r_Fbd, X[n % 2], r_X[n % 2], tt, rtt, cnt)
            tt5 = tt.rearrange("p m (u part k) -> p m u part k", u=8, part=2, k=16)
            for u in range(8):
                fourier_step3(K, 16, 128, tt5[:, :, u, :, :], rtt, GGs, r_GGs, ydg[ny % 2], r_ydg[ny % 2],
                              T["YDT"][u * 64:(u + 1) * 64, 4096 + q * 2048:4096 + (q + 1) * 2048],
                              [K.dr("YDT", 0)], cnt)
                ny += 1
        else:
            u = U["u"]
            fourier_step1(K, 128, 256, F128, r_F128, X[n % 2], r_X[n % 2], tt, rtt, cnt)
            tt4 = tt.rearrange("p m (part k) -> p m part k", part=2, k=128)
            fourier_step3(K, 128, 32, tt4, rtt, GGp, r_GGp, ydg[ny % 2], r_ydg[ny % 2],
                          T["YDT"][u * 64:(u + 1) * 64, 0:4096], [K.dr("YDT", 0)], cnt)
            ny += 1
```

```python
import contextlib
import numpy as np
import ml_dtypes
import concourse.bass as bass
import concourse.mybir as mybir
from concourse.bass_utils import run_bass_kernel_spmd

F32 = mybir.dt.float32
BF16 = mybir.dt.bfloat16
AF = mybir.ActivationFunctionType
ALU = mybir.AluOpType

D = 1024
DFF = 2816
NT = 64
NG = 16
ROWS = 8192
RMS_EPS = 1e-6
LN_EPS = 1e-5
SEM_LIMIT = 30000
ENGS = ("sync", "gpsimd", "scalar", "vector", "tensor")

NPASS_DEBUG = None


def seq_of_group(g):
    return 0 if g < 8 else (1 if g < 12 else 2)


class Res:
    __slots__ = ("name", "w", "r", "streams")

    def __init__(self, name):
        self.name = name
        self.w = None
        self.r = {}
        self.streams = {}


class Stream:
    def __init__(self, K, step, kind="eng"):
        self.K = K
        self.step = step
        self.kind = kind
        self.si = None
        self.cnt = 0

    def next(self):
        if self.si is None or self.cnt + self.step > SEM_LIMIT:
            self.si, self.cnt = self.K.acquire_sem(self.kind)
        self.cnt += self.step
        return (self.si, self.cnt)

    def last(self):
        if self.si is None:
            return None
        return (self.si, self.cnt)


class Kern:
    def __init__(self, nc, stack, arena_words):
        self.nc = nc
        self.stack = stack
        self.sems = []
        self.ops = {e: [] for e in ENGS}
        self.seen = {e: {} for e in ENGS}
        self.estream = {e: Stream(self, 1) for e in ENGS}
        self.dma_streams = []
        self.dma_res = []
        self.pools = {}
        self.arena = stack.enter_context(nc.sbuf_tensor("arena", [128, arena_words], F32))
        self.arena_words = arena_words
        self.off = 0
        self.banks = [stack.enter_context(nc.psum_tensor("ps%d" % i, [128, 512], F32)) for i in range(8)]
        self.bank_res = [Res("bank%d" % i) for i in range(8)]

    def alloc(self, shape, dtype):
        n = int(np.prod(shape))
        esz = 4 if dtype == F32 else 2
        words = (n * esz + 3) // 4
        assert self.off + words <= self.arena_words, ("SBUF arena overflow", self.off, words)
        ap = self.arena[:, self.off:self.off + words]
        self.off += words
        if dtype != F32:
            ap = ap.bitcast(dtype)[:, :n]
        if len(shape) == 2:
            ap = ap.rearrange("p (a b) -> p a b", a=shape[0], b=shape[1])
        elif len(shape) == 3:
            ap = ap.rearrange("p (a b c) -> p a b c", a=shape[0], b=shape[1], c=shape[2])
        return ap

    def mark(self):
        return self.off

    def release(self, m):
        self.off = m

    def acquire_sem(self, kind="eng"):
        pool = self.pools.setdefault(kind, [])
        while pool:
            si, c = pool.pop()
            if c + 2048 <= SEM_LIMIT:
                return si, c
        s = self.stack.enter_context(self.nc.semaphore("s%d" % len(self.sems)))
        self.sems.append(s)
        return len(self.sems) - 1, 0

    def op(self, eng, fn, R=(), W=(), dma=None, step=16):
        waits = {}

        def need(ev):
            if ev is None:
                return
            si, v = ev
            if waits.get(si, 0) < v:
                waits[si] = v

        for r in R:
            need(r.w)
        for w in W:
            need(w.w)
            for ev in w.r.values():
                need(ev)
        own = self.estream[eng].si
        seen = self.seen[eng]
        wl = []
        for si, v in waits.items():
            if eng == "tensor" and si == own:
                continue
            if seen.get(si, 0) < v:
                seen[si] = v
                wl.append((si, v))
        if dma is not None:
            kind = "cc" if step == 1 else ("sw" if eng == "gpsimd" else "hw")
            st = dma.streams.get(kind)
            if st is None:
                st = Stream(self, step, kind)
                dma.streams[kind] = st
                self.dma_streams.append(st)
                self.dma_res.append(dma)
            ev = st.next()
            inc = st.step
        else:
            ev = self.estream[eng].next()
            inc = 1
        self.ops[eng].append((wl, fn, ev, inc))
        for r in R:
            r.r[ev[0]] = ev
        for w in W:
            w.w = ev
            w.r = {}
        return ev

    def barrier(self):
        evs = []
        for e in ENGS:
            ev = self.estream[e].last()
            if ev is not None:
                evs.append(ev)
        for s in self.dma_streams:
            ev = s.last()
            if ev is not None:
                evs.append(ev)
        for e in ENGS:
            seen = self.seen[e]
            wl = []
            for si, v in evs:
                if seen.get(si, 0) < v:
                    seen[si] = v
                    wl.append((si, v))
            self.ops[e].append((wl, None, None, 0))
        for st in self.dma_streams:
            if st.si is not None:
                self.pools.setdefault(st.kind, []).append((st.si, st.cnt))
        for r in self.dma_res:
            r.streams = {}
        self.dma_streams = []
        self.dma_res = []

    def emit(self):
        nc = self.nc
        K = self

        def run(name, e):
            for wl, fn, ev, inc in K.ops[name]:
                for si, v in wl:
                    e.wait_ge(K.sems[si], v)
                if fn is not None:
                    ins = fn(e)
                    ins.then_inc(K.sems[ev[0]], inc)

        with nc.Block() as block:
            @block.sync
            def _(e):
                run("sync", e)

            @block.gpsimd
            def _(e):
                run("gpsimd", e)

            @block.scalar
            def _(e):
                run("scalar", e)

            @block.vector
            def _(e):
                run("vector", e)

            @block.tensor
            def _(e):
                run("tensor", e)

    def dma(self, out, in_, R, W, res, q="sync", **kw):
        return self.op(q, lambda e: e.dma_start(out=out, in_=in_, **kw), R, W, dma=res)

    def mm(self, out, pairs, R, W, transpose_ident=None):
        n = len(pairs)

        def fn(e):
            ins = None
            for i, (a, b) in enumerate(pairs):
                ins = e.matmul(out, a, b, start=(i == 0), stop=(i == n - 1))
            return ins
        return self.op("tensor", fn, R, W)

    def act(self, out, in_, func, R, W, bias=None, scale=None, accum_out=None):
        kw = {}
        if bias is not None:
            kw["bias"] = bias
        if scale is not None:
            kw["scale"] = scale
        if accum_out is not None:
            kw["accum_out"] = accum_out
        return self.op("scalar", lambda e: e.activation(out, in_, func, **kw), R, W)


def build_program():
    nc = bass.Bass("TRN2", target_bir_lowering=False)
    stack = contextlib.ExitStack()
    with stack:
        T = {}

        in_names = []

        def din(name, shape, dt=F32):
            T[name] = nc.dram_tensor(name, list(shape), dt, kind="ExternalInput")
            in_names.append(name)

        din("x", [ROWS, D])
        din("cT", [128, 8, 3])
        din("ada_w", [2, D, 9 * D])
        din("ada_bT", [128, 2, 72])
        din("gpreT", [128, 6, 8])
        din("gpostT", [128, 6, 8])
        din("ffn1_w_up", [2, D, 2 * DFF])
        din("ffn1_w_down", [2, DFF, D])
        din("ffn2_w_up", [2, D, 2 * DFF])
        din("ffn2_w_down", [2, DFF, D])
        din("identb", [128, 128], BF16)
        din("identf", [128, 128])
        din("onesf", [128, 128])
        T["y"] = nc.dram_tensor("y", [ROWS, D], F32, kind="ExternalOutput")
        T["S0"] = nc.dram_tensor("S0", [ROWS, D], F32)
        T["S1"] = nc.dram_tensor("S1", [ROWS, D], F32)
        T["GB"] = nc.dram_tensor("GB", [18, 128, D], F32)
        din("ev_w_in", [1, D, 1536])
        din("ev_w_out", [1, D, D])
        din("wsT", [128, 4, 128])
        din("wpoolT", [128, 4, 128])
        din("bs_bc", [128, 512])
        din("lng_bc", [128, 512])
        din("lnb_bc", [128, 512])
        din("pscT", [128, 4])
        din("pbM", [5, 128, 512], BF16)
        din("pbH", [3, 16, 512], BF16)
        din("pbE", [2, 64, 512], BF16)
        T["ZB"] = nc.dram_tensor("ZB", [ROWS + 16, 512], BF16)
        T["EDGE"] = nc.dram_tensor("EDGE", [16, 512], BF16)
        T["EDGEG"] = nc.dram_tensor("EDGEG", [64, 512], BF16)
        din("od_w_in", [1, D, 2048])
        din("od_w_out", [1, D, D])
        din("fwT", [128, 4, 128])
        din("convT", [128, 4, 3])
        din("fg_bc", [128, 512])
        din("cbM", [5, 128, 384], BF16)
        din("cbH", [3, 16, 384], BF16)
        din("cbE", [2, 64, 384], BF16)
        din("CD", [128, 256], BF16)
        din("F128", [128, 512], BF16)
        din("F16", [16, 64], BF16)
        din("Fbd", [128, 512], BF16)
        din("GGp", [128, 128 * 2 * 32], BF16)
        din("GGs", [128, 16 * 2 * 128], BF16)
        T["ZC"] = nc.dram_tensor("ZC", [ROWS + 16, 512], BF16)
        T["EDGE2"] = nc.dram_tensor("EDGE2", [16, 512], BF16)
        T["EDGE2G"] = nc.dram_tensor("EDGE2G", [64, 512], BF16)
        for u in range(8):
            T["ABP%d" % u] = nc.dram_tensor("ABP%d" % u, [4096, 128], BF16)
            T["ABG%d" % u] = nc.dram_tensor("ABG%d" % u, [16384, 128], BF16)
            T["ABS%d" % u] = nc.dram_tensor("ABS%d" % u, [4096, 128], BF16)
        T["YDT"] = nc.dram_tensor("YDT", [512, ROWS], BF16)

        K = Kern(nc, stack, 53100)
        K.T = T
        K.dres = {}

        def dr(name, idx):
            key = (name, idx)
            if key not in K.dres:
                K.dres[key] = Res("%s_%s" % key)
            return K.dres[key]
        K.dr = dr

        prologue(K)
        K.barrier()
        passes = [(0, 0, "ffn1"), (0, 1, "even"), (0, 2, "ffn2"), (1, 0, "ffn1"), (1, 1, "odd"), (1, 2, "ffn2")]
        if NPASS_DEBUG is not None:
            passes = passes[:NPASS_DEBUG]
        src = "x"
        for pi, (l, sub, kind) in enumerate(passes):
            dst = "y" if pi == len(passes) - 1 else ("S0" if pi % 2 == 0 else "S1")
            m = K.mark()
            if kind in ("ffn1", "ffn2"):
                ffn_pass(K, l, sub, kind, src, dst)
            elif kind == "even":
                even_pass2(K, l, sub, src, dst)
            else:
                odd_pass2(K, l, sub, src, dst)
            K.barrier()
            K.release(m)
            src = dst
        K.emit()
    nc._in_names = in_names
    return nc


def prologue(K):
    nc, T = K.nc, K.T
    K.identb = K.alloc([128], BF16)
    r_identb = Res("identb")
    K.r_identb = r_identb
    K.dma(K.identb, T["identb"][:, :], [], [r_identb], r_identb)
    K.AT = K.alloc([6, 8, 3], F32)
    K.ST = K.alloc([6, 8, 3], F32)
    K.GT = K.alloc([6, 8, 3], F32)
    K.r_mod = Res("modvecs")
    K.stat = K.alloc([64], F32)
    K.stat_res = [Res("stat%d" % i) for i in range(64)]
    K.eps_rms = K.alloc([1], F32)
    m = K.mark()

    identf = K.alloc([128], F32)
    onesf = K.alloc([128], F32)
    r_identf, r_onesf = Res("identf"), Res("onesf")
    K.dma(identf, T["identf"][:, :], [], [r_identf], r_identf)
    K.dma(onesf, T["onesf"][:, :], [], [r_onesf], r_onesf)
    cT = K.alloc([8, 3], F32)
    adab = K.alloc([2, 72], F32)
    gpre = K.alloc([6, 8], F32)
    gpost = K.alloc([6, 8], F32)
    r_c, r_ab, r_gp, r_gq = Res("cT"), Res("adab"), Res("gpre"), Res("gpost")
    K.dma(cT, T["cT"][:, :, :], [], [r_c], r_c)
    K.dma(adab, T["ada_bT"][:, :, :], [], [r_ab], r_ab)
    K.dma(gpre, T["gpreT"][:, :, :], [], [r_gp], r_gp)
    K.dma(gpost, T["gpostT"][:, :, :], [], [r_gq], r_gq)
    scT = K.alloc([8, 3], BF16)
    r_sc = Res("scT")
    K.act(scT, cT, AF.Silu, [r_c], [r_sc])
    modT = K.alloc([2, 72, 3], F32)
    r_modT = Res("modT")
    aw = [K.alloc([8, 512], BF16) for _ in range(4)]
    r_aw = [Res("aw%d" % i) for i in range(4)]
    it = 0
    for l in range(2):
        bank = K.banks[l]
        rb = K.bank_res[l]
        for fb in range(18):
            slot = it % 4
            it += 1
            src = T["ada_w"][l, :, fb * 512:(fb + 1) * 512].rearrange("(k p) f -> p k f", p=128)
            K.dma(aw[slot], src, [], [r_aw[slot]], r_aw[slot], q="gpsimd")
            for ch in range(4):
                cidx = fb * 4 + ch
                pairs = [(aw[slot][:, k, ch * 128:(ch + 1) * 128], scT[:, k, :]) for k in range(8)]
                K.mm(bank[:, cidx * 3:(cidx + 1) * 3], pairs, [r_aw[slot], r_sc], [rb])
        ps = bank[:, 0:216].rearrange("p (a b) -> p a b", a=72, b=3)
        bia = adab[:, l, :].unsqueeze(2).to_broadcast([128, 72, 3])
        K.op("vector", lambda e, o=modT[:, l], a=ps, b=bia: e.tensor_tensor(o, a, b, ALU.add),
             [rb, r_ab], [r_modT])
    for l in range(2):
        for sub in range(3):
            ls = l * 3 + sub
            sh = modT[:, l, (sub * 3 + 0) * 8:(sub * 3 + 1) * 8, :]
            sc = modT[:, l, (sub * 3 + 1) * 8:(sub * 3 + 2) * 8, :]
            ga = modT[:, l, (sub * 3 + 2) * 8:(sub * 3 + 3) * 8, :]
            gp = gpre[:, ls, :].unsqueeze(2).to_broadcast([128, 8, 3])
            gq = gpost[:, ls, :].unsqueeze(2).to_broadcast([128, 8, 3])
            K.op("vector", lambda e, o=K.AT[:, ls], a=sc, b=gp: e.scalar_tensor_tensor(
                out=o, in0=a, scalar=1.0, in1=b, op0=ALU.add, op1=ALU.mult), [r_modT, r_gp], [K.r_mod])
            K.op("vector", lambda e, o=K.ST[:, ls], a=sh: e.tensor_copy(o, a), [r_modT], [K.r_mod])
            K.op("vector", lambda e, o=K.GT[:, ls], a=ga, b=gq: e.scalar_tensor_tensor(
                out=o, in0=a, scalar=1.0, in1=b, op0=ALU.add, op1=ALU.mult), [r_modT, r_gq], [K.r_mod])
            if sub != 1:
                K.op("vector", lambda e, o=K.GT[:, ls]: e.tensor_scalar(o, o, 0.5, None, ALU.mult),
                     [K.r_mod], [K.r_mod])
    K.op("vector", lambda e: e.memset(K.eps_rms, RMS_EPS), [], [K.r_mod])
    Dg = [K.alloc([8, 128], F32) for _ in range(2)]
    r_Dg = [Res("Dg0"), Res("Dg1")]
    gb = [K.alloc([D], F32) for _ in range(2)]
    r_gb = [Res("gb0"), Res("gb1")]
    it = 0
    for ls in range(6):
        for s in range(3):
            slot = it % 2
            it += 1
            for c in range(8):
                K.op("vector", lambda e, o=Dg[slot][:, c, :], sc1=K.GT[:, ls, c, s:s + 1]: e.tensor_scalar(
                    o, identf, sc1, None, ALU.mult), [r_identf, K.r_mod], [r_Dg[slot]])
            for h in range(2):
                b = 2 + h
                for cc in range(4):
                    c = h * 4 + cc
                    K.mm(K.banks[b][:, cc * 128:(cc + 1) * 128], [(onesf, Dg[slot][:, c, :])],
                         [r_onesf, r_Dg[slot]], [K.bank_res[b]])
                K.act(gb[slot][:, h * 512:(h + 1) * 512], K.banks[b][:, :], AF.Identity,
                      [K.bank_res[b]], [r_gb[slot]])
            K.dma(T["GB"][ls * 3 + s, :, :], gb[slot], [r_gb[slot]], [K.dr("GB", ls * 3 + s)], r_gb[slot])
    K.barrier()
    K.release(m)


def load_weight_cast(K, dst, src, res):
    K.dma(dst, src, [], [res], res, q="gpsimd")


def pre_norm_A(K, g, ls, srcname, bufs):
    T = K.T
    xin, r_xin, xn, r_xn = bufs["xin"], bufs["r_xin"], bufs["xn"], bufs["r_xn"]
    gi = g % 2
    ss4 = K.stat[:, gi * 8:gi * 8 + 4]
    rs4 = K.stat[:, gi * 8 + 4:gi * 8 + 8]
    r_ss = K.stat_res[gi * 8:gi * 8 + 4]
    r_rs = K.stat_res[gi * 8 + 4]
    for j in range(4):
        t = g * 4 + j
        xs, rxs = xin[t % len(xin)], r_xin[t % len(xin)]
        K.dma(xs, T[srcname][t * 128:(t + 1) * 128, :], [K.dr(srcname, g)], [rxs], rxs)
    for j in range(4):
        t = g * 4 + j
        xs, rxs = xin[t % len(xin)], r_xin[t % len(xin)]
        K.act(xn[t % len(xn)], xs, AF.Square, [rxs], [r_xn[t % len(xn)], r_ss[j]], accum_out=ss4[:, j:j + 1])
    K.act(rs4, ss4, AF.Sqrt, r_ss + [K.r_mod], [r_rs], bias=K.eps_rms, scale=1.0 / D)
    K.op("vector", lambda e, o=rs4: e.reciprocal(o, o), [r_rs], [r_rs])
    for j in range(4):
        t = g * 4 + j
        xs, rxs = xin[t % len(xin)], r_xin[t % len(xin)]
        if j % 2 == 0:
            K.op("vector", lambda e, o=xn[t % len(xn)], a=xs, b=rs4[:, j:j + 1]: e.tensor_scalar(o, a, b, None, ALU.mult),
                 [rxs, r_rs], [r_xn[t % len(xn)]])
        else:
            K.act(xn[t % len(xn)], xs, AF.Identity, [rxs, r_rs], [r_xn[t % len(xn)]], scale=rs4[:, j:j + 1])


def pre_norm_B(K, g, ls, srcname, bufs, halves=(0, 1)):
    s = seq_of_group(g)
    xn, r_xn, hT, r_hT = bufs["xn"], bufs["r_xn"], bufs["hT"], bufs["r_hT"]
    if "hT2" in bufs:
        hT, r_hT = bufs["hT2"][g % 2], bufs["r_hT2"][g % 2]
    pts = [K.banks[6].bitcast(BF16), K.banks[7].bitcast(BF16)]
    for half in halves:
        for j in range(4):
            t = g * 4 + j
            xns, rxns = xn[t % len(xn)], r_xn[t % len(xn)]

            def fn(e, xns=xns, j=j, half=half):
                ins = None
                for cl in range(4):
                    c = half * 4 + cl
                    off = (cl % 2) * 512 + j * 128
                    ins = e.transpose(pts[cl // 2][:, off:off + 128], xns[:, c * 128:(c + 1) * 128], K.identb)
                return ins
            K.op("tensor", fn, [rxns, K.r_identb], [K.bank_res[6], K.bank_res[7]])
        for cl in range(4):
            c = half * 4 + cl
            tb = 6 + cl // 2
            o = hT[:, c, :]
            i_ = pts[cl // 2][:, (cl % 2) * 512:(cl % 2 + 1) * 512]
            a_ = K.AT[:, ls, c, s:s + 1]
            b_ = K.ST[:, ls, c, s:s + 1]
            if c % 2 == 0:
                K.act(o, i_, AF.Identity, [K.bank_res[tb], K.r_mod], [r_hT], bias=b_, scale=a_)
            else:
                K.op("vector", lambda e, o=o, i_=i_, a_=a_, b_=b_: e.tensor_scalar(o, i_, a_, b_, ALU.mult, ALU.add),
                     [K.bank_res[tb], K.r_mod], [r_hT])


def pre_norm_group(K, g, ls, srcname, bufs):
    pre_norm_A(K, g, ls, srcname, bufs)
    pre_norm_B(K, g, ls, srcname, bufs)


def load_gbc_seq(K, ls, s, bufs):
    K.dma(bufs["Gbc"][s], K.T["GB"][ls * 3 + s, :, :], [K.dr("GB", ls * 3 + s)], [bufs["r_G"][s]], bufs["r_G"][s])


def load_gbc(K, ls, bufs):
    for s in range(3):
        K.dma(bufs["Gbc"][s], K.T["GB"][ls * 3 + s, :, :], [K.dr("GB", ls * 3 + s)], [bufs["r_G"][s]], bufs["r_G"][s])


def common_bufs(K, n_gbc=3, n_xin=4, n_hT=1):
    n_x = 8 if n_hT == 2 else 4
    b = {}
    b["xin"] = [K.alloc([D], F32) for _ in range(n_x)]
    b["r_xin"] = [Res("xin%d" % i) for i in range(n_x)]
    b["xres"] = [K.alloc([D], F32) for _ in range(2)]
    b["r_xres"] = [Res("xres%d" % i) for i in range(2)]
    b["xn"] = [K.alloc([D], BF16) for _ in range(n_x)]
    b["r_xn"] = [Res("xn%d" % i) for i in range(n_x)]
    b["hT"] = K.alloc([8, 512], BF16)
    b["r_hT"] = Res("hT")
    if n_hT == 2:
        b["hT2"] = [b["hT"], K.alloc([8, 512], BF16)]
        b["r_hT2"] = [b["r_hT"], Res("hTb")]
    if n_gbc == 3:
        b["Gbc"] = [K.alloc([D], F32) for _ in range(3)]
        b["r_G"] = [Res("G%d" % i) for i in range(3)]
    else:
        g1, r1 = K.alloc([D], F32), Res("G")
        b["Gbc"] = [g1, g1, g1]
        b["r_G"] = [r1, r1, r1]
    b["junk"] = K.alloc([512], BF16)
    b["r_junk"] = Res("junk")
    b["tmp"] = [K.alloc([512], F32) for _ in range(2)]
    b["r_tmp"] = [Res("tmp%d" % i) for i in range(2)]
    return b


def ffn_pass(K, l, sub, kind, srcname, dstname):
    T = K.T
    ls = l * 3 + sub
    Wup = K.alloc([8, 2 * DFF], BF16)
    Wdn = K.alloc([22, D], BF16)
    r_Wupb = [Res("Wup%d" % b) for b in range(11)]
    r_Wdn = Res("Wdn")
    wu = T[kind + "_w_up"]
    wd = T[kind + "_w_down"]
    for b in range(11):
        for off in (0, DFF):
            c0 = off + b * 256
            load_weight_cast(K, Wup[:, :, c0:c0 + 256],
                             wu[l, :, c0:c0 + 256].rearrange("(k p) f -> p k f", p=128), r_Wupb[b])
    for q in range(2):
        load_weight_cast(K, Wdn[:, q * 11:(q + 1) * 11, :],
                         wd[l, q * 11 * 128:(q + 1) * 11 * 128, :].rearrange("(f p) d -> p f d", p=128), r_Wdn)
    bufs = common_bufs(K, n_gbc=1)
    gT = K.alloc([22, 512], BF16)
    r_gT = [Res("gT%d" % i) for i in range(22)]
    sil, r_sil = bufs["tmp"], bufs["r_tmp"]
    hT, r_hT = bufs["hT"], bufs["r_hT"]

    def up(g):
        for fc in range(22):
            if fc == 6 and g + 1 < NG:
                pre_norm_A(K, g + 1, ls, srcname, bufs)
            bg, bu = (0, 1) if fc % 2 == 0 else (2, 3)
            pg = [(Wup[:, k, fc * 128:(fc + 1) * 128], hT[:, k, :]) for k in range(8)]
            pu = [(Wup[:, k, DFF + fc * 128:DFF + (fc + 1) * 128], hT[:, k, :]) for k in range(8)]
            K.mm(K.banks[bg][:, :], pg, [r_Wupb[fc // 2], r_hT], [K.bank_res[bg]])
            K.mm(K.banks[bu][:, :], pu, [r_Wupb[fc // 2], r_hT], [K.bank_res[bu]])
            sl, rsl = sil[fc % 2], r_sil[fc % 2]
            K.act(sl, K.banks[bg][:, :], AF.Silu, [K.bank_res[bg]], [rsl])
            K.op("vector", lambda e, o=gT[:, fc, :], a=K.banks[bu][:, :], b=sl: e.tensor_tensor(o, a, b, ALU.mult),
                 [K.bank_res[bu], rsl], [r_gT[fc]])

    def down_post(g):
        def ybanks(j):
            return (4, 5)
        for j in range(4):
            yb = (4, 5) if j % 2 == 0 else (2, 3)
            for h in range(2):
                b = yb[h]
                pairs = [(gT[:, fc, j * 128:(j + 1) * 128], Wdn[:, fc, h * 512:(h + 1) * 512]) for fc in range(22)]
                K.mm(K.banks[b][:, :], pairs, r_gT + [r_Wdn], [K.bank_res[b]])
            post_norm_tile(K, g, j, ls, srcname, dstname, bufs, yb)

    pre_norm_group(K, 0, ls, srcname, bufs)
    for g in range(NG):
        up(g)
        if g + 1 < NG:
            pre_norm_B(K, g + 1, ls, srcname, bufs)
        if g in (0, 8, 12):
            load_gbc_seq(K, ls, seq_of_group(g), bufs)
        down_post(g)


def post_norm_tile(K, g, j, ls, srcname, dstname, bufs, yb):
    T = K.T
    s = seq_of_group(g)
    xres, r_xres, Gbc, r_G, junk, r_junk = (bufs["xres"], bufs["r_xres"], bufs["Gbc"], bufs["r_G"],
                                            bufs["junk"], bufs["r_junk"])
    t = g * 4 + j
    xs, rxs = xres[t % 2], r_xres[t % 2]
    K.dma(xs, T[srcname][t * 128:(t + 1) * 128, :], [K.dr(srcname, g)], [rxs], rxs)
    b0, b1 = yb
    si = 16 + (t % 4) * 4
    st = [(K.stat[:, si + i:si + i + 1], K.stat_res[si + i]) for i in range(4)]
    K.act(junk, K.banks[b0][:, :], AF.Square, [K.bank_res[b0]], [r_junk, st[0][1]], accum_out=st[0][0])
    K.act(junk, K.banks[b1][:, :], AF.Square, [K.bank_res[b1]], [r_junk, st[1][1]], accum_out=st[1][0])
    K.op("vector", lambda e, o=st[2][0], a=st[0][0], b=st[1][0]: e.tensor_tensor(o, a, b, ALU.add),
         [st[0][1], st[1][1]], [st[2][1]])
    K.act(st[2][0], st[2][0], AF.Sqrt, [st[2][1], K.r_mod], [st[2][1]], bias=K.eps_rms, scale=1.0 / D)
    K.op("vector", lambda e, o=st[3][0], a=st[2][0]: e.reciprocal(o, a), [st[2][1]], [st[3][1]])
    for h, b in ((0, b0), (1, b1)):
        tmp = bufs["tmp"][h]
        r_tmp = bufs["r_tmp"][h]
        K.op("vector", lambda e, o=tmp, a=K.banks[b][:, :], sc=st[3][0], g_=Gbc[s][:, h * 512:(h + 1) * 512]:
             e.scalar_tensor_tensor(out=o, in0=a, scalar=sc, in1=g_, op0=ALU.mult, op1=ALU.mult),
             [K.bank_res[b], st[3][1], r_G[s]], [r_tmp])
        K.op("gpsimd" if h == 0 else "vector",
             lambda e, o=xs[:, h * 512:(h + 1) * 512], a=tmp: e.tensor_tensor(o, o, a, ALU.add),
             [r_tmp, rxs], [rxs])
    K.dma(T[dstname][t * 128:(t + 1) * 128, :], xs, [rxs], [K.dr(dstname, g)], rxs, q="gpsimd")


_NC_CACHE = {}


def _fm(v, chunks):
    v = np.asarray(v, np.float32)
    lead = v.shape[:-1]
    v = v.reshape(lead + (chunks, 128))
    return np.ascontiguousarray(np.moveaxis(v, -1, 0))


def _band_build(kind, pos, S):
    n = 4 if kind == "pool" else 3
    M = np.zeros((128, n, 128), np.float64)
    H = np.zeros((16, n, 128), np.float64)

    def put(rel, i, t, val):
        if 0 <= rel < 128:
            M[rel, i, t] += val
        elif -8 <= rel < 0:
            H[rel + 8, i, t] += val
        elif 128 <= rel < 136:
            H[rel - 128 + 8, i, t] += val
        else:
            raise AssertionError
    for t in range(128):
        Tt = pos * 128 + t
        if kind == "pool":
            for i, w in enumerate((2, 4, 8, 16)):
                lo = max(Tt - w // 2, 0)
                hi = min(Tt + w // 2, S)
                for tp in range(lo, hi):
                    put(tp - pos * 128, i, t, 1.0 / (hi - lo))
                M[t, i, t] -= 1.0
        else:
            for i, dlt in enumerate((-1, 0, 1)):
                tp = Tt + dlt
                if 0 <= tp < S:
                    put(tp - pos * 128, i, t, 1.0)
    return M, H


_TAB_CACHE = {}


def _core_tables(r):
    if r in _TAB_CACHE:
        return _TAB_CACHE[r]
    bf = ml_dtypes.bfloat16
    out = {}
    for kind, pre in (("pool", "pb"), ("conv", "cb")):
        n = 4 if kind == "pool" else 3
        Mi, Hi = _band_build(kind, 1, 384)
        Mf, Hf = _band_build(kind, 0, 384)
        Ml, Hl = _band_build(kind, 2, 384)
        E0 = np.zeros((64, n, 128))
        E1 = np.zeros((64, n, 128))
        if r == 0:
            M0 = Mf
        else:
            M0 = Mi
            E0[(r - 1) * 16 + 8:(r - 1) * 16 + 16] = Hi[0:8]
        if r == 3:
            M31 = Ml
        else:
            M31 = Mi
            E1[(r + 1) * 16:(r + 1) * 16 + 8] = Hi[8:16]
        out[pre + "M"] = np.stack([Mi, Mf, Ml, M0, M31]).reshape(5, 128, n * 128).astype(np.float32).astype(bf)
        out[pre + "H"] = np.stack([Hi, Hf, Hl]).reshape(3, 16, n * 128).astype(np.float32).astype(bf)
        out[pre + "E"] = np.stack([E0, E1]).reshape(2, 64, n * 128).astype(np.float32).astype(bf)
    two_pi = 2.0 * np.pi
    dd = np.arange(128)
    ang = two_pi * ((dd[:, None] * dd[None, :]) % 128) / 128.0
    out["CD"] = np.concatenate([np.cos(ang)[:, 0:64], np.sin(ang)[:, 0:64], np.cos(ang)[:, 64:128],
                                np.sin(ang)[:, 64:128]], axis=1).astype(np.float32).astype(bf)
    c, s_ = np.cos(ang), np.sin(ang)
    out["F128"] = np.concatenate([c, -s_, -s_, -c], axis=1).astype(np.float32).astype(bf)
    a16 = np.arange(16)
    ang16 = two_pi * ((a16[:, None] * a16[None, :]) % 16) / 16.0
    c, s_ = np.cos(ang16), np.sin(ang16)
    out["F16"] = np.concatenate([c, -s_, -s_, -c], axis=1).astype(np.float32).astype(bf)
    fbd = np.zeros((128, 2, 8, 2, 16), np.float64)
    for u_ in range(8):
        fbd[u_ * 16:(u_ + 1) * 16, 0, u_, 0, :] = c
        fbd[u_ * 16:(u_ + 1) * 16, 0, u_, 1, :] = -s_
        fbd[u_ * 16:(u_ + 1) * 16, 1, u_, 0, :] = -s_
        fbd[u_ * 16:(u_ + 1) * 16, 1, u_, 1, :] = -c
    out["Fbd"] = fbd.reshape(128, 512).astype(np.float32).astype(bf)
    b_ = np.arange(128)[:, None, None]
    k1 = np.arange(128)[None, :, None]
    k2 = (32 * r + np.arange(32))[None, None, :]
    th = two_pi * ((b_ * (k1 + 128 * k2)) % 16384) / 16384.0
    nrm = 1.0 / np.sqrt(16384.0 * 128.0)
    out["GGp"] = np.stack([np.cos(th) * nrm, np.sin(th) * nrm], axis=2).reshape(128, -1).astype(np.float32).astype(bf)
    k1 = np.arange(16)[None, :, None]
    k2 = np.arange(128)[None, None, :]
    th = two_pi * ((b_ * (k1 + 16 * k2)) % 2048) / 2048.0
    nrm = 1.0 / np.sqrt(2048.0 * 128.0)
    out["GGs"] = np.stack([np.cos(th) * nrm, np.sin(th) * nrm], axis=2).reshape(128, -1).astype(np.float32).astype(bf)
    _TAB_CACHE[r] = out
    return out


def kernel(x_prompt, x_sample, c_prompt, c_sample, ada_w, ada_b, norm_pre, norm_post,
           ffn1_w_up, ffn1_w_down, ffn2_w_up, ffn2_w_down,
           ev_w_in, ev_ln_g, ev_ln_b, ev_w_spatial, ev_b_spatial, ev_w_pool, ev_pool_scale, ev_w_out,
           od_w_in, od_conv_w, od_fourier_g, od_fourier_w, od_w_out):
    f32 = np.float32
    if "nc" not in _NC_CACHE:
        _NC_CACHE["nc"] = build_program()
    nc = _NC_CACHE["nc"]
    x_prompt = np.asarray(x_prompt, f32)
    x_sample = np.asarray(x_sample, f32)
    shared = {
        "ada_w": np.ascontiguousarray(np.asarray(ada_w, f32)),
        "ada_bT": np.ascontiguousarray(_fm(ada_b, 72)),
        "gpreT": np.ascontiguousarray(_fm(np.asarray(norm_pre, f32).reshape(6, D), 8)),
        "gpostT": np.ascontiguousarray(_fm(np.asarray(norm_post, f32).reshape(6, D), 8)),
        "ffn1_w_up": np.ascontiguousarray(np.asarray(ffn1_w_up, f32)),
        "ffn1_w_down": np.ascontiguousarray(np.asarray(ffn1_w_down, f32)),
        "ffn2_w_up": np.ascontiguousarray(np.asarray(ffn2_w_up, f32)),
        "ffn2_w_down": np.ascontiguousarray(np.asarray(ffn2_w_down, f32)),
        "identb": np.eye(128, dtype=f32).astype(ml_dtypes.bfloat16),
        "identf": np.eye(128, dtype=f32),
        "onesf": np.ones((128, 128), f32),
    }
    bf = ml_dtypes.bfloat16
    tile128 = lambda v: np.ascontiguousarray(np.broadcast_to(np.asarray(v, f32).reshape(1, -1), (128, np.asarray(v).size)))
    shared.update({
        "ev_w_in": np.ascontiguousarray(np.asarray(ev_w_in, f32)),
        "ev_w_out": np.ascontiguousarray(np.asarray(ev_w_out, f32)),
        "wsT": np.ascontiguousarray(np.transpose(np.asarray(ev_w_spatial, f32)[0], (2, 0, 1))),
        "wpoolT": np.ascontiguousarray(np.transpose(np.asarray(ev_w_pool, f32)[0], (1, 0, 2))),
        "bs_bc": tile128(np.asarray(ev_b_spatial, f32)[0]),
        "lng_bc": tile128(np.asarray(ev_ln_g, f32)[0]),
        "lnb_bc": tile128(np.asarray(ev_ln_b, f32)[0]),
        "pscT": np.ascontiguousarray(np.asarray(ev_pool_scale, f32)[0].reshape(4, 128).T),
        "od_w_in": np.ascontiguousarray(np.asarray(od_w_in, f32)),
        "od_w_out": np.ascontiguousarray(np.asarray(od_w_out, f32)),
        "fwT": np.ascontiguousarray(np.transpose(np.asarray(od_fourier_w, f32)[0], (1, 0, 2))),
        "convT": np.ascontiguousarray(np.transpose(np.asarray(od_conv_w, f32)[0].reshape(3, 4, 128), (2, 1, 0))),
        "fg_bc": tile128(np.asarray(od_fourier_g, f32)[0]),
    })
    in_maps = []
    for i in range(8):
        b, r = i // 4, i % 4
        xc = np.concatenate([x_prompt[b, r * 4096:(r + 1) * 4096], x_sample[2 * i], x_sample[2 * i + 1]], axis=0)
        cc = np.stack([np.asarray(c_prompt, f32)[b], np.asarray(c_sample, f32)[2 * i],
                       np.asarray(c_sample, f32)[2 * i + 1]], axis=0)
        cT = np.ascontiguousarray(np.transpose(cc.reshape(3, 8, 128), (2, 1, 0)))
        m = dict(shared)
        m["x"] = np.ascontiguousarray(xc)
        m["cT"] = cT
        m.update(_core_tables(r))
        in_maps.append(m)
    in_maps = [{k: m[k] for k in nc._in_names} for m in in_maps]
    res = run_bass_kernel_spmd(nc, in_maps, core_ids=list(range(8)))
    y_prompt = np.empty((2, 16384, D), f32)
    y_sample = np.empty((16, 2048, D), f32)
    for i in range(8):
        b, r = i // 4, i % 4
        y = res.results[i]["y"]
        y_prompt[b, r * 4096:(r + 1) * 4096] = y[0:4096]
        y_sample[2 * i] = y[4096:6144]
        y_sample[2 * i + 1] = y[6144:8192]
    return (y_prompt, y_sample)


GROUPS4 = [[0, 1, 2, 3], [4, 5, 6, 7]]
MV = {"int": 0, "first": 1, "last": 2, "p0": 3, "p31": 4}
HV = {"int": 0, "first": 1, "last": 2, "p0": 1, "p31": 2}
EV = {"p0": 0, "p31": 1}


def tile_variant(t):
    if t == 0:
        return "p0"
    if t == 31:
        return "p31"
    if t in (32, 48):
        return "first"
    if t in (47, 63):
        return "last"
    return "int"


def allgather(K, src_t, dst_t, R, W):
    cres = Res("cc_" + src_t.name)
    K.op("gpsimd", lambda e: e.collective_compute(
        "AllGather", ALU.bypass, replica_groups=GROUPS4,
        ins=[src_t.ap().opt()], outs=[dst_t.ap().opt()]), R, W, dma=cres, step=1)


def zero_pads(K, ZN):
    T = K.T
    z = K.alloc([512], BF16)
    rz = Res("zpad")
    K.op("vector", lambda e: e.memset(z[0:16, :], 0.0), [], [rz])
    K.dma(T[ZN][0:8, :], z[0:8, :], [rz], [K.dr(ZN, "pad0")], rz)
    K.dma(T[ZN][8 + ROWS:16 + ROWS, :], z[0:8, :], [rz], [K.dr(ZN, "pad1")], rz)


def store_rows_with_edges(K, zt, rzt, t, ZN, EN):
    T = K.T
    K.dma(T[ZN][8 + t * 128:8 + (t + 1) * 128, :], zt, [rzt], [K.dr(ZN, t)], rzt)
    if t == 0:
        K.dma(T[EN][0:8, :], zt[0:8, :], [rzt], [K.dr(EN, 0)], rzt)
    if t == 31:
        K.dma(T[EN][8:16, :], zt[120:128, :], [rzt], [K.dr(EN, 1)], rzt)


def load_tile_with_halo(K, t, ZN, EGN, zt, rzt, ht, rht, et, ret):
    T = K.T
    K.dma(zt, T[ZN][8 + t * 128:8 + (t + 1) * 128, :], [K.dr(ZN, t)], [rzt], rzt)
    deps = [K.dr(ZN, "pad0"), K.dr(ZN, "pad1")]
    if t > 0:
        deps.append(K.dr(ZN, t - 1))
    if t < NT - 1:
        deps.append(K.dr(ZN, t + 1))
    K.dma(ht[0:8, :], T[ZN][t * 128:t * 128 + 8, :], deps, [rht], rht)
    K.dma(ht[8:16, :], T[ZN][8 + (t + 1) * 128:16 + (t + 1) * 128, :], deps, [rht], rht)
    if t in (0, 31):
        K.dma(et[0:64, :], T[EGN][:, :], [K.dr(EGN, 0)], [ret], ret)


def bandmix(K, t, outs, zt, rzt, ht, rht, et, ret, bM, bH, bE, r_tab):
    var = tile_variant(t)
    for (o, b, cc, tc) in outs:
        pairs = [(zt[:, cc * 128:(cc + 1) * 128], bM[MV[var]][:, tc * 128:(tc + 1) * 128]),
                 (ht[0:16, cc * 128:(cc + 1) * 128], bH[HV[var]][0:16, tc * 128:(tc + 1) * 128])]
        R = [rzt, rht, r_tab]
        if var in EV:
            pairs.append((et[0:64, cc * 128:(cc + 1) * 128], bE[EV[var]][0:64, tc * 128:(tc + 1) * 128]))
            R.append(ret)
        K.mm(o, pairs, R, [K.bank_res[b]])


def load_const(K, shape, dtype, src_ap, name, q="sync"):
    a = K.alloc(shape, dtype)
    r = Res(name)
    K.dma(a, src_ap, [], [r], r, q=q)
    return a, r


def wout_post(K, g, ls, srcname, dstname, bufs, yT, r_yT, Wout, r_Wout):
    for j in range(4):
        tc = slice(j * 128, (j + 1) * 128)
        yb = (4, 5) if j % 2 == 0 else (2, 3)
        for h in range(2):
            b = yb[h]
            K.mm(K.banks[b][:, :], [(yT[:, kc, tc], Wout[:, kc, h * 512:(h + 1) * 512]) for kc in range(8)],
                 [r_yT, r_Wout], [K.bank_res[b]])
        post_norm_tile(K, g, j, ls, srcname, dstname, bufs, yb)


def even_pass2(K, l, sub, srcname, dstname):
    T = K.T
    ls = l * 3 + sub
    j_ = l // 2
    m0 = K.mark()
    bufs = common_bufs(K, n_hT=2)
    Wz = K.alloc([8, 512], BF16)
    r_Wz = Res("Wz")
    load_weight_cast(K, Wz, T["ev_w_in"][j_, :, 1024:1536].rearrange("(k p) f -> p k f", p=128), r_Wz)
    zero_pads(K, "ZB")
    zb = [K.alloc([512], BF16) for _ in range(4)]
    r_zb = [Res("zb%d" % i) for i in range(4)]
    pre_norm_group(K, 0, ls, srcname, bufs)
    pre_norm_A(K, 1, ls, srcname, bufs)
    for g in range(NG):
        hT, r_hT = bufs["hT2"][g % 2], bufs["r_hT2"][g % 2]
        if g + 2 < NG:
            pre_norm_A(K, g + 2, ls, srcname, bufs)
        if g + 1 < NG:
            pre_norm_B(K, g + 1, ls, srcname, bufs)
        for j in range(4):
            b = j
            K.mm(K.banks[b][:, :], [(hT[:, k, j * 128:(j + 1) * 128], Wz[:, k, :]) for k in range(8)],
                 [r_hT, r_Wz], [K.bank_res[b]])
        for j in range(4):
            t = g * 4 + j
            if j % 2 == 0:
                K.act(zb[j], K.banks[j][:, :], AF.Identity, [K.bank_res[j]], [r_zb[j]])
            else:
                K.op("vector", lambda e, o=zb[j], a=K.banks[j][:, :]: e.tensor_copy(o, a), [K.bank_res[j]], [r_zb[j]])
            store_rows_with_edges(K, zb[j], r_zb[j], t, "ZB", "EDGE")
    K.barrier()
    K.release(m0)
    bufs = common_bufs(K, n_hT=2)
    Win = K.alloc([8, 1024], BF16)
    r_Win = Res("Win")
    load_weight_cast(K, Win, T["ev_w_in"][j_, :, 0:1024].rearrange("(k p) f -> p k f", p=128), r_Win)
    Wout = K.alloc([8, D], BF16)
    r_Wout = Res("Wout")
    load_weight_cast(K, Wout, T["ev_w_out"][j_, :, :].rearrange("(k p) f -> p k f", p=128), r_Wout)
    WsT, r_WsT = load_const(K, [4, 128], BF16, T["wsT"][:, :, :], "WsT", q="gpsimd")
    Wp, r_Wp = load_const(K, [4, 128], BF16, T["wpoolT"][:, :, :], "Wp", q="gpsimd")
    allgather(K, T["EDGE"], T["EDGEG"], [K.dr("EDGE", 0), K.dr("EDGE", 1)], [K.dr("EDGEG", 0)])
    bsb, r_bsb = load_const(K, [512], F32, T["bs_bc"][:, :], "bsb")
    lng, r_lng = load_const(K, [512], F32, T["lng_bc"][:, :], "lng")
    lnb, r_lnb = load_const(K, [512], F32, T["lnb_bc"][:, :], "lnb")
    psc, r_psc = load_const(K, [4], F32, T["pscT"][:, :], "psc")
    r_tab = Res("pooltabs")
    bM = [K.alloc([512], BF16) for _ in range(5)]
    bH = [K.alloc([512], BF16) for _ in range(3)]
    bE = [K.alloc([512], BF16) for _ in range(2)]
    for i in range(5):
        K.dma(bM[i], T["pbM"][i, :, :], [], [r_tab], r_tab)
    for i in range(3):
        K.dma(bH[i][0:16, :], T["pbH"][i, :, :], [], [r_tab], r_tab)
    for i in range(2):
        K.dma(bE[i][0:64, :], T["pbE"][i, :, :], [], [r_tab], r_tab)
    epsln = K.alloc([1], F32)
    r_eps = Res("epsln")
    K.op("vector", lambda e: e.memset(epsln, LN_EPS), [], [r_eps])
    load_gbc(K, ls, bufs)
    uT = K.alloc([4, 512], F32)
    r_uT = [Res("uT%d" % i) for i in range(4)]
    v = [K.alloc([512], F32) for _ in range(4)]
    r_v = [Res("v%d" % i) for i in range(4)]
    vn = [K.alloc([512], BF16) for _ in range(4)]
    r_vn = [Res("vn%d" % i) for i in range(4)]
    tya = [K.alloc([512], F32) for _ in range(2)]
    r_tya = [Res("tya0"), Res("tya1")]
    yT2 = [K.alloc([8, 512], BF16) for _ in range(2)]
    r_yT2 = [Res("yTa"), Res("yTb")]
    zt = [K.alloc([512], BF16) for _ in range(4)]
    r_zt = [Res("zt%d" % i) for i in range(4)]
    ht = [K.alloc([512], BF16) for _ in range(4)]
    r_ht = [Res("ht%d" % i) for i in range(4)]
    et = K.alloc([512], BF16)
    r_et = Res("et")
    dT = [K.alloc([512], BF16) for _ in range(4)]
    r_dT = [Res("dT%d" % i) for i in range(4)]
    lst = K.alloc([7, 4], F32)
    r_lst = [Res("lst%d" % i) for i in range(7)]
    r_s1 = [Res("s1_%d" % i) for i in range(4)]
    r_s2 = [Res("s2_%d" % i) for i in range(4)]
    junk, r_junk = bufs["junk"], bufs["r_junk"]
    s1, s2, mean, msq, var, rstd, nmr = [lst[:, i, :] for i in range(7)]

    pre_norm_group(K, 0, ls, srcname, bufs)
    pre_norm_A(K, 1, ls, srcname, bufs)
    for g in range(NG):
        hT, r_hT = bufs["hT2"][g % 2], bufs["r_hT2"][g % 2]
        yT, r_yT = yT2[g % 2], r_yT2[g % 2]
        if g + 2 < NG:
            pre_norm_A(K, g + 2, ls, srcname, bufs)
        for j in range(4):
            load_tile_with_halo(K, g * 4 + j, "ZB", "EDGEG", zt[j], r_zt[j], ht[j], r_ht[j], et, r_et)
        for j in range(4):
            b = 2 + j % 2
            K.mm(K.banks[b][:, :], [(hT[:, k, j * 128:(j + 1) * 128], Win[:, k, 512:1024]) for k in range(8)],
                 [r_Win, r_hT], [K.bank_res[b]])
            K.act(v[j], K.banks[b][:, :], AF.Gelu, [K.bank_res[b]], [r_v[j], r_s1[j]], accum_out=s1[:, j:j + 1])
        for fc in range(4):
            b = fc % 2
            K.mm(K.banks[b][:, :], [(Win[:, k, fc * 128:(fc + 1) * 128], hT[:, k, :]) for k in range(8)],
                 [r_Win, r_hT], [K.bank_res[b]])
            K.act(uT[:, fc, :], K.banks[b][:, :], AF.Gelu, [K.bank_res[b]], [r_uT[fc]])
        for j in range(4):
            K.act(junk, v[j], AF.Square, [r_v[j]], [r_junk, r_s2[j]], accum_out=s2[:, j:j + 1])
        K.op("vector", lambda e: e.tensor_scalar(mean, s1, 1.0 / 512, None, ALU.mult), r_s1, [r_lst[2]])
        K.op("vector", lambda e: e.tensor_tensor(msq, mean, mean, ALU.mult), [r_lst[2]], [r_lst[3]])
        K.op("vector", lambda e: e.scalar_tensor_tensor(out=var, in0=s2, scalar=1.0 / 512, in1=msq,
                                                         op0=ALU.mult, op1=ALU.subtract),
             r_s2 + [r_lst[3]], [r_lst[4]])
        K.act(rstd, var, AF.Sqrt, [r_lst[4], r_eps], [r_lst[5]], bias=epsln, scale=1.0)
        K.op("vector", lambda e: e.reciprocal(rstd, rstd), [r_lst[5]], [r_lst[5]])
        K.op("vector", lambda e: e.scalar_tensor_tensor(out=nmr, in0=mean, scalar=-1.0, in1=rstd,
                                                         op0=ALU.mult, op1=ALU.mult),
             [r_lst[2], r_lst[5]], [r_lst[6]])
        for j in range(4):
            if j % 2 == 0:
                K.act(v[j], v[j], AF.Identity, [r_lst[5], r_lst[6], r_v[j]], [r_v[j]],
                      bias=nmr[:, j:j + 1], scale=rstd[:, j:j + 1])
            else:
                K.op("vector", lambda e, o=v[j], a=rstd[:, j:j + 1], b_=nmr[:, j:j + 1]:
                     e.tensor_scalar(o, o, a, b_, ALU.mult, ALU.add), [r_lst[5], r_lst[6], r_v[j]], [r_v[j]])
        for j in range(4):
            K.op("vector", lambda e, o=v[j]: e.tensor_tensor(o, o, lng, ALU.mult), [r_v[j], r_lng], [r_v[j]])
            K.op("vector", lambda e, o=vn[j], a=v[j]: e.tensor_tensor(o, a, lnb, ALU.add), [r_v[j], r_lnb], [r_vn[j]])
        for j in range(4):
            t = g * 4 + j
            b = j % 2
            outs = [(K.banks[b][:, gc * 128:(gc + 1) * 128], b, gc, gc) for gc in range(4)]
            bandmix(K, t, outs, zt[j], r_zt[j], ht[j], r_ht[j], et, r_et, bM, bH, bE, r_tab)
            K.act(dT[j], K.banks[b][:, :], AF.Identity, [K.bank_res[b]], [r_dT[j]])
        if g + 1 < NG:
            pre_norm_B(K, g + 1, ls, srcname, bufs, halves=(0,))
        if g > 0:
            wout_post(K, g - 1, ls, srcname, dstname, bufs, yT2[(g - 1) % 2], r_yT2[(g - 1) % 2], Wout, r_Wout)
        if g + 1 < NG:
            pre_norm_B(K, g + 1, ls, srcname, bufs, halves=(1,))
        for j in range(4):
            tc = slice(j * 128, (j + 1) * 128)
            b = j % 2
            for h in range(4):
                K.mm(K.banks[b][:, h * 128:(h + 1) * 128], [(vn[j][:, h * 128:(h + 1) * 128], WsT[:, h, :])],
                     [r_vn[j], r_WsT], [K.bank_res[b]])
            K.op("vector", lambda e, o=tya[j % 2], a=K.banks[b][:, :]: e.tensor_tensor(o, a, bsb, ALU.add),
                 [K.bank_res[b], r_bsb], [r_tya[j % 2]])
            K.op("vector", lambda e, o=yT[:, 0:4, tc], a=tya[j % 2].rearrange("p (h q) -> p h q", h=4),
                 b_=uT[:, :, tc]: e.tensor_tensor(o, a, b_, ALU.mult), [r_tya[j % 2]] + r_uT, [r_yT])
        for j in range(4):
            tc = slice(j * 128, (j + 1) * 128)
            b = j % 2
            for gc in range(4):
                K.mm(K.banks[b][:, gc * 128:(gc + 1) * 128], [(Wp[:, gc, :], dT[j][:, gc * 128:(gc + 1) * 128])],
                     [r_Wp, r_dT[j]], [K.bank_res[b]])
            K.op("vector", lambda e, o=yT[:, 4:8, tc], a=K.banks[b][:, :].rearrange("p (h q) -> p h q", h=4),
                 b_=psc.unsqueeze(2).to_broadcast([128, 4, 128]): e.tensor_tensor(o, a, b_, ALU.mult),
                 [K.bank_res[b], r_psc], [r_yT])
    wout_post(K, NG - 1, ls, srcname, dstname, bufs, yT2[(NG - 1) % 2], r_yT2[(NG - 1) % 2], Wout, r_Wout)


def odd_pass2(K, l, sub, srcname, dstname):
    T = K.T
    ls = l * 3 + sub
    j_ = l // 2
    m0 = K.mark()
    bufs = common_bufs(K, n_hT=2)
    junk, r_junk = bufs["junk"], bufs["r_junk"]
    Wc = K.alloc([8, 512], BF16)
    r_Wc = Res("Wc")
    load_weight_cast(K, Wc, T["od_w_in"][j_, :, 0:512].rearrange("(k p) f -> p k f", p=128), r_Wc)
    Wxd = K.alloc([8, 1024], BF16)
    r_Wxd = Res("Wxd")
    load_weight_cast(K, Wxd, T["od_w_in"][j_, :, 1024:2048].rearrange("(k p) f -> p k f", p=128), r_Wxd)
    fgb, r_fgb = load_const(K, [512], F32, T["fg_bc"][:, :], "fgb")
    CD, r_CD = load_const(K, [256], BF16, T["CD"][:, :], "CD")
    zero_pads(K, "ZC")
    csb = [K.alloc([512], F32) for _ in range(2)]
    r_csb = [Res("csb0"), Res("csb1")]
    z = [K.alloc([512], BF16) for _ in range(4)]
    r_z = [Res("z%d" % i) for i in range(4)]
    zg = [K.alloc([512], BF16) for _ in range(2)]
    r_zg = [Res("zg0"), Res("zg1")]
    ztmp = [K.alloc([512], F32) for _ in range(2)]
    r_ztmp = [Res("ztmp0"), Res("ztmp1")]
    zgT = [K.alloc([512], BF16) for _ in range(2)]
    r_zgT = [Res("zgT0"), Res("zgT1")]
    ab = [K.alloc([1024], BF16) for _ in range(4)]
    r_ab = [Res("ab%d" % i) for i in range(4)]
    sd = K.alloc([2, 16], F32)
    r_ss = [Res("sdss%d" % i) for i in range(16)]
    abp_done = Res("abp_done")
    r_rd = Res("sdr")
    pre_norm_group(K, 0, ls, srcname, bufs)
    pre_norm_A(K, 1, ls, srcname, bufs)
    for g in range(NG):
        hT, r_hT = bufs["hT2"][g % 2], bufs["r_hT2"][g % 2]
        if g + 2 < NG:
            pre_norm_A(K, g + 2, ls, srcname, bufs)
        for j in range(4):
            t = g * 4 + j
            tc = slice(j * 128, (j + 1) * 128)
            b0, b1 = (0, 1) if j % 2 == 0 else (2, 3)
            K.mm(K.banks[b0][:, :], [(hT[:, k, tc], Wc[:, k, :]) for k in range(8)], [r_hT, r_Wc], [K.bank_res[b0]])
            K.mm(K.banks[b1][:, :], [(hT[:, k, tc], Wxd[:, k, 0:512]) for k in range(8)], [r_hT, r_Wxd],
                 [K.bank_res[b1]])
            K.act(csb[j % 2], K.banks[b0][:, :], AF.Identity, [K.bank_res[b0]], [r_csb[j % 2]])
            K.op("vector", lambda e, o=z[j], a=K.banks[b1][:, :], c_=csb[j % 2]: e.tensor_tensor(o, a, c_, ALU.mult),
                 [K.bank_res[b1], r_csb[j % 2]], [r_z[j]])
            store_rows_with_edges(K, z[j], r_z[j], t, "ZC", "EDGE2")
        for jp in (0, 2):
            if g + 1 < NG:
                pre_norm_B(K, g + 1, ls, srcname, bufs, halves=(jp // 2,))
            for j in (jp, jp + 1):
                q2 = j % 2
                tc = slice(j * 128, (j + 1) * 128)
                K.mm(K.banks[q2][:, :], [(hT[:, k, tc], Wxd[:, k, 512:1024]) for k in range(8)], [r_hT, r_Wxd],
                     [K.bank_res[q2]])
                for gq in range(4):
                    K.act(junk[:, 0:128], K.banks[q2][:, gq * 128:(gq + 1) * 128], AF.Square, [K.bank_res[q2]],
                          [r_junk, r_ss[q2 * 4 + gq]], accum_out=sd[:, 0, q2 * 4 + gq:q2 * 4 + gq + 1])
            K.act(sd[:, 1, 0:8], sd[:, 0, 0:8], AF.Sqrt, r_ss[0:8] + [K.r_mod], [r_rd], bias=K.eps_rms, scale=1.0 / 128)
            K.op("vector", lambda e: e.reciprocal(sd[:, 1, 0:8], sd[:, 1, 0:8]), [r_rd], [r_rd])
            for j in (jp, jp + 1):
                t = g * 4 + j
                q2 = j % 2
                rb = sd[:, 1, q2 * 4:(q2 + 1) * 4].unsqueeze(2).to_broadcast([128, 4, 128])
                K.op("vector", lambda e, o=ztmp[q2].rearrange("p (a b) -> p a b", a=4),
                     a=K.banks[q2][:, :].rearrange("p (a b) -> p a b", a=4), rb=rb: e.tensor_tensor(o, a, rb, ALU.mult),
                     [K.bank_res[q2], r_rd], [r_ztmp[q2]])
                K.op("vector", lambda e, o=zg[q2], a=ztmp[q2]: e.tensor_tensor(o, a, fgb, ALU.mult),
                     [r_ztmp[q2], r_fgb], [r_zg[q2]])
                tb = 4 + q2
                pt = K.banks[tb].bitcast(BF16)
                K.op("tensor", lambda e, pt=pt, zz=zg[q2]: [e.transpose(pt[:, q * 128:(q + 1) * 128],
                                                                      zz[:, q * 128:(q + 1) * 128], K.identb)
                                                          for q in range(4)][-1],
                     [r_zg[q2], K.r_identb], [K.bank_res[tb]])
                K.act(zgT[q2], pt[:, 0:512], AF.Identity, [K.bank_res[tb]], [r_zgT[q2]])
                ba, bb = 2, 3
                for gq in range(4):
                    b = ba if gq < 2 else bb
                    K.mm(K.banks[b][:, (gq % 2) * 256:(gq % 2 + 1) * 256],
                         [(zgT[q2][:, gq * 128:(gq + 1) * 128], CD)], [r_zgT[q2], r_CD], [K.bank_res[b]])
                K.act(ab[j][:, 0:512], K.banks[ba][:, :], AF.Identity, [K.bank_res[ba]], [r_ab[j]])
                K.op("vector", lambda e, o=ab[j][:, 512:1024], a=K.banks[bb][:, :]: e.tensor_copy(o, a),
                     [K.bank_res[bb]], [r_ab[j]])
                for u in range(8):
                    if t < 32:
                        K.dma(T["ABP%d" % u][t * 128:(t + 1) * 128, :], ab[j][:, u * 128:(u + 1) * 128],
                              [r_ab[j], abp_done], [K.dr("ABP", u)], r_ab[j])
                    else:
                        K.dma(T["ABS%d" % u][(t - 32) * 128:(t - 31) * 128, :], ab[j][:, u * 128:(u + 1) * 128],
                              [r_ab[j]], [K.dr("ABS", u)], r_ab[j])
        if g >= 7 and g < 15:
            u = g - 7
            allgather(K, T["ABP%d" % u], T["ABG%d" % u], [K.dr("ABP", u)],
                      [K.dr("ABG", u)] + ([abp_done] if u == 0 else []))
            if g == 7:
                allgather(K, T["EDGE2"], T["EDGE2G"], [K.dr("EDGE2", 0), K.dr("EDGE2", 1)], [K.dr("EDGE2G", 0)])
    K.barrier()
    K.release(m0)
    fourier_stage2(K)
    K.barrier()
    K.release(m0)
    bufs = common_bufs(K, n_hT=2)
    Wb = K.alloc([8, 512], BF16)
    r_Wb = Res("Wb")
    load_weight_cast(K, Wb, T["od_w_in"][j_, :, 512:1024].rearrange("(k p) f -> p k f", p=128), r_Wb)
    Wout = K.alloc([8, D], BF16)
    r_Wout = Res("Wout")
    load_weight_cast(K, Wout, T["od_w_out"][j_, :, :].rearrange("(k p) f -> p k f", p=128), r_Wout)
    fw, r_fw = load_const(K, [4, 128], BF16, T["fwT"][:, :, :], "fw", q="gpsimd")
    cw, r_cw = load_const(K, [4, 3], F32, T["convT"][:, :, :], "cw")
    r_tab = Res("convtabs")
    bM = [K.alloc([384], BF16) for _ in range(5)]
    bH = [K.alloc([384], BF16) for _ in range(3)]
    bE = [K.alloc([384], BF16) for _ in range(2)]
    for i in range(5):
        K.dma(bM[i], T["cbM"][i, :, :], [], [r_tab], r_tab)
    for i in range(3):
        K.dma(bH[i][0:16, :], T["cbH"][i, :, :], [], [r_tab], r_tab)
    for i in range(2):
        K.dma(bE[i][0:64, :], T["cbE"][i, :, :], [], [r_tab], r_tab)
    load_gbc(K, ls, bufs)
    bT = K.alloc([4, 512], F32)
    r_bT = [Res("bT%d" % i) for i in range(4)]
    yT2 = [K.alloc([8, 512], BF16) for _ in range(2)]
    r_yT2 = [Res("yTa"), Res("yTb")]
    ydin = K.alloc([4, 512], BF16)
    r_ydin = Res("ydin")
    zt = [K.alloc([512], BF16) for _ in range(4)]
    r_zt = [Res("zt%d" % i) for i in range(4)]
    ht = [K.alloc([512], BF16) for _ in range(4)]
    r_ht = [Res("ht%d" % i) for i in range(4)]
    et = K.alloc([512], BF16)
    r_et = Res("et")
    t1 = [K.alloc([512], F32) for _ in range(2)]
    r_t1 = [Res("t1a"), Res("t1b")]
    t2 = [K.alloc([512], F32) for _ in range(2)]
    r_t2 = [Res("t2a"), Res("t2b")]
    pre_norm_group(K, 0, ls, srcname, bufs)
    pre_norm_A(K, 1, ls, srcname, bufs)
    for g in range(NG):
        hT, r_hT = bufs["hT2"][g % 2], bufs["r_hT2"][g % 2]
        yT, r_yT = yT2[g % 2], r_yT2[g % 2]
        if g + 2 < NG:
            pre_norm_A(K, g + 2, ls, srcname, bufs)
        K.dma(ydin, T["YDT"][:, g * 512:(g + 1) * 512].rearrange("(q m) t -> m q t", q=4), [K.dr("YDT", 0)],
              [r_ydin], r_ydin)
        for j in range(4):
            load_tile_with_halo(K, g * 4 + j, "ZC", "EDGE2G", zt[j], r_zt[j], ht[j], r_ht[j], et, r_et)
        for fc in range(4):
            b = fc % 2
            K.mm(K.banks[b][:, :], [(Wb[:, k, fc * 128:(fc + 1) * 128], hT[:, k, :]) for k in range(8)],
                 [r_Wb, r_hT], [K.bank_res[b]])
            K.act(bT[:, fc, :], K.banks[b][:, :], AF.Identity, [K.bank_res[b]], [r_bT[fc]])
        for gq in range(4):
            b = 2 + gq % 2
            K.mm(K.banks[b][:, :], [(fw[:, gq, :], ydin[:, gq, :])], [r_fw, r_ydin], [K.bank_res[b]])
            if gq % 2 == 0:
                K.act(yT[:, 4 + gq, :], K.banks[b][:, :], AF.Identity, [K.bank_res[b]], [r_yT])
            else:
                K.op("vector", lambda e, o=yT[:, 4 + gq, :], a=K.banks[b][:, :]: e.tensor_copy(o, a),
                     [K.bank_res[b]], [r_yT])
        if g + 1 < NG:
            pre_norm_B(K, g + 1, ls, srcname, bufs, halves=(0,))
        if g > 0:
            wout_post(K, g - 1, ls, srcname, dstname, bufs, yT2[(g - 1) % 2], r_yT2[(g - 1) % 2], Wout, r_Wout)
        if g + 1 < NG:
            pre_norm_B(K, g + 1, ls, srcname, bufs, halves=(1,))
        for j in range(4):
            t = g * 4 + j
            tc = slice(j * 128, (j + 1) * 128)
            tb = (0, 1, 2) if j % 2 == 0 else (3, 4, 5)
            for tap in range(3):
                outs = [(K.banks[tb[tap]][:, cc * 128:(cc + 1) * 128], tb[tap], cc, tap) for cc in range(4)]
                bandmix(K, t, outs, zt[j], r_zt[j], ht[j], r_ht[j], et, r_et, bM, bH, bE, r_tab)
            q2 = j % 2
            v3 = lambda a: a.rearrange("p (a b) -> p a b", a=4)
            wbc = lambda tap: cw[:, :, tap].unsqueeze(2).to_broadcast([128, 4, 128])
            K.op("vector", lambda e, o=v3(t1[q2]), a=v3(K.banks[tb[0]][:, :]), w=wbc(0): e.tensor_tensor(o, a, w, ALU.mult),
                 [K.bank_res[tb[0]], r_cw], [r_t1[q2]])
            K.op("vector", lambda e, o=v3(t2[q2]), a=v3(K.banks[tb[1]][:, :]), w=wbc(1): e.tensor_tensor(o, a, w, ALU.mult),
                 [K.bank_res[tb[1]], r_cw], [r_t2[q2]])
            K.op("vector", lambda e, o=t1[q2], a=t2[q2]: e.tensor_tensor(o, o, a, ALU.add), [r_t2[q2], r_t1[q2]], [r_t1[q2]])
            K.op("vector", lambda e, o=v3(t2[q2]), a=v3(K.banks[tb[2]][:, :]), w=wbc(2): e.tensor_tensor(o, a, w, ALU.mult),
                 [K.bank_res[tb[2]], r_cw, r_t1[q2]], [r_t2[q2]])
            K.op("vector", lambda e, o=t1[q2], a=t2[q2]: e.tensor_tensor(o, o, a, ALU.add), [r_t2[q2], r_t1[q2]], [r_t1[q2]])
            K.op("vector", lambda e, o=yT[:, 0:4, tc], a=v3(t1[q2]), b_=bT[:, :, tc]: e.tensor_tensor(o, a, b_, ALU.mult),
                 [r_t1[q2]] + r_bT, [r_yT])
    wout_post(K, NG - 1, ls, srcname, dstname, bufs, yT2[(NG - 1) % 2], r_yT2[(NG - 1) % 2], Wout, r_Wout)


def fourier_load(K, srcs, X, r_X, dres_in):
    for (p0, p1, sap) in srcs:
        K.dma(X[p0:p1], sap, dres_in, [r_X], r_X)


def fourier_step1(K, NAp, NC, Ft, r_Ft, X, r_X, TTflat, r_TT, cnt):
    MB = 512 // NC
    for bi, m0 in enumerate(range(0, 64, MB)):
        b = bi % 4
        for mi in range(MB):
            m = m0 + mi
            out = K.banks[b][:, mi * NC:(mi + 1) * NC]
            K.mm(out, [(X[0:NAp, :, m], Ft[0:NAp, 0, :]), (X[0:NAp, :, 64 + m], Ft[0:NAp, 1, :])],
                 [r_X, r_Ft], [K.bank_res[b]])
        src = K.banks[b][:, 0:MB * NC]
        dst = TTflat[:, m0:m0 + MB, :].rearrange("p m c -> p (m c)")
        cnt[0] += 1
        if cnt[0] % 2 == 0:
            K.act(dst, src, AF.Identity, [K.bank_res[b]], [r_TT])
        else:
            K.op("vector", lambda e, o=dst, a=src: e.tensor_copy(o, a), [K.bank_res[b]], [r_TT])


def fourier_step3(K, NA, NK2, TT4, r_TT, GG, r_GG, ydg, r_ydg, dst_ap, dres_out, cnt):
    K1B = 512 // NK2
    yv = ydg[0:64, 0:NK2 * NA].rearrange("p (k2 k1) -> p k2 k1", k2=NK2, k1=NA)
    for kb in range(NA // K1B):
        b = 4 + kb % 2
        for kl in range(K1B):
            k1 = kb * K1B + kl
            K.mm(K.banks[b][0:64, kl * NK2:(kl + 1) * NK2],
                 [(TT4[:, :, 0, k1], GG[:, k1, 0, :]), (TT4[:, :, 1, k1], GG[:, k1, 1, :])],
                 [r_TT, r_GG], [K.bank_res[b]])
        src = K.banks[b][0:64, :].rearrange("p (kl k2) -> p k2 kl", kl=K1B, k2=NK2)
        dst = yv[:, :, kb * K1B:(kb + 1) * K1B]
        cnt[0] += 1
        if cnt[0] % 2 == 0:
            K.act(dst, src, AF.Identity, [K.bank_res[b]], [r_ydg])
        else:
            K.op("vector", lambda e, o=dst, a=src: e.tensor_copy(o, a), [K.bank_res[b]], [r_ydg])
    K.dma(dst_ap, ydg[0:64, 0:NK2 * NA], [r_ydg], dres_out, r_ydg, q="gpsimd")


def fourier_stage2(K):
    T = K.T
    F128, r_F128 = load_const(K, [2, 256], BF16, T["F128"][:, :].rearrange("p (a b) -> p a b", a=2), "F128")
    Fbd, r_Fbd = load_const(K, [2, 256], BF16, T["Fbd"][:, :].rearrange("p (a b) -> p a b", a=2), "Fbd")
    GGp, r_GGp = load_const(K, [128, 2, 32], BF16,
                            T["GGp"][:, :].rearrange("p (k a b) -> p k a b", k=128, a=2), "GGp")
    GGs, r_GGs = load_const(K, [16, 2, 128], BF16,
                            T["GGs"][:, :].rearrange("p (k a b) -> p k a b", k=16, a=2), "GGs")
    X = [K.alloc([128, 128], BF16) for _ in range(2)]
    r_X = [Res("X0"), Res("X1")]
    TT = [K.alloc([64, 256], BF16) for _ in range(2)]
    r_TT = [Res("TT0"), Res("TT1")]
    ydg = [K.alloc([4096], BF16) for _ in range(2)]
    r_ydg = [Res("ydg0"), Res("ydg1")]
    cnt = [0]
    units = []
    for q in range(2):
        srcs = [(u * 16, (u + 1) * 16, T["ABS%d" % u][q * 2048:(q + 1) * 2048, :].rearrange("(a b) c -> a b c", b=128))
                for u in range(8)]
        units.append(dict(kind="s", q=q, srcs=srcs, din=[K.dr("ABS", u) for u in range(8)]))
    for u in range(8):
        units.append(dict(kind="p", u=u, srcs=[(0, 128, T["ABG%d" % u][:, :].rearrange("(a b) c -> a b c", b=128))],
                          din=[K.dr("ABG", u)]))
    fourier_load(K, units[0]["srcs"], X[0], r_X[0], units[0]["din"])
    ny = 0
    for n, U in enumerate(units):
        if n + 1 < len(units):
            V = units[n + 1]
            fourier_load(K, V["srcs"], X[(n + 1) % 2], r_X[(n + 1) % 2], V["din"])
        tt, rtt = TT[n % 2], r_TT[n % 2]
        if U["kind"] == "s":
            q = U["q"]
            fourier_step1(K, 128, 256, Fbd, r_Fbd, X[n % 2], r_X[n % 2], tt, rtt, cnt)
            tt5 = tt.rearrange("p m (u part k) -> p m u part k", u=8, part=2, k=16)
            for u in range(8):
                fourier_step3(K, 16, 128, tt5[:, :, u, :, :], rtt, GGs, r_GGs, ydg[ny % 2], r_ydg[ny % 2],
                              T["YDT"][u * 64:(u + 1) * 64, 4096 + q * 2048:4096 + (q + 1) * 2048],
                              [K.dr("YDT", 0)], cnt)
                ny += 1
        else:
            u = U["u"]
            fourier_step1(K, 128, 256, F128, r_F128, X[n % 2], r_X[n % 2], tt, rtt, cnt)
            tt4 = tt.rearrange("p m (part k) -> p m part k", part=2, k=128)
            fourier_step3(K, 128, 32, tt4, rtt, GGp, r_GGp, ydg[ny % 2], r_ydg[ny % 2],
                          T["YDT"][u * 64:(u + 1) * 64, 0:4096], [K.dr("YDT", 0)], cnt)
            ny += 1
```
